# Optimizing a Trainium2 kernel written in Bass

```python
import math
import jax, jax.numpy as jnp
from jax import lax
import numpy as np

D_MODEL = 1024
BATCH = 2
SEQ = 8192
DEPTH = 2
DEC_BATCH = 32
DEC_SEQ = 8
PAST_LEN = 8192
PAGE_SIZE = 128

N_EVEN = (DEPTH + 1) // 2
N_ODD = DEPTH // 2
EPS = 1e-6
CONV_K = 4
SSD_HEADS = 16
SSD_HEAD_DIM = 64
SSD_INNER = SSD_HEADS * SSD_HEAD_DIM
SSD_GROUPS = 2
SSD_STATE = 64
SSD_CONV_DIM = SSD_INNER + 2 * SSD_GROUPS * SSD_STATE
SSD_CHUNK = 64
GDN_HEADS = 8
GDN_DK = 128
GDN_DV = 128
GDN_QK = GDN_HEADS * GDN_DK
GDN_VW = GDN_HEADS * GDN_DV
GDN_CONV_DIM = 2 * GDN_QK + GDN_VW
GDN_CHUNK = 64
IN_HYB = SSD_INNER + SSD_CONV_DIM + SSD_HEADS + GDN_CONV_DIM + GDN_VW + 2 * GDN_HEADS
MIX_HYB = SSD_INNER + GDN_VW
FOX_HEADS = 16
FOX_HEAD_DIM = 64
FOX_W = FOX_HEADS * FOX_HEAD_DIM
IN_FOX = 3 * FOX_W + FOX_HEADS
FOX_QBLOCK = 128
FOX_BIAS_LO = 3.0
FOX_BIAS_HI = 10.0
MEM_LEN = 256
X_HEADS = 4
X_HEAD_DIM = 128
X_W = X_HEADS * X_HEAD_DIM
D_FF = ((8 * D_MODEL // 3 + 255) // 256) * 256

kernel_name = 'hybrid_ssd_gdn_fox_decoder_step'


def _rms(x):
    return x * lax.rsqrt(jnp.mean(x * x, axis=-1, keepdims=True) + EPS)


def rmsnorm(x, g):
    return (_rms(x.astype(jnp.float32)) * g.astype(jnp.float32)).astype(x.dtype)


def _l2norm(x):
    return x * lax.rsqrt(jnp.sum(x * x, axis=-1, keepdims=True) + 1e-6)


def _split(x, sizes):
    return jnp.split(x, np.cumsum(sizes)[:-1].tolist(), axis=-1)


def _causal_conv(x, buf, w, b):
    seq = x.shape[1]
    xp = jnp.concatenate([buf.astype(x.dtype), x], axis=1)
    y = xp[:, 0:seq] * w[0]
    for j in range(1, CONV_K):
        y = y + xp[:, j:j + seq] * w[j]
    if b is not None:
        y = y + b
    return jax.nn.silu(y), xp[:, -(CONV_K - 1):]


def _ssd_scan(x, dt, a, bm, cm, h0):
    bsz, seq, nh, hp = x.shape
    ng, ns = bm.shape[2], bm.shape[3]
    hg = nh // ng
    q = math.gcd(seq, SSD_CHUNK)
    nc = seq // q

    def chunks(t):
        return t.reshape(bsz, nc, q, *t.shape[2:]).swapaxes(0, 1)

    xs = (chunks(x.reshape(bsz, seq, ng, hg, hp)), chunks(dt.reshape(bsz, seq, ng, hg)), chunks(bm), chunks(cm))
    ag = a.reshape(ng, hg)
    causal = jnp.tril(jnp.ones((q, q), bool))

    def body(h, inp):
        xc, dtc, bc, cc = inp
        cum = jnp.cumsum(dtc * ag, axis=1)
        seg = cum[:, :, None] - cum[:, None, :]
        lm = jnp.exp(jnp.where(causal[None, :, :, None, None], seg, -jnp.inf))
        cb = jnp.einsum('btgn,bsgn->btsg', cc, bc)
        xdt = xc * dtc[..., None]
        y = jnp.einsum('btsg,btsgh,bsghp->btghp', cb, lm, xdt)
        y = y + jnp.einsum('btgn,bghpn->btghp', cc, h) * jnp.exp(cum)[..., None]
        wlast = jnp.exp(cum[:, -1:] - cum)
        h = h * jnp.exp(cum[:, -1])[..., None, None] + jnp.einsum('bsgn,bsgh,bsghp->bghpn', bc, wlast, xdt)
        return h, y

    h, ys = lax.scan(body, h0.reshape(bsz, ng, hg, hp, ns), xs)
    return ys.swapaxes(0, 1).reshape(bsz, seq, nh, hp), h.reshape(bsz, nh, hp, ns)


def _gdn_scan(q, k, v, g, beta, s0):
    bsz, seq, nh, _ = q.shape
    dv = v.shape[-1]
    c = math.gcd(seq, GDN_CHUNK)
    nc = seq // c

    def chunks(t):
        return jnp.moveaxis(t.reshape(bsz, nc, c, nh, *t.shape[3:]), (1, 3), (0, 2))

    xs = (chunks(q), chunks(k), chunks(v), chunks(g), chunks(beta))
    tril = jnp.tril(jnp.ones((c, c), bool))
    strict = jnp.tril(jnp.ones((c, c), bool), -1)
    eye = jnp.eye(c, dtype=jnp.float32)

    def body(s, inp):
        qc, kc, vc, gc, bc = inp
        gam = jnp.cumsum(gc, axis=-1)
        dec = jnp.exp(jnp.where(tril, gam[..., :, None] - gam[..., None, :], -jnp.inf))
        kb = kc * bc[..., None]
        amat = jnp.where(strict, jnp.einsum('bhik,bhjk->bhij', kb, kc) * dec, 0.0)
        tinv = lax.linalg.triangular_solve(eye + amat, jnp.broadcast_to(eye, amat.shape),
                                           left_side=True, lower=True, unit_diagonal=True)
        u = jnp.einsum('bhij,bhjv->bhiv', tinv, vc * bc[..., None])
        w = jnp.einsum('bhij,bhjk->bhik', tinv, kb * jnp.exp(gam)[..., None])
        vn = u - jnp.einsum('bhik,bhkv->bhiv', w, s)
        qk = jnp.einsum('bhik,bhjk->bhij', qc, kc) * dec
        o = jnp.einsum('bhik,bhkv->bhiv', qc * jnp.exp(gam)[..., None], s) + jnp.einsum('bhij,bhjv->bhiv', qk, vn)
        glast = gam[..., -1]
        s = s * jnp.exp(glast)[..., None, None] + jnp.einsum('bhjk,bhjv->bhkv', kc * jnp.exp(glast[..., None] - gam)[..., None], vn)
        return s, o

    s, os_ = lax.scan(body, s0, xs)
    return jnp.moveaxis(os_, (0, 2), (1, 3)).reshape(bsz, seq, nh, dv), s


def _hybrid_mixer(hn, ssd_h0, ssd_buf, gdn_s0, gdn_buf, w_in, w_out, ssd_conv_w, ssd_conv_b,
                  ssd_dt_bias, ssd_a_log, ssd_d, ssd_norm, gdn_conv_w, gdn_dt_bias, gdn_a_log, gdn_norm):
    f32 = jnp.float32
    bsz, seq, _ = hn.shape
    z, xbc, dt, qkv, gate, b_raw, a_raw = _split(
        hn @ w_in, [SSD_INNER, SSD_CONV_DIM, SSD_HEADS, GDN_CONV_DIM, GDN_VW, GDN_HEADS, GDN_HEADS])
    xbc, ssd_buf_new = _causal_conv(xbc, ssd_buf, ssd_conv_w, ssd_conv_b)
    xs, bm, cm = _split(xbc.astype(f32), [SSD_INNER, SSD_GROUPS * SSD_STATE, SSD_GROUPS * SSD_STATE])
    xs = xs.reshape(bsz, seq, SSD_HEADS, SSD_HEAD_DIM)
    dt = jax.nn.softplus(dt.astype(f32) + ssd_dt_bias.astype(f32))
    a = -jnp.exp(ssd_a_log.astype(f32))
    y, ssd_h = _ssd_scan(xs, dt, a, bm.reshape(bsz, seq, SSD_GROUPS, SSD_STATE),
                         cm.reshape(bsz, seq, SSD_GROUPS, SSD_STATE), ssd_h0.astype(f32))
    y = (y + ssd_d.astype(f32)[:, None] * xs).reshape(bsz, seq, SSD_INNER) * jax.nn.silu(z.astype(f32))
    y = _rms(y.reshape(bsz, seq, SSD_GROUPS, SSD_INNER // SSD_GROUPS)).reshape(bsz, seq, SSD_INNER) * ssd_norm.astype(f32)
    qkv, gdn_buf_new = _causal_conv(qkv, gdn_buf, gdn_conv_w, None)
    q, k, v = _split(qkv.astype(f32), [GDN_QK, GDN_QK, GDN_VW])
    q = _l2norm(q.reshape(bsz, seq, GDN_HEADS, GDN_DK)) * GDN_DK ** -0.5
    k = _l2norm(k.reshape(bsz, seq, GDN_HEADS, GDN_DK))
    v = v.reshape(bsz, seq, GDN_HEADS, GDN_DV)
    beta = jax.nn.sigmoid(b_raw.astype(f32))
    g = -jnp.exp(gdn_a_log.astype(f32)) * jax.nn.softplus(a_raw.astype(f32) + gdn_dt_bias.astype(f32))
    o, gdn_s = _gdn_scan(q, k, v, g, beta, gdn_s0.astype(f32))
    o = _rms(o) * gdn_norm.astype(f32) * jax.nn.silu(gate.astype(f32).reshape(bsz, seq, GDN_HEADS, GDN_DV))
    mix = jnp.concatenate([y, o.reshape(bsz, seq, GDN_VW)], axis=-1).astype(hn.dtype)
    return (mix @ w_out, ssd_h.astype(hn.dtype), ssd_buf_new, gdn_s.astype(hn.dtype), gdn_buf_new)


def _fox_project(hn, w_in, b_f):
    bsz, seq, _ = hn.shape
    q, k, v, f = _split(hn @ w_in, [FOX_W, FOX_W, FOX_W, FOX_HEADS])
    shp = (bsz, seq, FOX_HEADS, FOX_HEAD_DIM)
    lf = jax.nn.log_sigmoid(f.astype(jnp.float32) + b_f.astype(jnp.float32))
    return q.reshape(shp), k.reshape(shp), v.reshape(shp), lf


def _fox_prompt_attn(q, k, v, lf):
    bsz, seq, nh, hd = q.shape
    scale = hd ** -0.5
    nb = seq // FOX_QBLOCK
    fcum = jnp.cumsum(lf, axis=1).transpose(0, 2, 1)
    qb = q.reshape(bsz, nb, FOX_QBLOCK, nh, hd).swapaxes(0, 1)
    fb = fcum.reshape(bsz, nh, nb, FOX_QBLOCK).transpose(2, 0, 1, 3)
    kpos = jnp.arange(seq)

    def block(args):
        qi, fi, i = args
        s = jnp.einsum('bqhd,bkhd->bhqk', qi, k).astype(jnp.float32) * scale + fi[..., None] - fcum[:, :, None, :]
        qpos = i * FOX_QBLOCK + jnp.arange(FOX_QBLOCK)
        s = jnp.where(kpos[None, :] <= qpos[:, None], s, -jnp.inf)
        p = jax.nn.softmax(s, axis=-1).astype(v.dtype)
        return jnp.einsum('bhqk,bkhd->bqhd', p, v)

    out = lax.map(block, (qb, fb, jnp.arange(nb)))
    return out.swapaxes(0, 1).reshape(bsz, seq, nh, hd)


def _fox_sample_attn(q, k, v, lf, kp, vp, lfp):
    hd = q.shape[-1]
    nt = q.shape[1]
    npast = kp.shape[1]
    scale = hd ** -0.5
    lfp = lfp.astype(jnp.float32)
    fn = jnp.cumsum(lf, axis=1).transpose(0, 2, 1)
    rev = (jnp.cumsum(lfp[:, ::-1], axis=1)[:, ::-1] - lfp).transpose(0, 2, 1)
    sp = jnp.einsum('bthd,bshd->bhts', q, kp).astype(jnp.float32) * scale + fn[..., None] + rev[:, :, None, :]
    sn = jnp.einsum('bthd,bshd->bhts', q, k).astype(jnp.float32) * scale + fn[..., None] - fn[:, :, None, :]
    sn = jnp.where(jnp.tril(jnp.ones((nt, nt), bool)), sn, -jnp.inf)
    p = jax.nn.softmax(jnp.concatenate([sp, sn], axis=-1), axis=-1)
    return (jnp.einsum('bhts,bshd->bthd', p[..., :npast].astype(vp.dtype), vp)
            + jnp.einsum('bhts,bshd->bthd', p[..., npast:].astype(v.dtype), v))


def _mem_kv(mem, g, wk, wv):
    bsz = mem.shape[0]
    m = rmsnorm(mem, g)
    shp = (bsz, MEM_LEN, X_HEADS, X_HEAD_DIM)
    return (m @ wk).reshape(shp), (m @ wv).reshape(shp)


def _cross_attn(hn, mk, mv, wq, wo):
    bsz, seq, _ = hn.shape
    q = (hn @ wq).reshape(bsz, seq, X_HEADS, X_HEAD_DIM)
    s = jnp.einsum('blhd,bmhd->bhlm', q, mk.astype(q.dtype)).astype(jnp.float32) * X_HEAD_DIM ** -0.5
    p = jax.nn.softmax(s, axis=-1).astype(q.dtype)
    o = jnp.einsum('bhlm,bmhd->blhd', p, mv.astype(q.dtype)).reshape(bsz, seq, X_W)
    return o @ wo


def _swiglu(hn, w1, w3, w2):
    return (jax.nn.silu(hn @ w1) * (hn @ w3)) @ w2


def _run_group(x, W, st, prompt):
    bsz, seq, _ = x.shape
    dty = x.dtype
    names = ('ssd', 'ssd_conv', 'gdn', 'gdn_conv', 'fox_k', 'fox_v', 'fox_lf', 'mem_k', 'mem_v')
    out = {n: [] for n in names}
    h = x
    for layer in range(DEPTH):
        hn = rmsnorm(h, W['norm_mix'][layer])
        if layer % 2 == 0:
            e = layer // 2
            if prompt:
                s0 = (jnp.zeros((bsz, SSD_HEADS, SSD_HEAD_DIM, SSD_STATE), jnp.float32),
                      jnp.zeros((bsz, CONV_K - 1, SSD_CONV_DIM), dty),
                      jnp.zeros((bsz, GDN_HEADS, GDN_DK, GDN_DV), jnp.float32),
                      jnp.zeros((bsz, CONV_K - 1, GDN_CONV_DIM), dty))
            else:
                s0 = (st['state_ssd'][e], st['state_ssd_conv'][e], st['state_gdn'][e], st['state_gdn_conv'][e])
            mix, s_ssd, c_ssd, s_gdn, c_gdn = _hybrid_mixer(
                hn, s0[0], s0[1], s0[2], s0[3], W['w_in_hyb'][e], W['w_out_hyb'][e], W['ssd_conv_w'][e],
                W['ssd_conv_b'][e], W['ssd_dt_bias'][e], W['ssd_A_log'][e], W['ssd_D'][e], W['ssd_norm'][e],
                W['gdn_conv_w'][e], W['gdn_dt_bias'][e], W['gdn_A_log'][e], W['gdn_norm'][e])
            out['ssd'].append(s_ssd)
            out['ssd_conv'].append(c_ssd)
            out['gdn'].append(s_gdn)
            out['gdn_conv'].append(c_gdn)
        else:
            o = layer // 2
            q, k, v, lf = _fox_project(hn, W['w_in_fox'][o], W['b_fox_f'][o])
            if prompt:
                att = _fox_prompt_attn(q, k, v, lf)
            else:
                pt = st['page_table']
                npast = pt.shape[1] * PAGE_SIZE
                kp = st['cache_fox_k'][o][pt].reshape(bsz, npast, FOX_HEADS, FOX_HEAD_DIM)
                vp = st['cache_fox_v'][o][pt].reshape(bsz, npast, FOX_HEADS, FOX_HEAD_DIM)
                lfp = st['cache_fox_lf'][o][pt].reshape(bsz, npast, FOX_HEADS)
                att = _fox_sample_attn(q, k, v, lf, kp, vp, lfp)
            mix = att.reshape(bsz, seq, FOX_W) @ W['w_out_fox'][o]
            out['fox_k'].append(k)
            out['fox_v'].append(v)
            out['fox_lf'].append(lf.astype(dty))
        h = h + mix
        if prompt:
            mk, mv = _mem_kv(st['mem'], W['norm_mem'][layer], W['wk_x'][layer], W['wv_x'][layer])
            out['mem_k'].append(mk)
            out['mem_v'].append(mv)
        else:
            mk, mv = st['cache_mem_k'][layer], st['cache_mem_v'][layer]
        h = h + _cross_attn(rmsnorm(h, W['norm_x'][layer]), mk, mv, W['wq_x'][layer], W['wo_x'][layer])
        h = h + _swiglu(rmsnorm(h, W['norm_ffn'][layer]), W['w1'][layer], W['w3'][layer], W['w2'][layer])
    y = rmsnorm(h, W['norm_final'])
    new = {n: jnp.stack(out[n]) for n in names if out[n]}
    return y, new


def setup_inputs(seed: int = 0) -> dict:
    key = jax.random.key(seed)
    ks = iter(jax.random.split(key, 64))
    f32 = jnp.float32

    def nrm(shape, scale=1.0):
        return jax.random.normal(next(ks), shape, f32) * scale

    def dense(n, fi, fo):
        return nrm((n, fi, fo), fi ** -0.5)

    def gain(shape):
        return 1.0 + nrm(shape, 0.01)

    def dt_bias(n, h):
        dt = jnp.exp(jax.random.uniform(next(ks), (n, h), f32, math.log(1e-3), math.log(1e-1)))
        return dt + jnp.log(-jnp.expm1(-dt))

    def a_log(n, h):
        return jnp.log(jax.random.uniform(next(ks), (n, h), f32, 1.0, 16.0))

    n_pages = PAST_LEN // PAGE_SIZE
    n_used = DEC_BATCH * n_pages
    n_pool = n_used + n_used // 4
    page_table = jax.random.permutation(next(ks), n_pool)[:n_used].reshape(DEC_BATCH, n_pages).astype(jnp.int32)
    fox_head_bias = jnp.linspace(FOX_BIAS_LO, FOX_BIAS_HI, FOX_HEADS, dtype=f32)
    return {
        'x_prompt': nrm((BATCH, SEQ, D_MODEL)),
        'x_sample': nrm((DEC_BATCH, DEC_SEQ, D_MODEL)),
        'mem_prompt': nrm((BATCH, MEM_LEN, D_MODEL)),
        'state_ssd': nrm((N_EVEN, DEC_BATCH, SSD_HEADS, SSD_HEAD_DIM, SSD_STATE), 0.1),
        'state_ssd_conv': nrm((N_EVEN, DEC_BATCH, CONV_K - 1, SSD_CONV_DIM)),
        'state_gdn': nrm((N_EVEN, DEC_BATCH, GDN_HEADS, GDN_DK, GDN_DV), 0.1),
        'state_gdn_conv': nrm((N_EVEN, DEC_BATCH, CONV_K - 1, GDN_CONV_DIM)),
        'cache_fox_k': nrm((N_ODD, n_pool, PAGE_SIZE, FOX_HEADS, FOX_HEAD_DIM)),
        'cache_fox_v': nrm((N_ODD, n_pool, PAGE_SIZE, FOX_HEADS, FOX_HEAD_DIM)),
        'cache_fox_lf': jax.nn.log_sigmoid(fox_head_bias + nrm((N_ODD, n_pool, PAGE_SIZE, FOX_HEADS), 0.5)),
        'page_table': page_table,
        'cache_mem_k': nrm((DEPTH, DEC_BATCH, MEM_LEN, X_HEADS, X_HEAD_DIM)),
        'cache_mem_v': nrm((DEPTH, DEC_BATCH, MEM_LEN, X_HEADS, X_HEAD_DIM)),
        'norm_mix': gain((DEPTH, D_MODEL)),
        'norm_x': gain((DEPTH, D_MODEL)),
        'norm_mem': gain((DEPTH, D_MODEL)),
        'norm_ffn': gain((DEPTH, D_MODEL)),
        'norm_final': gain((D_MODEL,)),
        'w_in_hyb': dense(N_EVEN, D_MODEL, IN_HYB),
        'w_out_hyb': dense(N_EVEN, MIX_HYB, D_MODEL),
        'ssd_conv_w': nrm((N_EVEN, CONV_K, SSD_CONV_DIM), CONV_K ** -0.5),
        'ssd_conv_b': nrm((N_EVEN, SSD_CONV_DIM), 0.02),
        'ssd_dt_bias': dt_bias(N_EVEN, SSD_HEADS),
        'ssd_A_log': a_log(N_EVEN, SSD_HEADS),
        'ssd_D': 1.0 + nrm((N_EVEN, SSD_HEADS), 0.1),
        'ssd_norm': gain((N_EVEN, SSD_INNER)),
        'gdn_conv_w': nrm((N_EVEN, CONV_K, GDN_CONV_DIM), CONV_K ** -0.5),
        'gdn_dt_bias': dt_bias(N_EVEN, GDN_HEADS),
        'gdn_A_log': a_log(N_EVEN, GDN_HEADS),
        'gdn_norm': gain((N_EVEN, GDN_DV)),
        'w_in_fox': dense(N_ODD, D_MODEL, IN_FOX),
        'b_fox_f': fox_head_bias + nrm((N_ODD, FOX_HEADS), 0.1),
        'w_out_fox': dense(N_ODD, FOX_W, D_MODEL),
        'wq_x': dense(DEPTH, D_MODEL, X_W),
        'wk_x': dense(DEPTH, D_MODEL, X_W),
        'wv_x': dense(DEPTH, D_MODEL, X_W),
        'wo_x': dense(DEPTH, X_W, D_MODEL),
        'w1': dense(DEPTH, D_MODEL, D_FF),
        'w3': dense(DEPTH, D_MODEL, D_FF),
        'w2': dense(DEPTH, D_FF, D_MODEL),
    }


def reference(x_prompt, x_sample, mem_prompt, state_ssd, state_ssd_conv, state_gdn, state_gdn_conv,
              cache_fox_k, cache_fox_v, cache_fox_lf, page_table, cache_mem_k, cache_mem_v,
              norm_mix, norm_x, norm_mem, norm_ffn, norm_final, w_in_hyb, w_out_hyb, ssd_conv_w, ssd_conv_b,
              ssd_dt_bias, ssd_A_log, ssd_D, ssd_norm, gdn_conv_w, gdn_dt_bias, gdn_A_log, gdn_norm,
              w_in_fox, b_fox_f, w_out_fox, wq_x, wk_x, wv_x, wo_x, w1, w3, w2):
    W = dict(norm_mix=norm_mix, norm_x=norm_x, norm_mem=norm_mem, norm_ffn=norm_ffn, norm_final=norm_final,
             w_in_hyb=w_in_hyb, w_out_hyb=w_out_hyb, ssd_conv_w=ssd_conv_w, ssd_conv_b=ssd_conv_b,
             ssd_dt_bias=ssd_dt_bias, ssd_A_log=ssd_A_log, ssd_D=ssd_D, ssd_norm=ssd_norm,
             gdn_conv_w=gdn_conv_w, gdn_dt_bias=gdn_dt_bias, gdn_A_log=gdn_A_log, gdn_norm=gdn_norm,
             w_in_fox=w_in_fox, b_fox_f=b_fox_f, w_out_fox=w_out_fox,
             wq_x=wq_x, wk_x=wk_x, wv_x=wv_x, wo_x=wo_x, w1=w1, w3=w3, w2=w2)
    y_prompt, pn = _run_group(x_prompt, W, dict(mem=mem_prompt), True)
    st = dict(state_ssd=state_ssd, state_ssd_conv=state_ssd_conv, state_gdn=state_gdn,
              state_gdn_conv=state_gdn_conv, cache_fox_k=cache_fox_k, cache_fox_v=cache_fox_v,
              cache_fox_lf=cache_fox_lf, page_table=page_table, cache_mem_k=cache_mem_k, cache_mem_v=cache_mem_v)
    y_sample, sn = _run_group(x_sample, W, st, False)
    return (y_prompt, y_sample,
            pn['ssd'], pn['ssd_conv'], pn['gdn'], pn['gdn_conv'],
            pn['fox_k'], pn['fox_v'], pn['fox_lf'], pn['mem_k'], pn['mem_v'],
            sn['ssd'], sn['ssd_conv'], sn['gdn'], sn['gdn_conv'],
            sn['fox_k'], sn['fox_v'], sn['fox_lf'])
```

```python
import numpy as np
import concourse.bass as bass
import concourse.mybir as mybir
from concourse.bass_utils import run_bass_kernel_spmd

F32 = mybir.dt.float32
BF16 = mybir.dt.bfloat16
I32 = mybir.dt.int32
AF = mybir.ActivationFunctionType
ALU = mybir.AluOpType
AX = mybir.AxisListType

ENGS = ("tensor", "vector", "scalar", "gpsimd", "sync")
NSLOT = 24
EPS = 1e-6


class Buf:
    __slots__ = ("name", "w", "r")

    def __init__(self, name=""):
        self.name = name
        self.w = None
        self.r = []


class _Op:
    __slots__ = ("eng", "idx", "fn", "waits", "signal", "semval", "kind", "slot", "slotval")

    def __init__(self, eng, idx, fn, kind):
        self.eng = eng
        self.idx = idx
        self.fn = fn
        self.waits = []
        self.signal = False
        self.semval = None
        self.kind = kind
        self.slot = None
        self.slotval = None


class Sched:
    def __init__(self, nc, same_engine_sync=True):
        self.nc = nc
        self.ops = {e: [] for e in ENGS}
        self.waited = {e: {f: -1 for f in ENGS} for e in ENGS}
        self.slot_waited = {e: {} for e in ENGS}
        self.slots = {e: [[i, 0] for i in range(NSLOT)] for e in ("sync", "gpsimd", "scalar")}
        self.slot_rr = {e: 0 for e in ("sync", "gpsimd", "scalar")}
        self.same_engine_sync = same_engine_sync
        self.dma_since_barrier = []

    def _deps(self, reads, writes):
        deps = []
        for b in reads:
            if b.w is not None:
                deps.append(b.w)
        for b in writes:
            if b.w is not None:
                deps.append(b.w)
            deps.extend(b.r)
        return deps

    def _add_waits(self, op, deps):
        e = op.eng
        for d in deps:
            if d.kind == "c":
                if d.eng == e and (e in ("tensor", "sync") or not self.same_engine_sync):
                    continue
                if self.waited[e][d.eng] >= d.idx:
                    continue
                self.waited[e][d.eng] = d.idx
                d.signal = True
                op.waits.append(d)
            else:
                key = (d.eng, d.slot[0])
                if self.slot_waited[e].get(key, 0) >= d.slotval:
                    continue
                self.slot_waited[e][key] = d.slotval
                op.waits.append(d)

    def op(self, eng, fn, reads=(), writes=()):
        o = _Op(eng, len(self.ops[eng]), fn, "c")
        self._add_waits(o, self._deps(reads, writes))
        self.ops[eng].append(o)
        for b in reads:
            b.r.append(o)
        for b in writes:
            b.w = o
            b.r = []
        return o

    def dma(self, eng, fn, reads=(), writes=()):
        o = _Op(eng, len(self.ops[eng]), fn, "d")
        s = self.slots[eng][self.slot_rr[eng] % NSLOT]
        self.slot_rr[eng] += 1
        self._add_waits(o, self._deps(reads, writes))
        if s[1] > 0:
            key = (eng, s[0])
            if self.slot_waited[eng].get(key, 0) < s[1]:
                self.slot_waited[eng][key] = s[1]
                prev = _Op(eng, -1, None, "d")
                prev.slot = s
                prev.slotval = s[1]
                o.waits.append(prev)
        s[1] += 16
        o.slot = s
        o.slotval = s[1]
        self.ops[eng].append(o)
        self.dma_since_barrier.append(o)
        for b in reads:
            b.r.append(o)
        for b in writes:
            b.w = o
            b.r = []
        return o

    def barrier(self):
        deps = [self.ops[e][-1] for e in ENGS if self.ops[e] and self.ops[e][-1].kind == "c"]
        deps = []
        for e in ENGS:
            for o in reversed(self.ops[e]):
                if o.kind == "c":
                    deps.append(o)
                    break
        latest = {}
        for o in self.dma_since_barrier:
            latest[(o.eng, o.slot[0])] = o
        deps.extend(latest.values())
        self.dma_since_barrier = []
        for e in ENGS:
            o = _Op(e, len(self.ops[e]), lambda engine: engine.nop(), "c")
            self._add_waits(o, deps)
            self.ops[e].append(o)

    def emit(self):
        nc = self.nc
        for e in ENGS:
            v = 0
            for o in self.ops[e]:
                if o.kind == "c" and o.signal:
                    v += 1
                    o.semval = v
        esem = {e: nc.alloc_semaphore("es_" + e) for e in ENGS}
        dsem = {e: [nc.alloc_semaphore("ds_%s_%d" % (e, i)) for i in range(NSLOT)] for e in self.slots}
        ops = self.ops
        final_slots = {e: [(dsem[e][s[0]], s[1]) for s in self.slots[e] if s[1] > 0] for e in self.slots}

        def run(e, engine):
            for o in ops[e]:
                for d in o.waits:
                    if d.kind == "c":
                        engine.wait_ge(esem[d.eng], d.semval)
                    else:
                        engine.wait_ge(dsem[d.eng][d.slot[0]], d.slotval)
                ins = o.fn(engine)
                if o.kind == "c":
                    if o.signal:
                        ins.then_inc(esem[e], 1)
                else:
                    ins.then_inc(dsem[e][o.slot[0]], 16)
            if e == "sync":
                for qe in final_slots:
                    for (sm, val) in final_slots[qe]:
                        engine.wait_ge(sm, val)

        with nc.Block() as block:
            @block.sync
            def _(eng):
                run("sync", eng)

            @block.scalar
            def _(eng):
                run("scalar", eng)

            @block.vector
            def _(eng):
                run("vector", eng)

            @block.gpsimd
            def _(eng):
                run("gpsimd", eng)

            @block.tensor
            def _(eng):
                run("tensor", eng)


class Tl:
    def __init__(self, h, name=""):
        self.h = h
        self.b = Buf(name)

    def __getitem__(self, idx):
        return self.h[idx]


def _bufs(lst):
    out = []
    for x in lst:
        if isinstance(x, Tl):
            out.append(x.b)
        elif isinstance(x, Buf):
            out.append(x)
        elif x is None:
            pass
        else:
            out.extend(_bufs(x))
    return out


class KB:
    SB_LIMIT = 229344

    def __init__(self, nc):
        self.nc = nc
        self.S = Sched(nc)
        self.off = 16512
        self.n = 0
        self.stack = []
        pst = nc.alloc_psum_tensor("psall", [128, 8, 512], F32)
        self.ps_h = pst
        self.ps_b = [Buf("ps%d" % i) for i in range(8)]
        self.ps_rr = 0
        self.ps_limit = 8
        self.dq = 0

    def sb(self, shape, dtype, name="t"):
        nbytes = int(np.prod(shape[1:])) * (4 if dtype in (F32, I32) else 2)
        nbytes = (nbytes + 63) // 64 * 64
        assert self.off + nbytes <= self.SB_LIMIT, ("SBUF overflow", name, self.off, nbytes)
        self.n += 1
        h = self.nc.alloc_sbuf_tensor_at("%s_%d" % (name, self.n), list(shape), dtype, offset=self.off)
        self.off += nbytes
        return Tl(h, name)

    def push(self):
        self.stack.append(self.off)

    def pop(self):
        self.S.barrier()
        self.off = self.stack.pop()

    def dram(self, name, shape, dtype, kind="Internal"):
        return Tl(self.nc.dram_tensor(name, list(shape), dtype, kind=kind).ap(), name)

    def psum(self, n=1):
        i = self.ps_rr
        if n > 1 and i % 2 == 1:
            i += 1
        if i + n > self.ps_limit:
            i = 0
        self.ps_rr = (i + n) % self.ps_limit
        return list(range(i, i + n))

    def psf(self, banks):
        if len(banks) == 1:
            return self.ps_h[:, banks[0], :]
        return self.ps_h[:, banks[0]:banks[0] + len(banks), :]

    def psb(self, bank):
        return self.ps_h[:, bank, :].bitcast(BF16)

    def pb(self, banks):
        return [self.ps_b[i] for i in banks]

    def mm(self, out, lhsT, rhs, start, stop, R, W, **kw):
        self.S.op("tensor", lambda e: e.matmul(out, lhsT=lhsT, rhs=rhs, start=start, stop=stop, **kw), _bufs(R), _bufs(W))

    def tr(self, out, in_, ident, R, W):
        self.S.op("tensor", lambda e: e.transpose(out=out, in_=in_, identity=ident), _bufs(R), _bufs(W))

    def act(self, out, in_, func, R, W, **kw):
        self.S.op("scalar", lambda e: e.activation(out=out, in_=in_, func=func, **kw), _bufs(R), _bufs(W))

    def ts(self, out, in0, s1, s2, op0, op1, R, W, eng="vector"):
        if op1 is None:
            self.S.op(eng, lambda e: e.tensor_scalar(out=out, in0=in0, scalar1=s1, scalar2=None, op0=op0), _bufs(R), _bufs(W))
        else:
            self.S.op(eng, lambda e: e.tensor_scalar(out=out, in0=in0, scalar1=s1, scalar2=s2, op0=op0, op1=op1), _bufs(R), _bufs(W))

    def tt(self, out, in0, in1, op, R, W, eng="vector"):
        self.S.op(eng, lambda e: e.tensor_tensor(out=out, in0=in0, in1=in1, op=op), _bufs(R), _bufs(W))

    def stt(self, out, in0, scalar, in1, op0, op1, R, W):
        self.S.op("vector", lambda e: e.scalar_tensor_tensor(out=out, in0=in0, scalar=scalar, in1=in1, op0=op0, op1=op1), _bufs(R), _bufs(W))

    def cp(self, out, in_, R, W, eng="vector"):
        if eng == "scalar":
            self.S.op(eng, lambda e: e.activation(out=out, in_=in_, func=AF.Copy), _bufs(R), _bufs(W))
        else:
            self.S.op(eng, lambda e: e.tensor_copy(out=out, in_=in_), _bufs(R), _bufs(W))

    def red(self, out, in_, R, W, op=ALU.add, axis=AX.X):
        self.S.op("vector", lambda e: e.tensor_reduce(out=out, in_=in_, axis=axis, op=op), _bufs(R), _bufs(W))

    def memset(self, out, val, W, eng="gpsimd"):
        self.S.op(eng, lambda e: e.memset(out, val), (), _bufs(W))

    def dma(self, out, in_, R, W, eng=None, **kw):
        if eng is None:
            eng = "sync"
        self.S.dma(eng, lambda e: e.dma_start(out=out, in_=in_, **kw), _bufs(R), _bufs(W))

    def dmac(self, out, in_, R, W, **kw):
        self.S.dma("gpsimd", lambda e: e.dma_start(out=out, in_=in_, **kw), _bufs(R), _bufs(W))


D = 1024
SSD_H, SSD_P, SSD_N, SSD_G = 16, 64, 64, 2
GDN_H, GDN_DK = 8, 128
IN_HYB = 6432
FOX_H, FOX_D = 16, 64
IN_FOX = 3088
DFF = 2816
MEM = 256
BIG = 30000.0


def make_consts(kb):
    c = {}
    nc = kb.nc
    S = kb.S
    ones_b = kb.sb([128, 128], BF16, "ones_b")
    ones_f = kb.sb([128, 128], F32, "ones_f")
    id_b = kb.sb([128, 128], BF16, "id_b")
    id_f = kb.sb([128, 128], F32, "id_f")
    tri_b = kb.sb([128, 128], BF16, "tri_b")
    tri_f = kb.sb([128, 128], F32, "tri_f")
    tri2_b = kb.sb([128, 128], BF16, "tri2_b")
    blk2_b = kb.sb([128, 128], BF16, "blk2_b")
    pos2 = kb.sb([128, 128], F32, "pos2")
    pos2i = kb.sb([128, 128], F32, "pos2i")
    neg2T = kb.sb([128, 128], F32, "neg2T")
    kb.memset(ones_b[:], 1.0, [ones_b])
    kb.memset(ones_f[:], 1.0, [ones_f])
    for t_, fill in ((id_b, 0.0), (id_f, 0.0)):
        kb.memset(t_[:], 1.0, [t_])
        S.op("gpsimd", lambda e, t_=t_: e.affine_select(out=t_[:], in_=t_[:], pattern=[[-1, 128]], compare_op=ALU.is_equal,
                                                         fill=0.0, base=0, channel_multiplier=1), [t_.b], [t_.b])
    for t_ in (tri_b, tri_f, tri2_b):
        kb.memset(t_[:], 1.0, [t_])
        S.op("gpsimd", lambda e, t_=t_: e.affine_select(out=t_[:], in_=t_[:], pattern=[[1, 128]], compare_op=ALU.is_ge,
                                                         fill=0.0, base=0, channel_multiplier=-1), [t_.b], [t_.b])
    kb.memset(tri2_b[0:64, 64:128], 0.0, [tri2_b])
    kb.memset(blk2_b[:], 0.0, [blk2_b])
    kb.memset(blk2_b[0:64, 0:64], 1.0, [blk2_b])
    kb.memset(blk2_b[64:128, 64:128], 1.0, [blk2_b])
    kb.memset(pos2[:], 0.0, [pos2])
    S.op("gpsimd", lambda e: e.affine_select(out=pos2[:], in_=pos2[:], pattern=[[-1, 128]], compare_op=ALU.is_gt,
                                             fill=BIG, base=0, channel_multiplier=1), [pos2.b], [pos2.b])
    kb.memset(pos2[64:128, 0:64], BIG, [pos2])
    kb.memset(neg2T[:], 0.0, [neg2T])
    S.op("gpsimd", lambda e: e.affine_select(out=neg2T[:], in_=neg2T[:], pattern=[[1, 128]], compare_op=ALU.is_ge,
                                             fill=-BIG, base=0, channel_multiplier=-1), [neg2T.b], [neg2T.b])
    kb.memset(neg2T[0:64, 64:128], -BIG, [neg2T])
    pos1 = kb.sb([128, 128], F32, "pos1")
    neg1T = kb.sb([128, 128], F32, "neg1T")
    kb.memset(pos1[:], 0.0, [pos1])
    S.op("gpsimd", lambda e: e.affine_select(out=pos1[:], in_=pos1[:], pattern=[[-1, 128]], compare_op=ALU.is_gt,
                                             fill=BIG, base=0, channel_multiplier=1), [pos1.b], [pos1.b])
    kb.memset(neg1T[:], 0.0, [neg1T])
    S.op("gpsimd", lambda e: e.affine_select(out=neg1T[:], in_=neg1T[:], pattern=[[1, 128]], compare_op=ALU.is_ge,
                                             fill=-BIG, base=0, channel_multiplier=-1), [neg1T.b], [neg1T.b])
    c.update(pos1=pos1, neg1T=neg1T)
    c.update(ones_b=ones_b, ones_f=ones_f, id_b=id_b, id_f=id_f, tri_b=tri_b, tri_f=tri_f, tri2_b=tri2_b,
             blk2_b=blk2_b, pos2=pos2, neg2T=neg2T)
    return c


def load_rowrep(kb, dram_ap_1d, n, name, rows=128):
    t = kb.sb([rows, n], F32, name)
    kb.dma(t[:], dram_ap_1d.partition_broadcast(rows), [], [t])
    return t


def load_w(kb, w2d, c0, c1, name, kchunks=None):
    K = w2d.shape[0]
    kc = K // 128
    t = kb.sb([128, kc, c1 - c0], BF16, name)
    src = w2d.rearrange("(c p) n -> p c n", p=128)
    for k in range(kc):
        kb.dmac(t[:, k, :], src[:, k, c0:c1], [], [t])
    return t


def rstd_from_ss(kb, ss_ap, out_ap, n, R, W, eps=EPS):
    kb.act(out_ap, ss_ap, AF.Ln, R, W, scale=1.0 / n, bias=eps)
    kb.act(out_ap, out_ap, AF.Exp, W, W, scale=-0.5)


def stage_norm_to_fm(kb, C, src_tm, tok0, ntok, gain_fm, A, a_tok0, tiles=128):
    kb.push()
    xt = [kb.sb([128, D], F32, "xt") for _ in range(2)]
    junk = kb.sb([128, D], BF16, "junk")
    hn = [kb.sb([128, D], BF16, "hn") for _ in range(2)]
    st = [kb.sb([128, 4], F32, "st") for _ in range(2)]
    hT = [kb.sb([128, 8, 128], BF16, "hT") for _ in range(2)]
    nt = (ntok + tiles - 1) // tiles
    for i in range(nt):
        L = min(tiles, ntok - i * tiles)
        x_, h_, s_, o_ = xt[i % 2], hn[i % 2], st[i % 2], hT[i % 2]
        r0 = tok0 + i * tiles
        kb.dma(x_[0:L, :], src_tm[r0:r0 + L, :], [], [x_])
        kb.act(junk[0:L, :], x_[0:L, :], AF.Square, [x_], [junk, s_], accum_out=s_[0:L, 0:1])
        rstd_from_ss(kb, s_[0:L, 0:1], s_[0:L, 1:2], D, [s_], [s_])
        kb.ts(h_[0:L, :], x_[0:L, :], s_[0:L, 1:2], None, ALU.mult, None, [x_, s_], [h_])
        bk = kb.psum(1)
        pv = kb.psb(bk[0])
        for c in range(8):
            kb.tr(pv[:, c * 128:c * 128 + L], h_[0:L, c * 128:(c + 1) * 128], C["id_b"][0:L, 0:L], [h_, C["id_b"]], kb.pb(bk))
        kb.tt(o_[:, :, 0:L], pv.rearrange("p (c t) -> p c t", c=8)[:, :, 0:L], gain_fm[:, :].unsqueeze(2).broadcast_to([128, 8, L]),
              ALU.mult, [kb.pb(bk), gain_fm], [o_])
        kb.dma(A[:, :, a_tok0 + i * tiles:a_tok0 + i * tiles + L].rearrange("c p t -> p c t"), o_[:, :, 0:L], [o_], [])
    kb.pop()


import os as _os
_STOP = int(_os.environ.get('SSD_STOP', '0'))
_GSTOP = int(_os.environ.get('GDN_STOP', '0'))
_GSUB = int(_os.environ.get('GDN_SUB', '0'))
_BSTOP = int(_os.environ.get('B_STOP', '0'))
_FA = int(_os.environ.get('FA_STOP', '0'))


def softplus_(kb, out_ap, in_ap, R, W):
    kb.act(out_ap, in_ap, AF.Exp, R, W)
    kb.act(out_ap, out_ap, AF.Ln, W, W, bias=1.0)


def ssd_weights(kb, P):
    w = P["w_in_hyb"]
    W = {}
    W["wz"] = load_w(kb, w, 0, 1024, "wz")
    W["wx"] = load_w(kb, w, 1024, 2304, "wx")
    W["wdt"] = load_w(kb, w, 2304, 2320, "wdt")
    cw = kb.sb([128, 10, 4], F32, "ssd_cw")
    for j in range(4):
        kb.dma(cw[:, :, j], P["ssd_conv_w"][j].rearrange("(c p) -> p c", p=128), [], [cw], allow_slow_non_contiguous=True)
    cb = kb.sb([128, 10], F32, "ssd_cb")
    kb.dma(cb[:], P["ssd_conv_b"].rearrange("(c p) -> p c", p=128), [], [cb], allow_slow_non_contiguous=True)
    W["cw"], W["cb"] = cw, cb
    W["dtb"] = load_rowrep(kb, P["ssd_dt_bias"], 16, "dtb")
    al = load_rowrep(kb, P["ssd_A_log"], 16, "alog")
    kb.act(al[:], al[:], AF.Exp, [al], [al])
    kb.ts(al[:], al[:], -1.0, None, ALU.mult, None, [al], [al])
    W["a"] = al
    W["dsk"] = load_rowrep(kb, P["ssd_D"], 16, "dsk")
    W["nw"] = load_rowrep(kb, P["ssd_norm"], 1024, "ssdnw")
    return W


def ssd_stream(kb, C, W, A, a_tok0, T, L, M, m_tok0, out_state, out_conv, init_state=None, init_conv=None):
    GT = min(256, T)
    ngroups = T // GT
    ntile = GT // L
    id_b, id_f, tri_b, ones_b = C["id_b"], C["id_f"], C["tri_b"], C["ones_b"]
    kb.push()
    HT = kb.sb([128, 16, 64], F32, "HT")
    HTb = kb.sb([128, 16, 64], BF16, "HTb")
    xp = kb.sb([128, 10, GT + 3], F32, "xp")
    ST = kb.sb([64, 16, 128], F32, "ST")
    if init_state is None:
        kb.memset(HT[:], 0.0, [HT])
        kb.memset(HTb[:], 0.0, [HTb])
        kb.memset(xp[:, :, 0:3], 0.0, [xp])
    else:
        kb.memset(ST[:], 0.0, [ST])
        kb.dma(ST[:, 0:8, 0:64], init_state[0:8].rearrange("h p n -> p h n"), [], [ST])
        kb.dma(ST[:, 8:16, 64:128], init_state[8:16].rearrange("h p n -> p h n"), [], [ST])
        for hh in range(2):
            bk = kb.psum(1)
            pv = kb.psf(bk).rearrange("p (h q) -> p h q", h=8)
            for h in range(8):
                kb.tr(pv[:, h, :], ST[:, hh * 8 + h, :], id_f[0:64, 0:64], [ST, id_f], kb.pb(bk))
            kb.cp(HT[:, hh * 8:hh * 8 + 8, :], pv, kb.pb(bk), [HT])
        kb.cp(HTb[:], HT[:], [HT], [HTb], eng="gpsimd")
        for r in range(3):
            kb.dma(xp[:, :, r], init_conv[r].rearrange("(c p) -> p c", p=128), [], [xp], allow_slow_non_contiguous=True)
    hT = [kb.sb([128, 8, GT], BF16, "hTg") for _ in range(2)]
    xc = [kb.sb([128, 10, GT], BF16, "xc") for _ in range(2)]
    acc = [kb.sb([128, GT], F32, "cacc") for _ in range(2)]
    mixT = [kb.sb([128, 8, GT], BF16, "mixT") for _ in range(2)]
    NB = 2
    sz = [kb.sb([128, 1024], BF16, "sz") for _ in range(NB)]
    sm = [kb.sb([128, 128], F32, "sm") for _ in range(NB)]
    smb = [kb.sb([128, 16], BF16, "smb") for _ in range(NB)]
    Xt = [kb.sb([128, 1024], BF16, "Xt") for _ in range(NB)]
    BCt = [kb.sb([128, 256], BF16, "BCt") for _ in range(NB)]
    TS = [kb.sb([128, 16, L], BF16, "TS") for _ in range(NB)]
    Lm = [kb.sb([128, 16, L], F32, "Lm") for _ in range(NB)]
    Lmb = [kb.sb([128, 16, L], BF16, "Lmb") for _ in range(NB)]
    Gm = [kb.sb([128, 2, L], BF16, "Gm") for _ in range(NB)]
    MT = [kb.sb([128, 16, L], BF16, "MT") for _ in range(NB)]
    Xdt = [kb.sb([128, 1024], BF16, "Xdt") for _ in range(NB)]
    Xw = [kb.sb([128, 1024], BF16, "Xw") for _ in range(NB)]
    y1 = [kb.sb([128, 1024], F32, "y1") for _ in range(NB)]
    y3 = [kb.sb([128, 1024], F32, "y3") for _ in range(NB)]
    mx = [kb.sb([128, 1024], BF16, "mx") for _ in range(NB)]
    junk = kb.sb([128, 512], BF16, "junk")
    Cblk = [kb.sb([128, 2, L], BF16, "Cblk") for _ in range(NB)]
    for q in range(NB):
        kb.memset(Cblk[q][:], 0.0, [Cblk[q]])
    tcount = 0
    for g in range(ngroups):
        t0 = g * GT
        h_, xc_, mT_ = hT[g % 2], xc[g % 2], mixT[g % 2]
        kb.dma(h_[:], A[:, :, a_tok0 + t0:a_tok0 + t0 + GT].rearrange("c p t -> p c t"), [], [h_])
        for cc in range(10):
            bk = kb.psum(1)
            pv = kb.psf(bk)[:, 0:GT]
            for k in range(8):
                kb.mm(pv, W["wx"][:, k, cc * 128:(cc + 1) * 128], h_[:, k, :], k == 0, k == 7, [W["wx"], h_], kb.pb(bk))
            kb.cp(xp[:, cc, 3:3 + GT], pv, kb.pb(bk), [xp], eng="scalar")
            a_ = acc[cc % 2]
            kb.ts(a_[:], xp[:, cc, 0:GT], W["cw"][:, cc, 0:1], W["cb"][:, cc:cc + 1], ALU.mult, ALU.add, [xp, W["cw"], W["cb"]], [a_])
            for j in range(1, 4):
                kb.stt(a_[:], xp[:, cc, j:j + GT], W["cw"][:, cc, j:j + 1], a_[:], ALU.mult, ALU.add, [xp, W["cw"], a_], [a_])
            kb.act(xc_[:, cc, :], a_[:], AF.Silu, [a_], [xc_])
        for i in range(ntile):
            q = tcount % NB
            tcount += 1
            cs = slice(i * L, (i + 1) * L)
            sz_, sm_, smb_, Xt_, BCt_, TS_, Lm_, Lmb_, Gm_, MT_, Xdt_, Xw_, y1_, y3_, mx_ = (
                sz[q], sm[q], smb[q], Xt[q], BCt[q], TS[q], Lm[q], Lmb[q], Gm[q], MT[q], Xdt[q], Xw[q], y1[q], y3[q], mx[q])
            bz = kb.psum(2)
            for nb in range(2):
                for k in range(8):
                    kb.mm(kb.psf([bz[nb]])[0:L, :], h_[:, k, cs], W["wz"][:, k, nb * 512:(nb + 1) * 512], k == 0, k == 7,
                          [h_, W["wz"]], kb.pb([bz[nb]]))
            kb.act(sz_[0:L, :], kb.psf(bz).rearrange("p a b -> p (a b)")[0:L, :], AF.Silu, kb.pb(bz), [sz_])
            if _STOP and _STOP <= 3:
                continue
            bd = kb.psum(1)
            pd = kb.psf(bd)
            for k in range(8):
                kb.mm(pd[0:L, 0:16], h_[:, k, cs], W["wdt"][:, k, :], k == 0, k == 7, [h_, W["wdt"]], kb.pb(bd))
            kb.tt(sm_[0:L, 0:16], pd[0:L, 0:16], W["dtb"][0:L, :], ALU.add, [kb.pb(bd), W["dtb"]], [sm_])
            softplus_(kb, sm_[0:L, 0:16], sm_[0:L, 0:16], [sm_], [sm_])
            kb.tt(sm_[0:L, 16:32], sm_[0:L, 0:16], W["a"][0:L, :], ALU.mult, [sm_, W["a"]], [sm_])
            kb.cp(smb_[0:L, :], sm_[0:L, 16:32], [sm_], [smb_])
            if _STOP and _STOP <= 4:
                continue
            bx = kb.psum(1)
            px = kb.psb(bx[0])
            for c in range(8):
                kb.tr(px[0:L, c * 128:(c + 1) * 128], xc_[:, c, cs], id_b[:, :], [xc_, id_b], kb.pb(bx))
            kb.cp(Xt_[0:L, :], px[0:L, :], kb.pb(bx), [Xt_], eng="scalar")
            bb = kb.psum(1)
            pbc = kb.psb(bb[0])
            for c in range(2):
                kb.tr(pbc[0:L, c * 128:(c + 1) * 128], xc_[:, 8 + c, cs], id_b[:, :], [xc_, id_b], kb.pb(bb))
            kb.cp(BCt_[0:L, :], pbc[0:L, 0:256], kb.pb(bb), [BCt_], eng="scalar")
            if _STOP and _STOP <= 5:
                continue
            bc_ = kb.psum(1)
            pc = kb.psf(bc_)
            kb.mm(pc[0:L, 0:16], tri_b[0:L, 0:L], smb_[0:L, :], True, True, [tri_b, smb_], kb.pb(bc_))
            kb.cp(sm_[0:L, 32:48], pc[0:L, 0:16], kb.pb(bc_), [sm_])
            kb.tt(TS_[0:L, :, :], tri_b[0:L, 0:L].unsqueeze(1).broadcast_to([L, 16, L]),
                  smb_[0:L, :].unsqueeze(2).broadcast_to([L, 16, L]), ALU.mult, [tri_b, smb_], [TS_])
            nbk = max(1, (16 * L) // 512)
            bcb = kb.psum(nbk)
            hpb = 16 // nbk
            for nb in range(nbk):
                kb.mm(kb.psf([bcb[nb]])[:, 0:hpb * L], ones_b[0:L, :], TS_[0:L, nb * hpb:(nb + 1) * hpb, :].rearrange("p h t -> p (h t)"),
                      True, True, [ones_b, TS_], kb.pb([bcb[nb]]))
            if nbk > 1:
                pcbf = kb.psf(bcb).rearrange("p a (h t) -> p (a h) t", t=L)
            else:
                pcbf = kb.psf(bcb)[:, 0:16 * L].rearrange("p (h t) -> p h t", t=L)
            pcb = pcbf[0:L]
            cumc_b = sm_[0:L, 32:48].unsqueeze(2).broadcast_to([L, 16, L])
            kb.tt(Lm_[0:L], pcb, cumc_b, ALU.subtract, [kb.pb(bcb), sm_], [Lm_])
            kb.ts(Lm_[0:L], Lm_[0:L], 0.0, None, ALU.min, None, [Lm_], [Lm_], eng="gpsimd")
            kb.act(Lmb_[0:L], Lm_[0:L], AF.Exp, [Lm_], [Lmb_])
            if _STOP and _STOP <= 6:
                continue
            kb.act(sm_[0:L, 48:64], sm_[0:L, 32:48], AF.Exp, [sm_], [sm_])
            kb.cp(sm_[0:L, 100:116], pcb[:, :, L - 1], kb.pb(bcb), [sm_])
            kb.act(sm_[:, 64:80], pcbf[:, :, L - 1], AF.Exp, kb.pb(bcb), [sm_])
            kb.tt(sm_[0:L, 80:96], sm_[0:L, 100:116], sm_[0:L, 32:48], ALU.subtract, [sm_], [sm_])
            kb.act(sm_[0:L, 80:96], sm_[0:L, 80:96], AF.Exp, [sm_], [sm_])
            kb.tt(sm_[0:L, 80:96], sm_[0:L, 80:96], sm_[0:L, 0:16], ALU.mult, [sm_], [sm_])
            if _STOP and _STOP <= 7:
                continue
            bg = kb.psum(1)
            pg = kb.psf(bg)[0:L, 0:2 * L].rearrange("p (g t) -> p g t", g=2)
            Cb_ = Cblk[q]
            for gi in range(2):
                kb.cp(Cb_[gi * 64:(gi + 1) * 64, gi, :], xc_[gi * 64:(gi + 1) * 64, 9, cs], [xc_], [Cb_], eng="gpsimd")
            kb.mm(kb.psf(bg)[0:L, 0:2 * L], xc_[:, 8, cs], Cb_[:, :, :].rearrange("p g t -> p (g t)"), True, True, [xc_, Cb_], kb.pb(bg))
            kb.tt(Gm_[0:L], pg, tri_b[0:L, 0:L].unsqueeze(1).broadcast_to([L, 2, L]), ALU.mult, [kb.pb(bg), tri_b], [Gm_])
            for gi in range(2):
                kb.tt(MT_[0:L, gi * 8:(gi + 1) * 8, :], Lmb_[0:L, gi * 8:(gi + 1) * 8, :],
                      Gm_[0:L, gi:gi + 1, :].broadcast_to([L, 8, L]), ALU.mult, [Lmb_, Gm_], [MT_])
            Xv = Xt_[0:L, :].rearrange("p (h q) -> p h q", h=16)
            kb.tt(Xdt_[0:L, :].rearrange("p (h q) -> p h q", h=16), Xv, sm_[0:L, 0:16].unsqueeze(2).broadcast_to([L, 16, 64]),
                  ALU.mult, [Xt_, sm_], [Xdt_])
            kb.tt(Xw_[0:L, :].rearrange("p (h q) -> p h q", h=16), Xv, sm_[0:L, 80:96].unsqueeze(2).broadcast_to([L, 16, 64]),
                  ALU.mult, [Xt_, sm_], [Xw_])
            if _STOP and _STOP <= 8:
                continue
            b1 = kb.psum(2)
            p1 = kb.psf(b1).rearrange("p a b -> p (a b)")
            for h in range(16):
                kb.mm(p1[0:L, h * 64:(h + 1) * 64], MT_[0:L, h, :], Xdt_[0:L, h * 64:(h + 1) * 64], True, True, [MT_, Xdt_], kb.pb([b1[h // 8]]))
            b2 = kb.psum(2)
            p2 = kb.psf(b2).rearrange("p a b -> p (a b)")
            for nb in range(2):
                kb.mm(kb.psf([b2[nb]])[0:L, :], xc_[:, 9, cs], HTb[:, nb * 8:(nb + 1) * 8, :].rearrange("p h q -> p (h q)"), True, True,
                      [xc_, HTb], kb.pb([b2[nb]]))
            kb.tt(y1_[0:L, :].rearrange("p (h q) -> p h q", h=16), p2[0:L, :].rearrange("p (h q) -> p h q", h=16),
                  sm_[0:L, 48:64].unsqueeze(2).broadcast_to([L, 16, 64]), ALU.mult, [kb.pb(b2), sm_], [y1_])
            kb.tt(y1_[0:L, :], y1_[0:L, :], p1[0:L, :], ALU.add, [y1_, kb.pb(b1)], [y1_])
            kb.tt(y3_[0:L, :].rearrange("p (h q) -> p h q", h=16), Xv, W["dsk"][0:L, :].unsqueeze(2).broadcast_to([L, 16, 64]),
                  ALU.mult, [Xt_, W["dsk"]], [y3_])
            kb.tt(y1_[0:L, :], y1_[0:L, :], y3_[0:L, :], ALU.add, [y1_, y3_], [y1_])
            kb.tt(y1_[0:L, :], y1_[0:L, :], sz_[0:L, :], ALU.mult, [y1_, sz_], [y1_])
            if _STOP and _STOP <= 9:
                continue
            bu = kb.psum(2)
            for nb in range(2):
                kb.mm(kb.psf([bu[nb]])[:, :], BCt_[0:L, 0:128], Xw_[0:L, nb * 512:(nb + 1) * 512], True, True, [BCt_, Xw_], kb.pb([bu[nb]]))
            for gi in range(2):
                ps_ = slice(gi * 64, (gi + 1) * 64)
                hv = HT[ps_, gi * 8:(gi + 1) * 8, :]
                kb.tt(hv, hv, sm_[ps_, 64 + gi * 8:64 + (gi + 1) * 8].unsqueeze(2).broadcast_to([64, 8, 64]), ALU.mult, [HT, sm_], [HT])
                kb.tt(hv, hv, kb.psf([bu[gi]])[ps_, :].rearrange("p (h q) -> p h q", h=8), ALU.add, [HT, kb.pb([bu[gi]])], [HT])
                kb.cp(HTb[ps_, gi * 8:(gi + 1) * 8, :], hv, [HT], [HTb], eng="scalar")
            if _STOP and _STOP <= 10:
                continue
            for gi in range(2):
                kb.act(junk[0:L, :], y1_[0:L, gi * 512:(gi + 1) * 512], AF.Square, [y1_], [junk, sm_], accum_out=sm_[0:L, 96 + gi:97 + gi])
            rstd_from_ss(kb, sm_[0:L, 96:98], sm_[0:L, 98:100], 512, [sm_], [sm_])
            for gi in range(2):
                kb.stt(mx_[0:L, gi * 512:(gi + 1) * 512], y1_[0:L, gi * 512:(gi + 1) * 512], sm_[0:L, 98 + gi:99 + gi],
                       W["nw"][0:L, gi * 512:(gi + 1) * 512], ALU.mult, ALU.mult, [y1_, sm_, W["nw"]], [mx_])
            bm = kb.psum(1)
            pm = kb.psb(bm[0])
            for c in range(8):
                kb.tr(pm[:, c * 128:c * 128 + L], mx_[0:L, c * 128:(c + 1) * 128], id_b[0:L, 0:L], [mx_, id_b], kb.pb(bm))
            kb.cp(mT_[:, :, cs], pm.rearrange("p (c t) -> p c t", c=8)[:, :, 0:L], kb.pb(bm), [mT_])
        kb.dma(M[0:8, :, m_tok0 + t0:m_tok0 + t0 + GT].rearrange("c p t -> p c t"), mT_[:], [mT_], [])
        if g < ngroups - 1:
            kb.cp(xp[:, :, 0:3], xp[:, :, GT:GT + 3], [xp], [xp], eng="gpsimd")
    for r in range(3):
        kb.dma(out_conv[r].rearrange("(c p) -> p c", p=128), xp[:, :, GT + r], [xp], [], allow_slow_non_contiguous=True)
    for q4 in range(4):
        bk = kb.psum(1)
        pv = kb.psf(bk)[0:64, :].rearrange("p (h q) -> p h q", h=4)
        for h in range(4):
            kb.tr(pv[:, h, :], HT[:, q4 * 4 + h, :], id_f[:, :], [HT, id_f], kb.pb(bk))
        gi = q4 // 2
        kb.cp(ST[:, q4 * 4:q4 * 4 + 4, 0:64], pv[:, :, gi * 64:(gi + 1) * 64], kb.pb(bk), [ST])
    kb.dma(out_state.rearrange("h p n -> p h n"), ST[:, :, 0:64], [ST], [])
    kb.pop()


def gdn_weights(kb, P):
    w = P["w_in_hyb"]
    W = {}
    W["wqkv"] = load_w(kb, w, 2320, 5392, "wqkv")
    W["wgate"] = load_w(kb, w, 5392, 6416, "wgate")
    W["wba"] = load_w(kb, w, 6416, 6432, "wba")
    cw = kb.sb([128, 24, 4], F32, "gdn_cw")
    for j in range(4):
        kb.dma(cw[:, :, j], P["gdn_conv_w"][j].rearrange("(c p) -> p c", p=128), [], [cw], allow_slow_non_contiguous=True)
    W["cw"] = cw
    W["dtb"] = load_rowrep(kb, P["gdn_dt_bias"], 8, "gdtb")
    al = load_rowrep(kb, P["gdn_A_log"], 8, "galog")
    kb.act(al[:], al[:], AF.Exp, [al], [al])
    kb.ts(al[:], al[:], -1.0, None, ALU.mult, None, [al], [al])
    W["nega"] = al
    W["gn"] = load_rowrep(kb, P["gdn_norm"], 128, "gnorm")
    return W


def gdn_stream(kb, C, W, A, a_tok0, T, L, M, m_tok0, out_state, out_conv, init_state=None, init_conv=None):
    GT = min(128, T)
    ngroups = T // GT
    ntile = GT // L
    CH = 64 if L == 128 else L
    nch = L // CH
    nsteps = {64: 5, 8: 2}[CH]
    id_b, id_f, ones_b = C["id_b"], C["id_f"], C["ones_b"]
    if L == 128:
        tri2, blk2, pos2, neg2T = C["tri2_b"], C["blk2_b"], C["pos2"], C["neg2T"]
    else:
        tri2, blk2, pos2, neg2T = C["tri_b"], C["ones_b"], C["pos1"], C["neg1T"]
    kb.push()
    Sf = kb.sb([128, 8, 128], F32, "Sf")
    Sb = kb.sb([128, 8, 128], BF16, "Sb")
    xp = kb.sb([128, 24, GT + 3], F32, "gxp")
    if init_state is None:
        kb.memset(Sf[:], 0.0, [Sf])
        kb.memset(Sb[:], 0.0, [Sb])
        kb.memset(xp[:, :, 0:3], 0.0, [xp])
    else:
        kb.dma(Sf[:], init_state.rearrange("h k v -> k h v"), [], [Sf])
        kb.cp(Sb[:], Sf[:], [Sf], [Sb])
        for r in range(3):
            kb.dma(xp[:, :, r], init_conv[r].rearrange("(c p) -> p c", p=128), [], [xp], allow_slow_non_contiguous=True)
    hT = [kb.sb([128, 8, GT], BF16, "ghT") for _ in range(2)]
    xc = [kb.sb([128, 24, GT], BF16, "gxc") for _ in range(2)]
    acc = [kb.sb([128, GT], F32, "gacc") for _ in range(2)]
    sqb = [kb.sb([128, GT], BF16, "gsq") for _ in range(2)]
    rin = [kb.sb([128, GT], F32, "grin") for _ in range(2)]
    mixT = [kb.sb([128, 8, GT], BF16, "gmixT") for _ in range(2)]
    sm = [kb.sb([128, 128], F32, "gsm") for _ in range(2)]
    gb = [kb.sb([128, 8], BF16, "ggb") for _ in range(2)]
    TS = [kb.sb([128, 8, L], BF16, "gTS") for _ in range(1)] * 2
    Egb = [kb.sb([128, 8, L], BF16, "gEgb") for _ in range(1)] * 2
    eL = [kb.sb([128, 2, 8], F32, "geL") for _ in range(2)]
    gbs = [kb.sb([128, 8, L], F32, "ggbs") for _ in range(1)] * 2
    Qg = [kb.sb([128, 8, L], BF16, "gQg") for _ in range(1)] * 2
    Kbg = [kb.sb([128, 8, 128], BF16, "gKbg") for _ in range(1)] * 2
    K2 = [kb.sb([128, 8, 128], BF16, "gK2") for _ in range(1)] * 2
    Vb = [kb.sb([128, 8, 128], BF16, "gVb") for _ in range(1)] * 2
    tmpf = [kb.sb([128, 4, L], F32, "gtmp") for _ in range(2)]
    dec = [kb.sb([128, 4, L], F32, "gdec") for _ in range(2)]
    Nf = [kb.sb([128, 4, L], F32, "gN") for _ in range(2)]
    Rf = [kb.sb([128, 4, L], F32, "gR") for _ in range(2)]
    Pf = [kb.sb([128, 4, L], F32, "gP") for _ in range(2)]
    Qf = [kb.sb([128, 4, L], F32, "gQ") for _ in range(2)]
    Xf = [kb.sb([128, 4, L], F32, "gX") for _ in range(2)]
    PmT = [kb.sb([128, 8, L], BF16, "gPmT") for _ in range(1)] * 2
    TTp = [[kb.sb([128, 8, L], BF16, "gTTp") for _ in range(nch)] for _ in range(2)]
    nWp = [[kb.sb([128, 8, L], BF16, "gnWp") for _ in range(nch)] for _ in range(2)]
    for q in range(2):
        for c in range(nch):
            kb.memset(TTp[q][c][:], 0.0, [TTp[q][c]])
            kb.memset(nWp[q][c][:], 0.0, [nWp[q][c]])
    Vnb = [kb.sb([128, 4, 128], BF16, "gVnb") for _ in range(2)]
    Of = [kb.sb([128, 8, 128], F32, "gOf") for _ in range(1)] * 2
    Osq = kb.sb([128, 8, 128], BF16, "gOsq")
    sg = [kb.sb([128, 1024], BF16, "gsg") for _ in range(1)] * 2
    mxb = [kb.sb([128, 1024], BF16, "gmx") for _ in range(1)] * 2
    osm = [kb.sb([128, 16], F32, "gosm") for _ in range(2)]
    tcount = 0
    ccount = 0
    for g in range(ngroups):
        t0 = g * GT
        h_, xc_, mT_ = hT[g % 2], xc[g % 2], mixT[g % 2]
        kb.dma(h_[:], A[:, :, a_tok0 + t0:a_tok0 + t0 + GT].rearrange("c p t -> p c t"), [], [h_])
        for cc in range(24):
            bk = kb.psum(1)
            pv = kb.psf(bk)[:, 0:GT]
            for k in range(8):
                kb.mm(pv, W["wqkv"][:, k, cc * 128:(cc + 1) * 128], h_[:, k, :], k == 0, k == 7, [W["wqkv"], h_], kb.pb(bk))
            kb.cp(xp[:, cc, 3:3 + GT], pv, kb.pb(bk), [xp], eng="scalar")
            a_ = acc[cc % 2]
            kb.ts(a_[:], xp[:, cc, 0:GT], W["cw"][:, cc, 0:1], None, ALU.mult, None, [xp, W["cw"]], [a_])
            for j in range(1, 4):
                kb.stt(a_[:], xp[:, cc, j:j + GT], W["cw"][:, cc, j:j + 1], a_[:], ALU.mult, ALU.add, [xp, W["cw"], a_], [a_])
            kb.act(xc_[:, cc, :], a_[:], AF.Silu, [a_], [xc_])
            if cc < 16:
                s_, r_ = sqb[cc % 2], rin[cc % 2]
                kb.act(s_[:], xc_[:, cc, :], AF.Square, [xc_], [s_])
                b2_ = kb.psum(1)
                p2_ = kb.psf(b2_)[:, 0:GT]
                kb.mm(p2_, ones_b[:, :], s_[:], True, True, [ones_b, s_], kb.pb(b2_))
                kb.act(r_[:], p2_, AF.Ln, kb.pb(b2_), [r_], bias=1e-6)
                kb.act(r_[:], r_[:], AF.Exp, [r_], [r_], scale=-0.5)
                kb.stt(xc_[:, cc, :], xc_[:, cc, :], (GDN_DK ** -0.5) if cc < 8 else 1.0, r_[:], ALU.mult, ALU.mult, [xc_, r_], [xc_])
        for i in range(ntile):
            if _GSTOP and _GSTOP <= 1:
                continue
            q = tcount % 2
            tcount += 1
            cs = slice(i * L, (i + 1) * L)
            sm_, gb_, TS_, Egb_, eL_, Qg_, Kbg_, K2_, Vb_, PmT_ = sm[q], gb[q], TS[q], Egb[q], eL[q], Qg[q], Kbg[q], K2[q], Vb[q], PmT[q]
            bd = kb.psum(1)
            pd = kb.psf(bd)
            for k in range(8):
                kb.mm(pd[0:L, 0:16], h_[:, k, cs], W["wba"][:, k, :], k == 0, k == 7, [h_, W["wba"]], kb.pb(bd))
            kb.act(sm_[0:L, 0:8], pd[0:L, 0:8], AF.Exp, kb.pb(bd), [sm_], scale=-1.0)
            kb.ts(sm_[0:L, 0:8], sm_[0:L, 0:8], 1.0, None, ALU.add, None, [sm_], [sm_])
            kb.S.op("vector", lambda e, o_=sm_[0:L, 0:8]: e.reciprocal(out=o_, in_=o_), [sm_.b], [sm_.b])
            kb.tt(sm_[0:L, 8:16], pd[0:L, 8:16], W["dtb"][0:L, :], ALU.add, [kb.pb(bd), W["dtb"]], [sm_])
            softplus_(kb, sm_[0:L, 8:16], sm_[0:L, 8:16], [sm_], [sm_])
            kb.tt(sm_[0:L, 8:16], sm_[0:L, 8:16], W["nega"][0:L, :], ALU.mult, [sm_, W["nega"]], [sm_])
            kb.cp(gb_[0:L, :], sm_[0:L, 8:16], [sm_], [gb_])
            if _GSUB and _GSUB <= 1:
                continue
            bc_ = kb.psum(1)
            pc = kb.psf(bc_)
            kb.mm(pc[0:L, 0:8], tri2[0:L, 0:L], gb_[0:L, :], True, True, [tri2, gb_], kb.pb(bc_))
            kb.mm(pc[0:L, 8:16], blk2[0:L, 0:L], gb_[0:L, :], True, True, [blk2, gb_], kb.pb(bc_))
            kb.cp(sm_[0:L, 16:32], pc[0:L, 0:16], kb.pb(bc_), [sm_])
            if _GSUB and _GSUB <= 2:
                continue
            kb.tt(TS_[0:L, :, :], tri2[0:L, 0:L].unsqueeze(1).broadcast_to([L, 8, L]),
                  gb_[0:L, :].unsqueeze(2).broadcast_to([L, 8, L]), ALU.mult, [tri2, gb_], [TS_])
            nbk = max(1, (8 * L) // 512)
            hpb = 8 // nbk
            bgm = kb.psum(nbk)
            for nb in range(nbk):
                kb.mm(kb.psf([bgm[nb]])[:, 0:hpb * L], ones_b[0:L, :], TS_[0:L, nb * hpb:(nb + 1) * hpb, :].rearrange("p h t -> p (h t)"),
                      True, True, [ones_b, TS_], kb.pb([bgm[nb]]))
            if nbk > 1:
                gbc = kb.psf(bgm).rearrange("p a (h t) -> p (a h) t", t=L)
            else:
                gbc = kb.psf(bgm)[:, 0:8 * L].rearrange("p (h t) -> p h t", t=L)
            if _GSUB and _GSUB <= 3:
                continue
            kb.act(sm_[0:L, 32:40], sm_[0:L, 16:24], AF.Exp, [sm_], [sm_])
            kb.tt(sm_[0:L, 32:40], sm_[0:L, 32:40], sm_[0:L, 0:8], ALU.mult, [sm_], [sm_])
            kb.tt(sm_[0:L, 40:48], sm_[0:L, 24:32], sm_[0:L, 16:24], ALU.subtract, [sm_], [sm_])
            kb.act(sm_[0:L, 40:48], sm_[0:L, 40:48], AF.Exp, [sm_], [sm_])
            kb.ts(sm_[0:L, 48:56], sm_[0:L, 0:8], -1.0, None, ALU.mult, None, [sm_], [sm_])
            if _GSUB and _GSUB <= 4:
                continue
            kb.act(Egb_[:], gbc, AF.Exp, kb.pb(bgm), [Egb_])
            for c in range(nch):
                kb.act(eL_[:, c, :], gbc[:, :, (c + 1) * CH - 1], AF.Exp, kb.pb(bgm), [eL_])
            if _GSUB and _GSUB <= 5:
                continue
            gbs_ = gbs[q]
            kb.cp(gbs_[:], gbc, kb.pb(bgm), [gbs_], eng="scalar")
            if _GSUB and _GSUB <= 6:
                continue
            kb.tt(Qg_[:], xc_[:, 0:8, cs], Egb_[:], ALU.mult, [xc_, Egb_], [Qg_])
            if _GSTOP and _GSTOP <= 2:
                continue
            bk_ = kb.psum(1)
            pk = kb.psb(bk_[0])
            for h in range(8):
                kb.tr(pk[0:L, h * 128:(h + 1) * 128], xc_[:, 8 + h, cs], id_b[:, :], [xc_, id_b], kb.pb(bk_))
            pk3 = pk[0:L, :].rearrange("p (h d) -> p h d", h=8)
            kb.tt(Kbg_[0:L], pk3, sm_[0:L, 32:40].unsqueeze(2).broadcast_to([L, 8, 128]), ALU.mult, [kb.pb(bk_), sm_], [Kbg_])
            kb.tt(K2_[0:L], pk3, sm_[0:L, 40:48].unsqueeze(2).broadcast_to([L, 8, 128]), ALU.mult, [kb.pb(bk_), sm_], [K2_])
            bv_ = kb.psum(1)
            pvv = kb.psb(bv_[0])
            for h in range(8):
                kb.tr(pvv[0:L, h * 128:(h + 1) * 128], xc_[:, 16 + h, cs], id_b[:, :], [xc_, id_b], kb.pb(bv_))
            kb.tt(Vb_[0:L], pvv[0:L, :].rearrange("p (h d) -> p h d", h=8), sm_[0:L, 0:8].unsqueeze(2).broadcast_to([L, 8, 128]),
                  ALU.mult, [kb.pb(bv_), sm_], [Vb_])
            if _GSTOP and _GSTOP <= 3:
                continue
            for hq in range(2):
                hs = slice(hq * 4, hq * 4 + 4)
                w = (tcount * 2 + hq) % 2
                tmp_, dec_, N_, R_, P_, Q_, X_ = tmpf[w], dec[w], Nf[w], Rf[w], Pf[w], Qf[w], Xf[w]
                nbq = max(1, (4 * L) // 512)
                bG = kb.psum(nbq)
                pG = kb.psf(bG)[0:L, 0:4 * L].rearrange("p (h t) -> p h t", h=4)
                bKQ = kb.psum(nbq)
                pKQ = kb.psf(bKQ)[0:L, 0:4 * L].rearrange("p (h t) -> p h t", h=4)
                for h in range(4):
                    hh = hq * 4 + h
                    kb.mm(pG[:, h, :], xc_[:, 8 + hh, cs], xc_[:, 8 + hh, cs], True, True, [xc_], kb.pb(bG))
                    kb.mm(pKQ[:, h, :], xc_[:, 8 + hh, cs], xc_[:, hh, cs], True, True, [xc_], kb.pb(bKQ))
                if _GSTOP and _GSTOP <= 4:
                    continue
                gam_b = sm_[0:L, 16 + hq * 4:16 + hq * 4 + 4].unsqueeze(2).broadcast_to([L, 4, L])
                kb.tt(tmp_[0:L], gbs_[0:L, hs, :], gam_b, ALU.subtract, [gbs_, sm_], [tmp_])
                kb.tt(tmp_[0:L], tmp_[0:L], pos2[0:L, 0:L].unsqueeze(1).broadcast_to([L, 4, L]), ALU.max, [tmp_, pos2], [tmp_])
                kb.act(dec_[0:L], tmp_[0:L], AF.Exp, [tmp_], [dec_], scale=-1.0)
                kb.tt(N_[0:L], pG, dec_[0:L], ALU.mult, [kb.pb(bG), dec_], [N_])
                kb.tt(N_[0:L], N_[0:L], sm_[0:L, 48 + hq * 4:48 + hq * 4 + 4].unsqueeze(2).broadcast_to([L, 4, L]), ALU.mult, [N_, sm_], [N_])
                kb.tt(tmp_[0:L], gbs_[0:L, hs, :], gam_b, ALU.subtract, [gbs_, sm_], [tmp_])
                kb.tt(tmp_[0:L], tmp_[0:L], neg2T[0:L, 0:L].unsqueeze(1).broadcast_to([L, 4, L]), ALU.min, [tmp_, neg2T], [tmp_])
                kb.act(dec_[0:L], tmp_[0:L], AF.Exp, [tmp_], [dec_])
                kb.tt(PmT_[0:L, hs, :], pKQ, dec_[0:L], ALU.mult, [kb.pb(bKQ), dec_], [PmT_])
                if _GSTOP and _GSTOP <= 5:
                    continue
                bR = kb.psum(nbq)
                pR = kb.psf(bR)[0:L, 0:4 * L].rearrange("p (h t) -> p h t", h=4)
                for h in range(4):
                    kb.tr(pR[:, h, :], N_[0:L, h, :], id_f[0:L, 0:L], [N_, id_f], kb.pb(bR))
                kb.cp(R_[0:L], pR, kb.pb(bR), [R_], eng="scalar")
                kb.tt(X_[0:L], pR, id_f[0:L, 0:L].unsqueeze(1).broadcast_to([L, 4, L]), ALU.add, [kb.pb(bR), id_f], [X_])
                if _GSTOP and _GSTOP <= 6:
                    continue
                Pc, Qc = N_, R_
                Pn, Qn = P_, Q_
                for m in range(1, nsteps + 1):
                    bP = kb.psum(nbq)
                    pP = kb.psf(bP)[0:L, 0:4 * L].rearrange("p (h t) -> p h t", h=4)
                    for h in range(4):
                        kb.mm(pP[:, h, :], Qc[0:L, h, :], Pc[0:L, h, :], True, True, [Qc, Pc], kb.pb(bP))
                    if m < nsteps:
                        bQ = kb.psum(nbq)
                        pQ = kb.psf(bQ)[0:L, 0:4 * L].rearrange("p (h t) -> p h t", h=4)
                        for h in range(4):
                            kb.mm(pQ[:, h, :], Pc[0:L, h, :], Qc[0:L, h, :], True, True, [Qc, Pc], kb.pb(bQ))
                    kb.cp(Pn[0:L], pP, kb.pb(bP), [Pn], eng="scalar")
                    if m < nsteps:
                        kb.cp(Qn[0:L], pQ, kb.pb(bQ), [Qn])
                    bX = kb.psum(nbq)
                    pX = kb.psf(bX)[0:L, 0:4 * L].rearrange("p (h t) -> p h t", h=4)
                    for h in range(4):
                        kb.mm(pX[:, h, :], Pn[0:L, h, :], X_[0:L, h, :], True, True, [Pn, X_], kb.pb(bX))
                    kb.tt(X_[0:L], X_[0:L], pX, ALU.add, [X_, kb.pb(bX)], [X_])
                    Pc, Pn = Pn, Pc
                    Qc, Qn = Qn, Qc
                    if m == 1:
                        pass
                if _GSTOP and _GSTOP <= 7:
                    continue
                for c in range(nch):
                    ccs = slice(c * CH, (c + 1) * CH)
                    kb.cp(TTp[q][c][0:L, hs, ccs], X_[0:L, :, ccs], [X_], [TTp[q][c]], eng="gpsimd")
                bW = kb.psum(1)
                pW = kb.psf(bW)[:, 0:4 * L].rearrange("p (h t) -> p h t", h=4)
                for h in range(4):
                    hh = hq * 4 + h
                    for c in range(nch):
                        ccs = slice(c * CH, (c + 1) * CH)
                        kb.mm(pW[:, h, ccs], Kbg_[0:L, hh, :], TTp[q][c][0:L, hh, ccs], True, True, [Kbg_, TTp[q][c]], kb.pb(bW))
                for c in range(nch):
                    ccs = slice(c * CH, (c + 1) * CH)
                    kb.ts(nWp[q][c][:, hs, ccs], pW[:, :, ccs], -1.0, None, ALU.mult, None, kb.pb(bW), [nWp[q][c]])
            if _GSTOP and _GSTOP <= 8:
                continue
            for c in range(nch):
                ccs = slice(c * CH, (c + 1) * CH)
                tcs = slice(i * L + c * CH, i * L + (c + 1) * CH)
                o = ccount % 2
                ccount += 1
                Of_, sg_, mx_, osm_ = Of[o], sg[o], mxb[o], osm[o]
                bz = kb.psum(2)
                for nb in range(2):
                    for k in range(8):
                        kb.mm(kb.psf([bz[nb]])[0:CH, :], h_[:, k, tcs], W["wgate"][:, k, nb * 512:(nb + 1) * 512], k == 0, k == 7,
                              [h_, W["wgate"]], kb.pb([bz[nb]]))
                kb.act(sg_[0:CH, :], kb.psf(bz).rearrange("p a b -> p (a b)")[0:CH, :], AF.Silu, kb.pb(bz), [sg_])
                for hq in range(2):
                    hs = slice(hq * 4, hq * 4 + 4)
                    v_ = Vnb[hq]
                    bV = kb.psum(1)
                    pV = kb.psf(bV).rearrange("p (h v) -> p h v", h=4)
                    for h in range(4):
                        hh = hq * 4 + h
                        kb.mm(pV[0:L, h, :], TTp[q][c][0:L, hh, :], Vb_[0:L, hh, :], True, False, [TTp[q][c], Vb_], kb.pb(bV))
                        kb.mm(pV[0:L, h, :], nWp[q][c][:, hh, :], Sb[:, hh, :], False, True, [nWp[q][c], Sb], kb.pb(bV))
                    kb.cp(v_[0:L], pV[0:L], kb.pb(bV), [v_], eng="scalar")
                    bO = kb.psum(1)
                    pO = kb.psf(bO).rearrange("p (h v) -> p h v", h=4)
                    bS = kb.psum(1)
                    pS = kb.psf(bS).rearrange("p (h v) -> p h v", h=4)
                    for h in range(4):
                        hh = hq * 4 + h
                        kb.mm(pO[0:CH, h, :], Qg_[:, hh, ccs], Sb[:, hh, :], True, False, [Qg_, Sb], kb.pb(bO))
                        kb.mm(pO[0:CH, h, :], PmT_[0:L, hh, ccs], v_[0:L, h, :], False, True, [PmT_, v_], kb.pb(bO))
                    for h in range(4):
                        hh = hq * 4 + h
                        kb.mm(pS[:, h, :], K2_[0:L, hh, :], v_[0:L, h, :], True, True, [K2_, v_], kb.pb(bS))
                    kb.cp(Of_[0:CH, hs, :], pO[0:CH], kb.pb(bO), [Of_], eng="scalar")
                    kb.tt(Sf[:, hs, :], Sf[:, hs, :], eL_[:, c, hs].unsqueeze(2).broadcast_to([128, 4, 128]), ALU.mult, [Sf, eL_], [Sf])
                    kb.tt(Sf[:, hs, :], Sf[:, hs, :], pS, ALU.add, [Sf, kb.pb(bS)], [Sf])
                    kb.cp(Sb[:, hs, :], Sf[:, hs, :], [Sf], [Sb], eng="gpsimd")
                kb.tt(Osq[0:CH], Of_[0:CH], Of_[0:CH], ALU.mult, [Of_], [Osq], eng="gpsimd")
                kb.red(osm_[0:CH, 0:8], Osq[0:CH], [Osq], [osm_])
                rstd_from_ss(kb, osm_[0:CH, 0:8], osm_[0:CH, 8:16], 128, [osm_], [osm_])
                kb.tt(Of_[0:CH], Of_[0:CH], osm_[0:CH, 8:16].unsqueeze(2).broadcast_to([CH, 8, 128]), ALU.mult, [Of_, osm_], [Of_])
                kb.tt(Of_[0:CH], Of_[0:CH], W["gn"][0:CH, :].unsqueeze(1).broadcast_to([CH, 8, 128]), ALU.mult, [Of_, W["gn"]], [Of_], eng="gpsimd")
                kb.tt(mx_[0:CH, :], Of_[0:CH].rearrange("p h v -> p (h v)"), sg_[0:CH, :], ALU.mult, [Of_, sg_], [mx_])
                bm = kb.psum(1)
                pm = kb.psb(bm[0])
                for cc in range(8):
                    kb.tr(pm[:, cc * 128:cc * 128 + CH], mx_[0:CH, cc * 128:(cc + 1) * 128], id_b[0:CH, 0:CH], [mx_, id_b], kb.pb(bm))
                kb.cp(mT_[:, :, tcs], pm.rearrange("p (c t) -> p c t", c=8)[:, :, 0:CH], kb.pb(bm), [mT_])
        kb.dma(M[8:16, :, m_tok0 + t0:m_tok0 + t0 + GT].rearrange("c p t -> p c t"), mT_[:], [mT_], [])
        if g < ngroups - 1:
            kb.cp(xp[:, :, 0:3], xp[:, :, GT:GT + 3], [xp], [xp], eng="gpsimd")
    for r in range(3):
        kb.dma(out_conv[r].rearrange("(c p) -> p c", p=128), xp[:, :, GT + r], [xp], [], allow_slow_non_contiguous=True)
    kb.dma(out_state.rearrange("h k v -> k h v"), Sf[:], [Sf], [])
    kb.pop()


def load_gain_fm(kb, g1d, name):
    t = kb.sb([128, 8], F32, name)
    kb.dma(t[:], g1d.rearrange("(c p) -> p c", p=128), [], [t], allow_slow_non_contiguous=True)
    return t


def norm_tile_to_fm(kb, C, h_, L, gain_fm, st_, hn_, oT_, junk):
    kb.act(junk[0:L, :], h_[0:L, :], AF.Square, [h_], [junk, st_], accum_out=st_[0:L, 0:1])
    rstd_from_ss(kb, st_[0:L, 0:1], st_[0:L, 1:2], D, [st_], [st_])
    kb.ts(hn_[0:L, :], h_[0:L, :], st_[0:L, 1:2], None, ALU.mult, None, [h_, st_], [hn_])
    bk = kb.psum(1)
    pv = kb.psb(bk[0])
    for c in range(8):
        kb.tr(pv[:, c * 128:c * 128 + L], hn_[0:L, c * 128:(c + 1) * 128], C["id_b"][0:L, 0:L], [hn_, C["id_b"]], kb.pb(bk))
    kb.tt(oT_[:, :, 0:L], pv.rearrange("p (c t) -> p c t", c=8)[:, :, 0:L], gain_fm[:, :].unsqueeze(2).broadcast_to([128, 8, L]),
          ALU.mult, [kb.pb(bk), gain_fm], [oT_])


def xattn_kv_from_mem(kb, C, mem_tm, g_mem, wk2d, wv2d, out_k, out_v):
    gfm = load_gain_fm(kb, g_mem, "gmem")
    KmT = kb.sb([128, 4, 256], BF16, "KmT")
    Vaug = kb.sb([128, 2, 4, 130], BF16, "Vaug")
    kb.memset(Vaug[:, :, :, 128:129], 1.0, [Vaug])
    kb.push()
    wk = load_w(kb, wk2d, 0, 512, "wk")
    wv = load_w(kb, wv2d, 0, 512, "wv")
    mT = kb.sb([128, 8, 256], BF16, "memT")
    xt = kb.sb([128, D], F32, "mxt")
    junk = kb.sb([128, D], BF16, "mjunk")
    hn = kb.sb([128, D], BF16, "mhn")
    st = kb.sb([128, 4], F32, "mst")
    oT = kb.sb([128, 8, 128], BF16, "moT")
    ko = kb.sb([128, 512], F32, "mko")
    for i in range(2):
        kb.dma(xt[:], mem_tm[i * 128:(i + 1) * 128, :], [], [xt])
        norm_tile_to_fm(kb, C, xt, 128, gfm, st, hn, oT, junk)
        kb.cp(mT[:, :, i * 128:(i + 1) * 128], oT[:], [oT], [mT], eng="gpsimd")
    for i in range(2):
        for (w_, out_, isv) in ((wk, out_k, False), (wv, out_v, True)):
            bk = kb.psum(1)
            pv = kb.psf(bk)
            for k in range(8):
                kb.mm(pv, mT[:, k, i * 128:(i + 1) * 128], w_[:, k, :], k == 0, k == 7, [mT, w_], kb.pb(bk))
            kb.cp(ko[:], pv, kb.pb(bk), [ko], eng="scalar")
            kb.dma(out_[i * 128:(i + 1) * 128, :], ko[:], [ko], [])
            if isv:
                kb.cp(Vaug[:, i, :, 0:128], ko[:, :].rearrange("p (h d) -> p h d", h=4), [ko], [Vaug])
    for h in range(4):
        bk = kb.psum(1)
        pv = kb.psf(bk)[:, 0:256]
        for k in range(8):
            kb.mm(pv, wk[:, k, h * 128:(h + 1) * 128], mT[:, k, :], k == 0, k == 7, [wk, mT], kb.pb(bk))
        kb.cp(KmT[:, h, :], pv, kb.pb(bk), [KmT])
    kb.pop()
    return KmT, Vaug


def xattn_kv_from_cache(kb, C, ck, cv, KmT, Vaug):
    kb.push()
    kt = kb.sb([128, 2, 512], BF16, "ckt")
    kb.dmac(kt[:], ck.rearrange("(c p) h d -> p c (h d)", p=128), [], [kt])
    for i in range(2):
        kb.dmac(Vaug[:, i, :, 0:128], cv[i * 128:(i + 1) * 128], [], [Vaug])
    bk = kb.psum(1)
    pv = kb.psb(bk[0])
    for i in range(2):
        for h in range(4):
            kb.tr(pv[:, (i * 4 + h) * 128:(i * 4 + h + 1) * 128], kt[:, i, h * 128:(h + 1) * 128], C["id_b"][:, :], [kt, C["id_b"]], kb.pb(bk))
    for i in range(2):
        kb.cp(KmT[:, :, i * 128:(i + 1) * 128], pv[:, i * 512:(i + 1) * 512].rearrange("p (h m) -> p h m", h=4), kb.pb(bk), [KmT])
    kb.pop()


def stage_mix_xattn(kb, C, tiles, M, KC, wout2d, resid_src, H, A, P, layer, kv_for_tile):
    kb.push()
    wout = load_w(kb, wout2d, 0, 1024, "wout")
    wq = load_w(kb, P["wq_x"][layer], 0, 512, "wq")
    wo = load_w(kb, P["wo_x"][layer], 0, 1024, "wo")
    gx = load_gain_fm(kb, P["norm_x"][layer], "gx")
    gf = load_gain_fm(kb, P["norm_ffn"][layer], "gf")
    mT = [kb.sb([128, KC, 128], BF16, "xmT") for _ in range(2)]
    rs = [kb.sb([128, D], F32, "xrs") for _ in range(2)]
    hh = [kb.sb([128, D], F32, "xh") for _ in range(2)]
    junk = kb.sb([128, D], BF16, "xjunk")
    hn = [kb.sb([128, D], BF16, "xhn") for _ in range(2)]
    st = [kb.sb([128, 8], F32, "xst") for _ in range(2)]
    oT = [kb.sb([128, 8, 128], BF16, "xoT") for _ in range(2)]
    qT = [kb.sb([128, 4, 128], BF16, "xqT") for _ in range(2)]
    pT = [kb.sb([128, 8, 128], BF16, "xpT") for _ in range(2)]
    on = [kb.sb([128, 4, 129], F32, "xon") for _ in range(2)]
    ob = [kb.sb([128, 512], BF16, "xob") for _ in range(2)]
    obT = [kb.sb([128, 4, 128], BF16, "xobT") for _ in range(2)]
    sc = X_HEAD_SCALE
    for ti, (tok0, L, rsrc) in enumerate(tiles):
        q = ti % 2
        KmT, Vaug = kv_for_tile(ti)
        m_, r_, h_, hn_, st_, oT_, qT_, pT_, on_, ob_, obT_ = mT[q], rs[q], hh[q], hn[q], st[q], oT[q], qT[q], pT[q], on[q], ob[q], obT[q]
        kb.dma(m_[:, :, 0:L], M[0:KC, :, tok0:tok0 + L].rearrange("c p t -> p c t"), [], [m_])
        kb.dma(r_[0:L, :], rsrc, [], [r_])
        bo = kb.psum(2)
        for nb in range(2):
            for k in range(KC):
                kb.mm(kb.psf([bo[nb]])[0:L, :], m_[:, k, 0:L], wout[:, k, nb * 512:(nb + 1) * 512], k == 0, k == KC - 1, [m_, wout], kb.pb([bo[nb]]))
        kb.tt(h_[0:L, :], r_[0:L, :], kb.psf(bo).rearrange("p a b -> p (a b)")[0:L, :], ALU.add, [r_, kb.pb(bo)], [h_])
        norm_tile_to_fm(kb, C, h_, L, gx, st_, hn_, oT_, junk)
        bq = kb.psum(1)
        pq = kb.psf(bq).rearrange("p (h t) -> p h t", h=4)
        for h in range(4):
            for k in range(8):
                kb.mm(pq[:, h, 0:L], wq[:, k, h * 128:(h + 1) * 128], oT_[:, k, 0:L], k == 0, k == 7, [wq, oT_], kb.pb(bq))
        kb.cp(qT_[:, :, 0:L], pq[:, :, 0:L], kb.pb(bq), [qT_], eng="scalar")
        bs = kb.psum(2)
        psc = kb.psf(bs).rearrange("p a (x t) -> p (a x) t", t=128)
        for h in range(4):
            for mc in range(2):
                kb.mm(psc[:, h * 2 + mc, 0:L], KmT[:, h, mc * 128:(mc + 1) * 128], qT_[:, h, 0:L], True, True, [KmT, qT_], kb.pb([bs[(h * 2 + mc) // 4]]))
        kb.act(pT_[:, :, 0:L], psc[:, :, 0:L], AF.Exp, kb.pb(bs), [pT_], scale=sc)
        bv = kb.psum(2)
        pvv = kb.psf(bv)
        for h in range(4):
            for mc in range(2):
                kb.mm(pvv[0:L, h // 2, (h % 2) * 129:(h % 2) * 129 + 129], pT_[:, h * 2 + mc, 0:L], Vaug[:, mc, h, 0:129], mc == 0, mc == 1,
                      [pT_, Vaug], kb.pb([bv[h // 2]]))
        for a in range(2):
            kb.cp(on_[0:L, a * 2:a * 2 + 2, :], pvv[0:L, a, 0:258].rearrange("p (h d) -> p h d", h=2), kb.pb([bv[a]]), [on_], eng="scalar")
        kb.S.op("vector", lambda e, o_=st_[0:L, 4:8], i_=on_[0:L, :, 128]: e.reciprocal(out=o_, in_=i_), [on_.b], [st_.b])
        kb.tt(ob_[0:L, :].rearrange("p (h d) -> p h d", h=4), on_[0:L, :, 0:128], st_[0:L, 4:8].unsqueeze(2).broadcast_to([L, 4, 128]),
              ALU.mult, [on_, st_], [ob_])
        bt = kb.psum(1)
        pt = kb.psb(bt[0])
        for c in range(4):
            kb.tr(pt[:, c * 128:c * 128 + L], ob_[0:L, c * 128:(c + 1) * 128], C["id_b"][0:L, 0:L], [ob_, C["id_b"]], kb.pb(bt))
        kb.cp(obT_[:, :, 0:L], pt[:, 0:512].rearrange("p (c t) -> p c t", c=4)[:, :, 0:L], kb.pb(bt), [obT_])
        bw = kb.psum(2)
        for nb in range(2):
            for k in range(4):
                kb.mm(kb.psf([bw[nb]])[0:L, :], obT_[:, k, 0:L], wo[:, k, nb * 512:(nb + 1) * 512], k == 0, k == 3, [obT_, wo], kb.pb([bw[nb]]))
        kb.tt(h_[0:L, :], h_[0:L, :], kb.psf(bw).rearrange("p a b -> p (a b)")[0:L, :], ALU.add, [h_, kb.pb(bw)], [h_])
        kb.dma(H[tok0:tok0 + L, :], h_[0:L, :], [h_], [])
        norm_tile_to_fm(kb, C, h_, L, gf, st_, hn_, oT_, junk)
        kb.dma(A[:, :, tok0:tok0 + L].rearrange("c p t -> p c t"), oT_[:, :, 0:L], [oT_], [])
    kb.pop()


X_HEAD_SCALE = 128 ** -0.5


def stage_ffn(kb, C, groups, H, A, P, layer, epilogue, gain, out_rows=None):
    kb.push()
    w1 = load_w(kb, P["w1"][layer], 0, DFF, "w1")
    w3 = load_w(kb, P["w3"][layer], 0, DFF, "w3")
    w2 = load_w(kb, P["w2"][layer], 0, 1024, "w2")
    if epilogue == "fm":
        g_ = load_gain_fm(kb, gain, "gnext")
    else:
        g_ = load_rowrep(kb, gain, 1024, "gfin")
    hT = [kb.sb([128, 8, 512], BF16, "fhT") for _ in range(2)]
    gT = kb.sb([128, 22, 512], BF16, "fgT")
    sa = [kb.sb([128, 512], BF16, "fsa") for _ in range(2)]
    hr = [kb.sb([128, D], F32, "fhr") for _ in range(2)]
    junk = kb.sb([128, D], BF16, "fjunk")
    hn = kb.sb([128, D], BF16, "fhn")
    st = [kb.sb([128, 4], F32, "fst") for _ in range(2)]
    oT = [kb.sb([128, 8, 128], BF16, "foT") for _ in range(2)]
    tcount = 0
    for gi, (tok0, n) in enumerate(groups):
        h_ = hT[gi % 2]
        kb.dma(h_[:, :, 0:n], A[:, :, tok0:tok0 + n].rearrange("c p t -> p c t"), [], [h_])
        for f in range(22):
            ba, bb = kb.psum(1), kb.psum(1)
            pa, pb_ = kb.psf(ba)[:, 0:n], kb.psf(bb)[:, 0:n]
            for k in range(8):
                kb.mm(pa, w1[:, k, f * 128:(f + 1) * 128], h_[:, k, 0:n], k == 0, k == 7, [w1, h_], kb.pb(ba))
            for k in range(8):
                kb.mm(pb_, w3[:, k, f * 128:(f + 1) * 128], h_[:, k, 0:n], k == 0, k == 7, [w3, h_], kb.pb(bb))
            s_ = sa[f % 2]
            kb.act(s_[:, 0:n], pa, AF.Silu, kb.pb(ba), [s_])
            kb.tt(gT[:, f, 0:n], s_[:, 0:n], pb_, ALU.mult, [s_, kb.pb(bb)], [gT])
        nt = (n + 127) // 128
        for i in range(nt):
            L = min(128, n - i * 128)
            q = tcount % 2
            tcount += 1
            r_, st_, oT_ = hr[q], st[q], oT[q]
            r0 = tok0 + i * 128
            kb.dma(r_[0:L, :], H[r0:r0 + L, :], [], [r_])
            bo = kb.psum(2)
            for nb in range(2):
                for f in range(22):
                    kb.mm(kb.psf([bo[nb]])[0:L, :], gT[:, f, i * 128:i * 128 + L], w2[:, f, nb * 512:(nb + 1) * 512], f == 0, f == 21, [gT, w2], kb.pb([bo[nb]]))
            kb.tt(r_[0:L, :], r_[0:L, :], kb.psf(bo).rearrange("p a b -> p (a b)")[0:L, :], ALU.add, [r_, kb.pb(bo)], [r_])
            if epilogue == "fm":
                kb.dma(H[r0:r0 + L, :], r_[0:L, :], [r_], [])
                norm_tile_to_fm(kb, C, r_, L, g_, st_, hn, oT_, junk)
                kb.dma(A[:, :, r0:r0 + L].rearrange("c p t -> p c t"), oT_[:, :, 0:L], [oT_], [])
            else:
                kb.act(junk[0:L, :], r_[0:L, :], AF.Square, [r_], [junk, st_], accum_out=st_[0:L, 0:1])
                rstd_from_ss(kb, st_[0:L, 0:1], st_[0:L, 1:2], D, [st_], [st_])
                kb.stt(r_[0:L, :], r_[0:L, :], st_[0:L, 1:2], g_[0:L, :], ALU.mult, ALU.mult, [r_, st_, g_], [r_])
                kb.dma(out_rows(r0, L), r_[0:L, :], [r_], [])
    kb.pop()


def stage_fox_proj(kb, C, groups, A, P, QT, KT, VA, LFS, negF, Fbase, out_k, out_v, out_lf, n_prompt):
    kb.push()
    wf = load_w(kb, P["w_in_fox"], 0, IN_FOX, "wfox")
    bfr = load_rowrep(kb, P["b_fox_f"], 16, "bfox")
    hT = [kb.sb([128, 8, 512], BF16, "phT") for _ in range(2)]
    qs = [kb.sb([64, 16, 512], BF16, "pqs") for _ in range(2)]
    ks = [kb.sb([64, 16, 512], BF16, "pks") for _ in range(2)]
    kf = [kb.sb([128, 1024], F32, "pkf") for _ in range(2)]
    vf = [kb.sb([128, 1024], F32, "pvf") for _ in range(2)]
    va = [kb.sb([128, 16, 66], BF16, "pva") for _ in range(2)]
    for q in range(2):
        kb.memset(va[q][:], 0.0, [va[q]])
        kb.memset(va[q][:, :, 0:1], 1.0, [va[q]])
    lf = [kb.sb([128, 32], F32, "plf") for _ in range(2)]
    carry = kb.sb([128, 16], F32, "pcarry")
    kb.memset(carry[:], 0.0, [carry])
    tcount = 0
    for gi, (tok0, n) in enumerate(groups):
        h_, qs_, ks_ = hT[gi % 2], qs[gi % 2], ks[gi % 2]
        kb.dma(h_[:, :, 0:n], A[:, :, tok0:tok0 + n].rearrange("c p t -> p c t"), [], [h_])
        for (dst, c0) in ((qs_, 0), (ks_, 1024)):
            for h in range(16):
                bk = kb.psum(1)
                pv = kb.psf(bk)[0:64, 0:n]
                for k in range(8):
                    kb.mm(pv, wf[:, k, c0 + h * 64:c0 + (h + 1) * 64], h_[:, k, 0:n], k == 0, k == 7, [wf, h_], kb.pb(bk))
                kb.cp(dst[:, h, 0:n], pv, kb.pb(bk), [dst], eng="scalar" if h % 2 else "vector")
        kb.dma(QT[:, :, tok0:tok0 + n].rearrange("h d t -> d h t"), qs_[:, :, 0:n], [qs_], [])
        kb.dma(KT[:, :, tok0:tok0 + n].rearrange("h d t -> d h t"), ks_[:, :, 0:n], [ks_], [])
        if tok0 < n_prompt and (tok0 // 512) < 16:
            kb.cp(Fbase[:, tok0 // 512, :], carry[:], [carry], [Fbase], eng="gpsimd")
        nt = (n + 127) // 128
        for i in range(nt):
            L = min(128, n - i * 128)
            q = tcount % 2
            tcount += 1
            kf_, vf_, va_, lf_ = kf[q], vf[q], va[q], lf[q]
            r0 = tok0 + i * 128
            cs = slice(i * 128, i * 128 + L)
            for (dstf, c0, isv) in ((kf_, 1024, False), (vf_, 2048, True)):
                bo = kb.psum(2)
                for nb in range(2):
                    for k in range(8):
                        kb.mm(kb.psf([bo[nb]])[0:L, :], h_[:, k, cs], wf[:, k, c0 + nb * 512:c0 + (nb + 1) * 512], k == 0, k == 7, [h_, wf], kb.pb([bo[nb]]))
                kb.cp(dstf[0:L, :], kb.psf(bo).rearrange("p a b -> p (a b)")[0:L, :], kb.pb(bo), [dstf], eng="scalar")
                if isv:
                    kb.cp(va_[0:L, :, 2:66], dstf[0:L, :].rearrange("p (h d) -> p h d", h=16), [dstf], [va_], eng="gpsimd")
            kb.dma(out_k(r0, L), kf_[0:L, :], [kf_], [])
            kb.dma(out_v(r0, L), vf_[0:L, :], [vf_], [])
            kb.dma(VA[r0:r0 + L], va_[0:L], [va_], [])
            bf_ = kb.psum(1)
            pf = kb.psf(bf_)
            for k in range(8):
                kb.mm(pf[0:L, 0:16], h_[:, k, cs], wf[:, k, 3072:3088], k == 0, k == 7, [h_, wf], kb.pb(bf_))
            kb.tt(lf_[0:L, 0:16], pf[0:L, 0:16], bfr[0:L, :], ALU.add, [kb.pb(bf_), bfr], [lf_])
            kb.act(lf_[0:L, 0:16], lf_[0:L, 0:16], AF.Exp, [lf_], [lf_], scale=-1.0)
            kb.act(lf_[0:L, 0:16], lf_[0:L, 0:16], AF.Ln, [lf_], [lf_], bias=1.0)
            kb.ts(lf_[0:L, 0:16], lf_[0:L, 0:16], -1.0, None, ALU.mult, None, [lf_], [lf_])
            kb.dma(out_lf(r0, L), lf_[0:L, 0:16], [lf_], [])
            if r0 < n_prompt:
                blk = r0 // 128
                bc_ = kb.psum(1)
                pc = kb.psf(bc_)
                kb.mm(pc[:, 0:16], C["tri_f"][:, :], lf_[:, 0:16], True, True, [C["tri_f"], lf_], kb.pb(bc_))
                kb.mm(pc[:, 16:32], C["ones_f"][:, :], lf_[:, 0:16], True, True, [C["ones_f"], lf_], kb.pb(bc_))
                kb.tt(lf_[:, 16:32], pc[:, 0:16], carry[:], ALU.add, [kb.pb(bc_), carry], [lf_])
                kb.ts(negF[:, blk, :], lf_[:, 16:32], -1.0, None, ALU.mult, None, [lf_], [negF])
                kb.tt(carry[:], carry[:], pc[:, 16:32], ALU.add, [carry, kb.pb(bc_)], [carry])
            else:
                kb.dma(LFS[r0 - n_prompt:r0 - n_prompt + L, :], lf_[0:L, 0:16], [lf_], [])
        if tok0 + n == n_prompt:
            kb.cp(Fbase[:, n_prompt // 512, :], carry[:], [carry], [Fbase], eng="gpsimd")
    kb.pop()


def stage_fox_prompt_attn(kb, C, QT, KT, VA, negF, Fbase, M, T):
    kb.push()
    NB = T // 128
    NG = T // 512
    kb.ps_limit = 6
    kb.ps_rr = 0
    masks = kb.sb([128, 4, 512], BF16, "fmask")
    kb.memset(masks[:], 1.0, [masks])
    for r in range(4):
        kb.S.op("gpsimd", lambda e, r=r: e.affine_select(out=masks[:, r, :], in_=masks[:, r, :], pattern=[[1, 512]], compare_op=ALU.is_ge,
                                                          fill=0.0, base=-128 * r, channel_multiplier=-1), [masks.b], [masks.b])
    ktl = [kb.sb([64, T], BF16, "aK") for _ in range(2)]
    qtl = [kb.sb([64, T], BF16, "aQ") for _ in range(2)]
    vtl = [kb.sb([128, NB, 66], BF16, "aV") for _ in range(2)]
    bt = [kb.sb([128, NB], F32, "abt") for _ in range(2)]
    bmid = [kb.sb([128, 2], F32, "abm") for _ in range(2)]
    pT = [kb.sb([128, 512], BF16, "apT") for _ in range(3)]
    accs = [kb.sb([66, 512], F32, "aacc") for _ in range(2)]
    rc = [kb.sb([1, 512], F32, "arc") for _ in range(2)]
    ob = [kb.sb([66, 512], BF16, "aob") for _ in range(2)]
    o2 = [kb.sb([64, 512], BF16, "ao2") for _ in range(2)]
    sel = kb.sb([128, 128], BF16, "asel")
    kb.memset(sel[:], 1.0, [sel])
    kb.S.op("gpsimd", lambda e: e.affine_select(out=sel[:], in_=sel[:], pattern=[[-1, 128]], compare_op=ALU.is_equal,
                                                 fill=0.0, base=-2, channel_multiplier=1), [sel.b], [sel.b])
    pcount = 0
    gcount = 0
    for h in range(16):
        k_, q_, v_ = ktl[h % 2], qtl[h % 2], vtl[h % 2]
        kb.dma(k_[:], KT[h, :, 0:T], [], [k_])
        kb.dma(q_[:], QT[h, :, 0:T], [], [q_])
        kb.dma(v_[:], VA[0:T, h, :].rearrange("(b p) e -> p b e", p=128), [], [v_])
        for g in range(NG):
            if _FA == 1:
                continue
            w = gcount % 2
            gcount += 1
            bt_, acc_, rc_, ob_ = bt[w], accs[w], rc[w], ob[w]
            nj = 4 * (g + 1)
            bm_ = bmid[w]
            kb.tt(bm_[:, 0:1], Fbase[:, g, h:h + 1], Fbase[:, g + 1, h:h + 1], ALU.add, [Fbase], [bm_])
            kb.ts(bm_[:, 0:1], bm_[:, 0:1], 0.5, None, ALU.mult, None, [bm_], [bm_])
            kb.ts(bt_[:, 0:nj], negF[:, 0:nj, h], bm_[:, 0:1], None, ALU.add, None, [negF, bm_], [bt_])
            bacc = [6 + w]
            pacc = kb.psf(bacc)[0:66, :]
            for j in range(nj):
                p_ = pT[pcount % 3]
                pcount += 1
                bs = kb.psum(1)
                ps_ = kb.psf(bs)
                kb.mm(ps_, k_[:, j * 128:(j + 1) * 128], q_[:, g * 512:(g + 1) * 512], True, True, [k_, q_], kb.pb(bs))
                kb.act(p_[:], ps_, AF.Exp, kb.pb(bs) + [bt_.b], [p_], scale=0.125, bias=bt_[:, j:j + 1])
                if j >= 4 * g and _FA != 2:
                    kb.tt(p_[:], p_[:], masks[:, j - 4 * g, :], ALU.mult, [p_, masks], [p_], eng="gpsimd")
                if _FA in (2, 3):
                    continue
                kb.mm(pacc, v_[:, j, :], p_[:], j == 0, j == nj - 1, [v_, p_], kb.pb(bacc))
            if _FA in (2, 3, 4):
                continue
            kb.cp(acc_[:], pacc, kb.pb(bacc), [acc_])
            kb.S.op("vector", lambda e, o_=rc_[:], i_=acc_[0:1, :]: e.reciprocal(out=o_, in_=i_), [acc_.b], [rc_.b])
            if _FA == 5:
                continue
            bb = kb.psum(1)
            pbb = kb.psf(bb)[0:66, :]
            kb.mm(pbb, C["ones_f"][0:1, 0:66], rc_[:], True, True, [C["ones_f"], rc_], kb.pb(bb))
            kb.tt(ob_[:], acc_[:], pbb, ALU.mult, [acc_, kb.pb(bb)], [ob_])
            if _FA == 6:
                continue
            b2_ = kb.psum(1)
            p2_ = kb.psf(b2_)[0:64, :]
            kb.mm(p2_, sel[0:66, 0:64], ob_[:], True, True, [sel, ob_], kb.pb(b2_))
            o2_ = o2[w]
            kb.cp(o2_[:], p2_, kb.pb(b2_), [o2_], eng="scalar")
            kb.dma(M[h // 2, (h % 2) * 64:(h % 2) * 64 + 64, g * 512:(g + 1) * 512], o2_[:], [o2_], [])
    kb.ps_limit = 8
    kb.pop()


def stage_fox_sample_attn(kb, C, QT, KT, VA, LFS, M, n_prompt, nseq, pt_rows, pool_k, pool_v, pool_lf, NPAGE):
    kb.push()
    kb.ps_limit = 5
    kb.ps_rr = 0
    id_b = C["id_b"]
    iota_i = kb.sb([128, 1], I32, "iota_i")
    kb.S.op("gpsimd", lambda e: e.iota(out=iota_i[:], pattern=[[0, 1]], base=0, channel_multiplier=1), [], [iota_i.b])
    iota_f = kb.sb([128, 1], F32, "iota_f")
    kb.cp(iota_f[:], iota_i[:], [iota_i], [iota_f])
    tri_su = kb.sb([128, 128], F32, "tri_su")
    kb.memset(tri_su[:], 1.0, [tri_su])
    kb.S.op("gpsimd", lambda e: e.affine_select(out=tri_su[:], in_=tri_su[:], pattern=[[-1, 128]], compare_op=ALU.is_gt,
                                                 fill=0.0, base=0, channel_multiplier=1), [tri_su.b], [tri_su.b])
    pti = kb.sb([128, NPAGE], I32, "pti")
    ptf = kb.sb([128, NPAGE], F32, "ptf")
    idx = kb.sb([128, NPAGE], I32, "idx")
    Lp = kb.sb([128, NPAGE, 16], F32, "Lp")
    rev = kb.sb([128, NPAGE, 16], F32, "rev")
    pre = kb.sb([128, 16, NPAGE], F32, "pre")
    pre2 = kb.sb([128, 16, NPAGE], F32, "pre2")
    sfx = kb.sb([128, 16, NPAGE], F32, "sfx")
    ones16 = kb.sb([128, NPAGE], F32, "ones16")
    kb.memset(ones16[:], 1.0, [ones16])
    Qblk = kb.sb([128, 8, 16], BF16, "Qblk")
    kb.memset(Qblk[:], 0.0, [Qblk])
    Kp = [kb.sb([128, 1024], F32, "Kp") for _ in range(2)]
    Vp = [kb.sb([128, 1024], F32, "Vp") for _ in range(2)]
    Kb = [kb.sb([128, 1024], BF16, "Kb") for _ in range(2)]
    KT2 = [kb.sb([128, 8, 128], BF16, "KT2") for _ in range(2)]
    Vb = [kb.sb([128, 16, 66], BF16, "Vbs") for _ in range(2)]
    for q in range(2):
        kb.memset(Vb[q][:], 0.0, [Vb[q]])
        kb.memset(Vb[q][:, :, 0:1], 1.0, [Vb[q]])
    tmp = [kb.sb([128, 16, 8], F32, "stmp") for _ in range(2)]
    PT = [kb.sb([128, 16, 8], BF16, "sPT") for _ in range(2)]
    KT2n = kb.sb([128, 8, 8], BF16, "KT2n")
    VAn = kb.sb([8, 16, 66], BF16, "VAn")
    lfn = kb.sb([8, 32], F32, "lfn")
    osb = kb.sb([8, 16, 66], F32, "osb")
    ofull = kb.sb([128, 1056], F32, "ofull")
    OSCt = kb.dram("OSC", [nseq, 8, 16, 66], F32)
    OSC, OSCb = OSCt.h, OSCt.b
    rcp = kb.sb([8, 16], F32, "rcp")
    att = kb.sb([8, 1024], BF16, "satt")
    attT = kb.sb([128, 8, 8], BF16, "sattT")
    pcount = 0
    for s in range(nseq):
        c0 = n_prompt + 8 * s
        kb.dma(pti[:], pt_rows[s].partition_broadcast(128), [], [pti])
        kb.cp(ptf[:], pti[:], [pti], [ptf])
        kb.ts(ptf[:], ptf[:], 128.0, iota_f[:, 0:1], ALU.mult, ALU.add, [ptf, iota_f], [ptf])
        kb.cp(idx[:], ptf[:], [ptf], [idx])
        for j in range(NPAGE):
            kb.S.dma("gpsimd", lambda e, j=j: e.indirect_dma_start(out=Lp[:, j, :], out_offset=None, in_=pool_lf,
                                                                   in_offset=bass.IndirectOffsetOnAxis(ap=idx[:, j:j + 1], axis=0)),
                     [idx.b], [Lp.b])
        Lflat = Lp[:, :, :].rearrange("p j h -> p (j h)")
        nb2 = (NPAGE * 16 + 511) // 512
        for nb in range(nb2):
            w_ = min(512, NPAGE * 16 - nb * 512)
            b1 = kb.psum(1)
            b2 = kb.psum(1)
            kb.mm(kb.psf(b1)[:, 0:w_], tri_su[:, :], Lflat[:, nb * 512:nb * 512 + w_], True, True, [tri_su, Lp], kb.pb(b1))
            kb.mm(kb.psf(b2)[:, 0:w_], C["ones_f"][:, :], Lflat[:, nb * 512:nb * 512 + w_], True, True, [C["ones_f"], Lp], kb.pb(b2))
            jn = w_ // 16
            j0 = nb * 32
            kb.cp(rev[:, j0:j0 + jn, :], kb.psf(b1)[:, 0:w_].rearrange("p (j h) -> p j h", h=16), kb.pb(b1), [rev], eng="scalar")
            kb.cp(pre[:, :, j0:j0 + jn], kb.psf(b2)[:, 0:w_].rearrange("p (j h) -> p h j", h=16), kb.pb(b2), [pre])
        for h in range(16):
            kb.S.op("vector", lambda e, h=h: e.tensor_tensor_scan(out=pre2[:, h, :], data0=ones16[:, :], data1=pre[:, h, :], initial=0.0,
                                                                  op0=ALU.mult, op1=ALU.add), [pre.b, ones16.b], [pre2.b])
        kb.tt(sfx[:, :, :], pre2[:, :, NPAGE - 1:NPAGE].broadcast_to([128, 16, NPAGE]), pre2[:, :, :], ALU.subtract, [pre2], [sfx])
        kb.tt(rev[:, :, :], rev[:, :, :], sfx[:, :, :].rearrange("p h j -> p j h"), ALU.add, [rev, sfx], [rev])
        for c in range(8):
            kb.dma(Qblk[0:64, c, 0:8], QT[2 * c, :, c0:c0 + 8], [], [Qblk])
            kb.dma(Qblk[64:128, c, 8:16], QT[2 * c + 1, :, c0:c0 + 8], [], [Qblk])
        kb.dma(KT2n[:], KT[:, :, c0:c0 + 8].rearrange("(c h2) d t -> (h2 d) c t", h2=2), [], [KT2n])
        kb.dma(VAn[:], VA[c0:c0 + 8], [], [VAn])
        kb.dma(lfn[:, 0:16], LFS[8 * s:8 * s + 8, :], [], [lfn])
        bacc = [5, 6, 7]
        NSP = ((0, 512), (512, 1024), (1024, 1056))
        for j in range(NPAGE):
            w = pcount % 2
            pcount += 1
            Kp_, Vp_, Kb_, KT2_, Vb_, tmp_, PT_ = Kp[w], Vp[w], Kb[w], KT2[w], Vb[w], tmp[w], PT[w]
            kb.S.dma("gpsimd", lambda e, j=j, Kp_=Kp_: e.indirect_dma_start(out=Kp_[:, :], out_offset=None, in_=pool_k,
                                                                            in_offset=bass.IndirectOffsetOnAxis(ap=idx[:, j:j + 1], axis=0)),
                     [idx.b], [Kp_.b])
            kb.S.dma("gpsimd", lambda e, j=j, Vp_=Vp_: e.indirect_dma_start(out=Vp_[:, :], out_offset=None, in_=pool_v,
                                                                            in_offset=bass.IndirectOffsetOnAxis(ap=idx[:, j:j + 1], axis=0)),
                     [idx.b], [Vp_.b])
            kb.cp(Kb_[:], Kp_[:], [Kp_], [Kb_], eng="scalar")
            kb.cp(Vb_[:, :, 2:66], Vp_[:, :].rearrange("p (h d) -> p h d", h=16), [Vp_], [Vb_])
            bt_ = kb.psum(1)
            ptk = kb.psb(bt_[0])
            for c in range(8):
                kb.tr(ptk[:, c * 128:(c + 1) * 128], Kb_[:, c * 128:(c + 1) * 128], id_b[:, :], [Kb_, id_b], kb.pb(bt_))
            kb.cp(KT2_[:], ptk.rearrange("p (c t) -> p c t", c=8), kb.pb(bt_), [KT2_], eng="scalar")
            bs = kb.psum(1)
            pss = kb.psf(bs)[:, 0:128]
            for c in range(8):
                kb.mm(pss[:, c * 16:(c + 1) * 16], KT2_[:, c, :], Qblk[:, c, :], True, True, [KT2_, Qblk], kb.pb(bs))
            kb.stt(tmp_[:], pss.rearrange("p (h q) -> p h q", q=8), 0.125, rev[:, j, :].unsqueeze(2).broadcast_to([128, 16, 8]),
                   ALU.mult, ALU.add, kb.pb(bs) + [rev.b], [tmp_])
            kb.act(PT_[:], tmp_[:], AF.Exp, [tmp_], [PT_])
            for a, (n0, n1) in enumerate(NSP):
                kb.mm(kb.psf([bacc[a]])[:, 0:n1 - n0], PT_[:, :, :].rearrange("p h q -> p (h q)"), Vb_[:, :, :].rearrange("p h e -> p (h e)")[:, n0:n1],
                      j == 0, False, [PT_, Vb_], kb.pb([bacc[a]]))
        bn = kb.psum(1)
        psn = kb.psf(bn)[0:8, 0:128]
        for c in range(8):
            kb.mm(psn[:, c * 16:(c + 1) * 16], KT2n[:, c, :], Qblk[:, c, :], True, True, [KT2n, Qblk], kb.pb(bn))
        bfn = kb.psum(1)
        kb.mm(kb.psf(bfn)[0:8, 0:16], C["tri_f"][0:8, 0:8], lfn[0:8, 0:16], True, True, [C["tri_f"], lfn], kb.pb(bfn))
        kb.ts(lfn[0:8, 16:32], kb.psf(bfn)[0:8, 0:16], -1.0, None, ALU.mult, None, kb.pb(bfn), [lfn])
        tn, PTn = tmp[0], PT[0]
        kb.stt(tn[0:8], psn.rearrange("p (h q) -> p h q", q=8), 0.125, lfn[0:8, 16:32].unsqueeze(2).broadcast_to([8, 16, 8]),
               ALU.mult, ALU.add, kb.pb(bn) + [lfn.b], [tn])
        kb.tt(tn[0:8], tn[0:8], C["neg1T"][0:8, 0:8].unsqueeze(1).broadcast_to([8, 16, 8]), ALU.add, [tn, C["neg1T"]], [tn])
        kb.act(PTn[0:8], tn[0:8], AF.Exp, [tn], [PTn])
        for a, (n0, n1) in enumerate(NSP):
            kb.mm(kb.psf([bacc[a]])[:, 0:n1 - n0], PTn[0:8, :, :].rearrange("p h q -> p (h q)"), VAn[0:8, :, :].rearrange("p h e -> p (h e)")[:, n0:n1],
                  False, True, [PTn, VAn], kb.pb([bacc[a]]))
        for a, (n0, n1) in enumerate(NSP):
            kb.cp(ofull[:, n0:n1], kb.psf([bacc[a]])[:, 0:n1 - n0], kb.pb([bacc[a]]), [ofull], eng="scalar")
        for h in range(16):
            kb.dma(OSC[s, :, h, :], ofull[h * 8:(h + 1) * 8, h * 66:(h + 1) * 66], [ofull], [OSCb])
        kb.dma(osb[:], OSC[s], [OSCb], [osb])
        kb.S.op("vector", lambda e: e.reciprocal(out=rcp[:, :], in_=osb[:, :, 0]), [osb.b], [rcp.b])
        kb.tt(att[:, :].rearrange("p (h d) -> p h d", h=16), osb[:, :, 2:66], rcp[:, :].unsqueeze(2).broadcast_to([8, 16, 64]), ALU.mult,
              [osb, rcp], [att])
        bo = kb.psum(1)
        po = kb.psb(bo[0])
        for c in range(8):
            kb.tr(po[:, c * 128:c * 128 + 8], att[0:8, c * 128:(c + 1) * 128], id_b[0:8, 0:8], [att, id_b], kb.pb(bo))
        kb.cp(attT[:], po.rearrange("p (c t) -> p c t", c=8)[:, :, 0:8], kb.pb(bo), [attT])
        kb.dma(M[0:8, :, c0:c0 + 8].rearrange("c p t -> p c t"), attT[:], [attT], [])
    kb.ps_limit = 8
    kb.pop()


WEIGHT_SHAPES = {
    "norm_mix": [2, 1024], "norm_x": [2, 1024], "norm_mem": [2, 1024], "norm_ffn": [2, 1024], "norm_final": [1024],
    "w_in_hyb": [1, 1024, IN_HYB], "w_out_hyb": [1, 2048, 1024], "ssd_conv_w": [1, 4, 1280], "ssd_conv_b": [1, 1280],
    "ssd_dt_bias": [1, 16], "ssd_A_log": [1, 16], "ssd_D": [1, 16], "ssd_norm": [1, 1024],
    "gdn_conv_w": [1, 4, 3072], "gdn_dt_bias": [1, 8], "gdn_A_log": [1, 8], "gdn_norm": [1, 128],
    "w_in_fox": [1, 1024, IN_FOX], "b_fox_f": [1, 16], "w_out_fox": [1, 1024, 1024],
    "wq_x": [2, 1024, 512], "wk_x": [2, 1024, 512], "wv_x": [2, 1024, 512], "wo_x": [2, 512, 1024],
    "w1": [2, 1024, DFF], "w3": [2, 1024, DFF], "w2": [2, DFF, 1024],
}
NSEQ = 4


def build(T, NPAGE, NPOOL):
    nc = bass.Bass("TRN2", target_bir_lowering=False)
    kb = KB(nc)
    NT = T + 8 * NSEQ

    def inp(n, shp, dt=F32):
        return nc.dram_tensor(n, list(shp), dt, kind="ExternalInput").ap()

    def outp(n, shp):
        return nc.dram_tensor(n, list(shp), F32, kind="ExternalOutput").ap()

    Wd = {n: inp(n, shp) for n, shp in WEIGHT_SHAPES.items()}
    xp = inp("xp", [T, D]); xs = inp("xs", [8 * NSEQ, D]); mem = inp("mem", [MEM, D])
    st_ssd = inp("st_ssd", [NSEQ, 16, 64, 64]); st_ssd_conv = inp("st_ssd_conv", [NSEQ, 3, 1280])
    st_gdn = inp("st_gdn", [NSEQ, 8, 128, 128]); st_gdn_conv = inp("st_gdn_conv", [NSEQ, 3, 3072])
    pool_k = inp("pool_k", [NPOOL, 128, 16, 64]); pool_v = inp("pool_v", [NPOOL, 128, 16, 64]); pool_lf = inp("pool_lf", [NPOOL, 128, 16])
    pt = inp("pt", [NSEQ, NPAGE], I32)
    cmk = inp("cmk", [2, NSEQ, MEM, 4, 128]); cmv = inp("cmv", [2, NSEQ, MEM, 4, 128])
    O = {}
    O["y_p"] = outp("y_p", [T, D]); O["y_s"] = outp("y_s", [8 * NSEQ, D])
    O["p_ssd"] = outp("p_ssd", [16, 64, 64]); O["p_ssd_conv"] = outp("p_ssd_conv", [3, 1280])
    O["p_gdn"] = outp("p_gdn", [8, 128, 128]); O["p_gdn_conv"] = outp("p_gdn_conv", [3, 3072])
    O["p_fox_k"] = outp("p_fox_k", [T, 1024]); O["p_fox_v"] = outp("p_fox_v", [T, 1024]); O["p_fox_lf"] = outp("p_fox_lf", [T, 16])
    O["p_mem_k"] = outp("p_mem_k", [2, MEM, 512]); O["p_mem_v"] = outp("p_mem_v", [2, MEM, 512])
    O["s_ssd"] = outp("s_ssd", [NSEQ, 16, 64, 64]); O["s_ssd_conv"] = outp("s_ssd_conv", [NSEQ, 3, 1280])
    O["s_gdn"] = outp("s_gdn", [NSEQ, 8, 128, 128]); O["s_gdn_conv"] = outp("s_gdn_conv", [NSEQ, 3, 3072])
    O["s_fox_k"] = outp("s_fox_k", [8 * NSEQ, 1024]); O["s_fox_v"] = outp("s_fox_v", [8 * NSEQ, 1024]); O["s_fox_lf"] = outp("s_fox_lf", [8 * NSEQ, 16])
    H = kb.dram("H", [NT, D], F32); A = kb.dram("A", [8, 128, NT], BF16); M = kb.dram("M", [16, 128, NT], BF16)
    QT = kb.dram("QT", [16, 64, NT], BF16); KT = kb.dram("KT", [16, 64, NT], BF16); VA = kb.dram("VA", [NT, 16, 66], BF16)
    LFS = kb.dram("LFS", [8 * NSEQ, 16], F32)
    C = make_consts(kb)
    negF = kb.sb([128, max(T // 128, 1), 16], F32, "negF")
    Fbase = kb.sb([128, T // 512 + 1, 16], F32, "Fbase")
    P0 = {"w_in_hyb": Wd["w_in_hyb"][0], "ssd_conv_w": Wd["ssd_conv_w"][0], "ssd_conv_b": Wd["ssd_conv_b"][0], "ssd_dt_bias": Wd["ssd_dt_bias"][0],
          "ssd_A_log": Wd["ssd_A_log"][0], "ssd_D": Wd["ssd_D"][0], "ssd_norm": Wd["ssd_norm"][0], "gdn_conv_w": Wd["gdn_conv_w"][0],
          "gdn_dt_bias": Wd["gdn_dt_bias"][0], "gdn_A_log": Wd["gdn_A_log"][0], "gdn_norm": Wd["gdn_norm"][0],
          "w_in_fox": Wd["w_in_fox"][0], "b_fox_f": Wd["b_fox_f"][0]}
    PL = {k: Wd[k] for k in ("wq_x", "wo_x", "norm_x", "norm_ffn", "w1", "w3", "w2")}

    kb.push()
    g0 = load_gain_fm(kb, Wd["norm_mix"][0], "g0")
    stage_norm_to_fm(kb, C, xp, 0, T, g0, A, 0)
    stage_norm_to_fm(kb, C, xs, 0, 8 * NSEQ, g0, A, T)
    kb.pop()
    if _BSTOP == 1:
        kb.S.emit()
        return nc
    kb.push()
    W = ssd_weights(kb, P0)
    ssd_stream(kb, C, W, A, 0, T, 128, M, 0, O["p_ssd"], O["p_ssd_conv"])
    for s in range(NSEQ):
        ssd_stream(kb, C, W, A, T + 8 * s, 8, 8, M, T + 8 * s, O["s_ssd"][s], O["s_ssd_conv"][s], st_ssd[s], st_ssd_conv[s])
    kb.pop()
    if _BSTOP == 2:
        kb.S.emit()
        return nc
    kb.push()
    W = gdn_weights(kb, P0)
    gdn_stream(kb, C, W, A, 0, T, 128, M, 0, O["p_gdn"], O["p_gdn_conv"])
    for s in range(NSEQ):
        gdn_stream(kb, C, W, A, T + 8 * s, 8, 8, M, T + 8 * s, O["s_gdn"][s], O["s_gdn_conv"][s], st_gdn[s], st_gdn_conv[s])
    kb.pop()
    if _BSTOP == 3:
        kb.S.emit()
        return nc
    groups = [(g * 512, 512) for g in range(T // 512)] + [(T, 8 * NSEQ)]

    def xattn_layer(layer, KC, wout2d, resid_p, resid_s):
        kb.push()
        KmT, Vaug = xattn_kv_from_mem(kb, C, mem, Wd["norm_mem"][layer], Wd["wk_x"][layer], Wd["wv_x"][layer], O["p_mem_k"][layer], O["p_mem_v"][layer])
        skv = []
        for s in range(NSEQ):
            KmS = kb.sb([128, 4, 256], BF16, "KmS")
            VaS = kb.sb([128, 2, 4, 130], BF16, "VaS")
            kb.memset(VaS[:, :, :, 128:129], 1.0, [VaS])
            xattn_kv_from_cache(kb, C, cmk[layer, s], cmv[layer, s], KmS, VaS)
            skv.append((KmS, VaS))
        tiles = [(i * 128, 128, resid_p[i * 128:(i + 1) * 128, :]) for i in range(T // 128)]
        tiles += [(T + 8 * s, 8, resid_s[8 * s:8 * s + 8, :]) for s in range(NSEQ)]
        npt = T // 128
        stage_mix_xattn(kb, C, tiles, M, KC, wout2d, None, H, A, PL, layer, lambda ti: (KmT, Vaug) if ti < npt else skv[ti - npt])
        kb.pop()

    xattn_layer(0, 16, Wd["w_out_hyb"][0], xp, xs)
    if _BSTOP == 4:
        kb.S.emit()
        return nc
    stage_ffn(kb, C, groups, H, A, PL, 0, "fm", Wd["norm_mix"][1])
    if _BSTOP == 5:
        kb.S.emit()
        return nc
    def rows(ap_p, ap_s):
        return lambda r0, L: (ap_p[r0:r0 + L, :] if r0 < T else ap_s[r0 - T:r0 - T + L, :])
    stage_fox_proj(kb, C, groups, A, P0, QT, KT, VA, LFS, negF, Fbase, rows(O["p_fox_k"], O["s_fox_k"]), rows(O["p_fox_v"], O["s_fox_v"]),
                   rows(O["p_fox_lf"], O["s_fox_lf"]), T)
    if _BSTOP == 6:
        kb.S.emit()
        return nc
    stage_fox_prompt_attn(kb, C, QT, KT, VA, negF, Fbase, M, T)
    if _BSTOP == 7:
        kb.S.emit()
        return nc
    stage_fox_sample_attn(kb, C, QT, KT, VA, LFS, M, T, NSEQ, pt, pool_k.rearrange("n p h d -> (n p) (h d)"),
                          pool_v.rearrange("n p h d -> (n p) (h d)"), pool_lf.rearrange("n p h -> (n p) h"), NPAGE)
    if _BSTOP == 8:
        kb.S.emit()
        return nc
    Hh = H.h
    xattn_layer(1, 8, Wd["w_out_fox"][0], Hh[0:T], Hh[T:NT])
    stage_ffn(kb, C, groups, H, A, PL, 1, "final", Wd["norm_final"], out_rows=rows(O["y_p"], O["y_s"]))
    kb.S.emit()
    return nc


def kernel(**inp):
    f32 = lambda a: np.ascontiguousarray(np.asarray(a), dtype=np.float32)
    xpr = np.asarray(inp["x_prompt"]); xsa = np.asarray(inp["x_sample"])
    B, T, _ = xpr.shape
    DB = xsa.shape[0]
    ptab = np.asarray(inp["page_table"]).astype(np.int32)
    NPAGE = ptab.shape[1]
    ck = np.asarray(inp["cache_fox_k"])[0]; cv = np.asarray(inp["cache_fox_v"])[0]; clf = np.asarray(inp["cache_fox_lf"])[0]
    NPOOL = ck.shape[0]
    ncore = 8
    assert DB == NSEQ * ncore and B * 4 == ncore
    nc = build(T, NPAGE, NPOOL)
    wmap = {n: f32(inp[n]) for n in WEIGHT_SHAPES}
    ck = f32(ck); cv = f32(cv); clf = f32(clf)
    in_maps = []
    for c in range(ncore):
        b = c // 4
        sl = slice(NSEQ * c, NSEQ * (c + 1))
        m = dict(wmap)
        m.update(xp=f32(xpr[b]), xs=f32(xsa[sl].reshape(8 * NSEQ, D)), mem=f32(inp["mem_prompt"][b]),
                 st_ssd=f32(inp["state_ssd"][0, sl]), st_ssd_conv=f32(inp["state_ssd_conv"][0, sl]),
                 st_gdn=f32(inp["state_gdn"][0, sl]), st_gdn_conv=f32(inp["state_gdn_conv"][0, sl]),
                 pool_k=ck, pool_v=cv, pool_lf=clf, pt=np.ascontiguousarray(ptab[sl]),
                 cmk=f32(np.asarray(inp["cache_mem_k"])[:, sl]), cmv=f32(np.asarray(inp["cache_mem_v"])[:, sl]))
        in_maps.append(m)
    res = run_bass_kernel_spmd(nc, in_maps, core_ids=list(range(ncore)))
    R = res.results
    pc = [0, 4]
    cat = lambda name: np.stack([R[c][name] for c in pc])
    scat = lambda name: np.concatenate([R[c][name] for c in range(ncore)], axis=0)
    y_prompt = cat("y_p")
    y_sample = scat("y_s").reshape(DB, 8, D)
    out = (
        y_prompt, y_sample,
        cat("p_ssd")[None], cat("p_ssd_conv")[None], cat("p_gdn")[None], cat("p_gdn_conv")[None],
        cat("p_fox_k").reshape(1, B, T, 16, 64), cat("p_fox_v").reshape(1, B, T, 16, 64), cat("p_fox_lf").reshape(1, B, T, 16),
        np.stack([R[c]["p_mem_k"] for c in pc], axis=1).reshape(2, B, MEM, 4, 128),
        np.stack([R[c]["p_mem_v"] for c in pc], axis=1).reshape(2, B, MEM, 4, 128),
        scat("s_ssd")[None], scat("s_ssd_conv")[None], scat("s_gdn")[None], scat("s_gdn_conv")[None],
        scat("s_fox_k").reshape(1, DB, 8, 16, 64), scat("s_fox_v").reshape(1, DB, 8, 16, 64), scat("s_fox_lf").reshape(1, DB, 8, 16),
    )
    return tuple(np.ascontiguousarray(o, dtype=np.float32) for o in out)
```

```python
import numpy as np
import concourse.bass as bass
import concourse.mybir as mybir
from concourse.bass_utils import run_bass_kernel_spmd

F32 = mybir.dt.float32
BF16 = mybir.dt.bfloat16
I32 = mybir.dt.int32
AF = mybir.ActivationFunctionType
ALU = mybir.AluOpType
AX = mybir.AxisListType

ENGS = ("tensor", "vector", "scalar", "gpsimd", "sync")
NSLOT = 24
EPS = 1e-6


class Buf:
    __slots__ = ("name", "w", "r")

    def __init__(self, name=""):
        self.name = name
        self.w = None
        self.r = []


class _Op:
    __slots__ = ("eng", "idx", "fn", "waits", "signal", "semval", "kind", "slot", "slotval")

    def __init__(self, eng, idx, fn, kind):
        self.eng = eng
        self.idx = idx
        self.fn = fn
        self.waits = []
        self.signal = False
        self.semval = None
        self.kind = kind
        self.slot = None
        self.slotval = None


class Sched:
    def __init__(self, nc, same_engine_sync=True):
        self.nc = nc
        self.ops = {e: [] for e in ENGS}
        self.waited = {e: {f: -1 for f in ENGS} for e in ENGS}
        self.slot_waited = {e: {} for e in ENGS}
        self.slots = {e: [[i, 0] for i in range(NSLOT)] for e in ("sync", "gpsimd", "scalar")}
        self.slot_rr = {e: 0 for e in ("sync", "gpsimd", "scalar")}
        self.same_engine_sync = same_engine_sync
        self.dma_since_barrier = []

    def _deps(self, reads, writes):
        deps = []
        for b in reads:
            if b.w is not None:
                deps.append(b.w)
        for b in writes:
            if b.w is not None:
                deps.append(b.w)
            deps.extend(b.r)
        return deps

    def _add_waits(self, op, deps):
        e = op.eng
        for d in deps:
            if d.kind == "c":
                if d.eng == e and (e in ("tensor", "sync") or not self.same_engine_sync):
                    continue
                if self.waited[e][d.eng] >= d.idx:
                    continue
                self.waited[e][d.eng] = d.idx
                d.signal = True
                op.waits.append(d)
            else:
                key = (d.eng, d.slot[0])
                if self.slot_waited[e].get(key, 0) >= d.slotval:
                    continue
                self.slot_waited[e][key] = d.slotval
                op.waits.append(d)

    def op(self, eng, fn, reads=(), writes=()):
        o = _Op(eng, len(self.ops[eng]), fn, "c")
        self._add_waits(o, self._deps(reads, writes))
        self.ops[eng].append(o)
        for b in reads:
            b.r.append(o)
        for b in writes:
            b.w = o
            b.r = []
        return o

    def dma(self, eng, fn, reads=(), writes=()):
        o = _Op(eng, len(self.ops[eng]), fn, "d")
        s = self.slots[eng][self.slot_rr[eng] % NSLOT]
        self.slot_rr[eng] += 1
        self._add_waits(o, self._deps(reads, writes))
        if s[1] > 0:
            key = (eng, s[0])
            if self.slot_waited[eng].get(key, 0) < s[1]:
                self.slot_waited[eng][key] = s[1]
                prev = _Op(eng, -1, None, "d")
                prev.slot = s
                prev.slotval = s[1]
                o.waits.append(prev)
        s[1] += 16
        o.slot = s
        o.slotval = s[1]
        self.ops[eng].append(o)
        self.dma_since_barrier.append(o)
        for b in reads:
            b.r.append(o)
        for b in writes:
            b.w = o
            b.r = []
        return o

    def barrier(self):
        deps = [self.ops[e][-1] for e in ENGS if self.ops[e] and self.ops[e][-1].kind == "c"]
        deps = []
        for e in ENGS:
            for o in reversed(self.ops[e]):
                if o.kind == "c":
                    deps.append(o)
                    break
        latest = {}
        for o in self.dma_since_barrier:
            latest[(o.eng, o.slot[0])] = o
        deps.extend(latest.values())
        self.dma_since_barrier = []
        for e in ENGS:
            o = _Op(e, len(self.ops[e]), lambda engine: engine.nop(), "c")
            self._add_waits(o, deps)
            self.ops[e].append(o)

    def emit(self):
        nc = self.nc
        for e in ENGS:
            v = 0
            for o in self.ops[e]:
                if o.kind == "c" and o.signal:
                    v += 1
                    o.semval = v
        esem = {e: nc.alloc_semaphore("es_" + e) for e in ENGS}
        dsem = {e: [nc.alloc_semaphore("ds_%s_%d" % (e, i)) for i in range(NSLOT)] for e in self.slots}
        ops = self.ops
        final_slots = {e: [(dsem[e][s[0]], s[1]) for s in self.slots[e] if s[1] > 0] for e in self.slots}

        def run(e, engine):
            for o in ops[e]:
                for d in o.waits:
                    if d.kind == "c":
                        engine.wait_ge(esem[d.eng], d.semval)
                    else:
                        engine.wait_ge(dsem[d.eng][d.slot[0]], d.slotval)
                ins = o.fn(engine)
                if o.kind == "c":
                    if o.signal:
                        ins.then_inc(esem[e], 1)
                else:
                    ins.then_inc(dsem[e][o.slot[0]], 16)
            if e == "sync":
                for qe in final_slots:
                    for (sm, val) in final_slots[qe]:
                        engine.wait_ge(sm, val)

        with nc.Block() as block:
            @block.sync
            def _(eng):
                run("sync", eng)

            @block.scalar
            def _(eng):
                run("scalar", eng)

            @block.vector
            def _(eng):
                run("vector", eng)

            @block.gpsimd
            def _(eng):
                run("gpsimd", eng)

            @block.tensor
            def _(eng):
                run("tensor", eng)


class Tl:
    def __init__(self, h, name=""):
        self.h = h
        self.b = Buf(name)

    def __getitem__(self, idx):
        return self.h[idx]


def _bufs(lst):
    out = []
    for x in lst:
        if isinstance(x, Tl):
            out.append(x.b)
        elif isinstance(x, Buf):
            out.append(x)
        elif x is None:
            pass
        else:
            out.extend(_bufs(x))
    return out


class KB:
    SB_LIMIT = 229344

    def __init__(self, nc):
        self.nc = nc
        self.S = Sched(nc)
        self.off = 16512
        self.n = 0
        self.stack = []
        pst = nc.alloc_psum_tensor("psall", [128, 8, 512], F32)
        self.ps_h = pst
        self.ps_b = [Buf("ps%d" % i) for i in range(8)]
        self.ps_rr = 0
        self.ps_limit = 8
        self.dq = 0

    def sb(self, shape, dtype, name="t"):
        nbytes = int(np.prod(shape[1:])) * (4 if dtype in (F32, I32) else 2)
        nbytes = (nbytes + 63) // 64 * 64
        assert self.off + nbytes <= self.SB_LIMIT, ("SBUF overflow", name, self.off, nbytes)
        self.n += 1
        h = self.nc.alloc_sbuf_tensor_at("%s_%d" % (name, self.n), list(shape), dtype, offset=self.off)
        self.off += nbytes
        return Tl(h, name)

    def push(self):
        self.stack.append(self.off)

    def pop(self):
        self.S.barrier()
        self.off = self.stack.pop()

    def dram(self, name, shape, dtype, kind="Internal"):
        return Tl(self.nc.dram_tensor(name, list(shape), dtype, kind=kind).ap(), name)

    def psum(self, n=1):
        i = self.ps_rr
        if n > 1 and i % 2 == 1:
            i += 1
        if i + n > self.ps_limit:
            i = 0
        self.ps_rr = (i + n) % self.ps_limit
        return list(range(i, i + n))

    def psf(self, banks):
        if len(banks) == 1:
            return self.ps_h[:, banks[0], :]
        return self.ps_h[:, banks[0]:banks[0] + len(banks), :]

    def psb(self, bank):
        return self.ps_h[:, bank, :].bitcast(BF16)

    def pb(self, banks):
        return [self.ps_b[i] for i in banks]

    def mm(self, out, lhsT, rhs, start, stop, R, W, **kw):
        self.S.op("tensor", lambda e: e.matmul(out, lhsT=lhsT, rhs=rhs, start=start, stop=stop, **kw), _bufs(R), _bufs(W))

    def tr(self, out, in_, ident, R, W):
        self.S.op("tensor", lambda e: e.transpose(out=out, in_=in_, identity=ident), _bufs(R), _bufs(W))

    def act(self, out, in_, func, R, W, **kw):
        self.S.op("scalar", lambda e: e.activation(out=out, in_=in_, func=func, **kw), _bufs(R), _bufs(W))

    def ts(self, out, in0, s1, s2, op0, op1, R, W, eng="vector"):
        if op1 is None:
            self.S.op(eng, lambda e: e.tensor_scalar(out=out, in0=in0, scalar1=s1, scalar2=None, op0=op0), _bufs(R), _bufs(W))
        else:
            self.S.op(eng, lambda e: e.tensor_scalar(out=out, in0=in0, scalar1=s1, scalar2=s2, op0=op0, op1=op1), _bufs(R), _bufs(W))

    def tt(self, out, in0, in1, op, R, W, eng="vector"):
        self.S.op(eng, lambda e: e.tensor_tensor(out=out, in0=in0, in1=in1, op=op), _bufs(R), _bufs(W))

    def stt(self, out, in0, scalar, in1, op0, op1, R, W):
        self.S.op("vector", lambda e: e.scalar_tensor_tensor(out=out, in0=in0, scalar=scalar, in1=in1, op0=op0, op1=op1), _bufs(R), _bufs(W))

    def cp(self, out, in_, R, W, eng="vector"):
        if eng == "scalar":
            self.S.op(eng, lambda e: e.activation(out=out, in_=in_, func=AF.Copy), _bufs(R), _bufs(W))
        else:
            self.S.op(eng, lambda e: e.tensor_copy(out=out, in_=in_), _bufs(R), _bufs(W))

    def red(self, out, in_, R, W, op=ALU.add, axis=AX.X):
        self.S.op("vector", lambda e: e.tensor_reduce(out=out, in_=in_, axis=axis, op=op), _bufs(R), _bufs(W))

    def memset(self, out, val, W, eng="gpsimd"):
        self.S.op(eng, lambda e: e.memset(out, val), (), _bufs(W))

    def dma(self, out, in_, R, W, eng=None, **kw):
        if eng is None:
            eng = "sync"
        self.S.dma(eng, lambda e: e.dma_start(out=out, in_=in_, **kw), _bufs(R), _bufs(W))

    def dmac(self, out, in_, R, W, **kw):
        self.S.dma("gpsimd", lambda e: e.dma_start(out=out, in_=in_, **kw), _bufs(R), _bufs(W))


D = 1024
SSD_H, SSD_P, SSD_N, SSD_G = 16, 64, 64, 2
GDN_H, GDN_DK = 8, 128
IN_HYB = 6432
FOX_H, FOX_D = 16, 64
IN_FOX = 3088
DFF = 2816
MEM = 256
BIG = 30000.0


def make_consts(kb):
    c = {}
    nc = kb.nc
    S = kb.S
    ones_b = kb.sb([128, 128], BF16, "ones_b")
    ones_f = kb.sb([128, 128], F32, "ones_f")
    id_b = kb.sb([128, 128], BF16, "id_b")
    id_f = kb.sb([128, 128], F32, "id_f")
    tri_b = kb.sb([128, 128], BF16, "tri_b")
    tri_f = kb.sb([128, 128], F32, "tri_f")
    tri2_b = kb.sb([128, 128], BF16, "tri2_b")
    blk2_b = kb.sb([128, 128], BF16, "blk2_b")
    pos2 = kb.sb([128, 128], F32, "pos2")
    pos2i = kb.sb([128, 128], F32, "pos2i")
    neg2T = kb.sb([128, 128], F32, "neg2T")
    kb.memset(ones_b[:], 1.0, [ones_b])
    kb.memset(ones_f[:], 1.0, [ones_f])
    for t_, fill in ((id_b, 0.0), (id_f, 0.0)):
        kb.memset(t_[:], 1.0, [t_])
        S.op("gpsimd", lambda e, t_=t_: e.affine_select(out=t_[:], in_=t_[:], pattern=[[-1, 128]], compare_op=ALU.is_equal,
                                                         fill=0.0, base=0, channel_multiplier=1), [t_.b], [t_.b])
    for t_ in (tri_b, tri_f, tri2_b):
        kb.memset(t_[:], 1.0, [t_])
        S.op("gpsimd", lambda e, t_=t_: e.affine_select(out=t_[:], in_=t_[:], pattern=[[1, 128]], compare_op=ALU.is_ge,
                                                         fill=0.0, base=0, channel_multiplier=-1), [t_.b], [t_.b])
    kb.memset(tri2_b[0:64, 64:128], 0.0, [tri2_b])
    kb.memset(blk2_b[:], 0.0, [blk2_b])
    kb.memset(blk2_b[0:64, 0:64], 1.0, [blk2_b])
    kb.memset(blk2_b[64:128, 64:128], 1.0, [blk2_b])
    kb.memset(pos2[:], 0.0, [pos2])
    S.op("gpsimd", lambda e: e.affine_select(out=pos2[:], in_=pos2[:], pattern=[[-1, 128]], compare_op=ALU.is_gt,
                                             fill=BIG, base=0, channel_multiplier=1), [pos2.b], [pos2.b])
    kb.memset(pos2[64:128, 0:64], BIG, [pos2])
    kb.memset(neg2T[:], 0.0, [neg2T])
    S.op("gpsimd", lambda e: e.affine_select(out=neg2T[:], in_=neg2T[:], pattern=[[1, 128]], compare_op=ALU.is_ge,
                                             fill=-BIG, base=0, channel_multiplier=-1), [neg2T.b], [neg2T.b])
    kb.memset(neg2T[0:64, 64:128], -BIG, [neg2T])
    pos1 = kb.sb([128, 128], F32, "pos1")
    neg1T = kb.sb([128, 128], F32, "neg1T")
    kb.memset(pos1[:], 0.0, [pos1])
    S.op("gpsimd", lambda e: e.affine_select(out=pos1[:], in_=pos1[:], pattern=[[-1, 128]], compare_op=ALU.is_gt,
                                             fill=BIG, base=0, channel_multiplier=1), [pos1.b], [pos1.b])
    kb.memset(neg1T[:], 0.0, [neg1T])
    S.op("gpsimd", lambda e: e.affine_select(out=neg1T[:], in_=neg1T[:], pattern=[[1, 128]], compare_op=ALU.is_ge,
                                             fill=-BIG, base=0, channel_multiplier=-1), [neg1T.b], [neg1T.b])
    c.update(pos1=pos1, neg1T=neg1T)
    c.update(ones_b=ones_b, ones_f=ones_f, id_b=id_b, id_f=id_f, tri_b=tri_b, tri_f=tri_f, tri2_b=tri2_b,
             blk2_b=blk2_b, pos2=pos2, neg2T=neg2T)
    return c


def load_rowrep(kb, dram_ap_1d, n, name, rows=128):
    t = kb.sb([rows, n], F32, name)
    kb.dma(t[:], dram_ap_1d.partition_broadcast(rows), [], [t])
    return t


def load_w(kb, w2d, c0, c1, name, kchunks=None):
    K = w2d.shape[0]
    kc = K // 128
    t = kb.sb([128, kc, c1 - c0], BF16, name)
    src = w2d.rearrange("(c p) n -> p c n", p=128)
    for k in range(kc):
        kb.dmac(t[:, k, :], src[:, k, c0:c1], [], [t])
    return t


def rstd_from_ss(kb, ss_ap, out_ap, n, R, W, eps=EPS):
    kb.act(out_ap, ss_ap, AF.Ln, R, W, scale=1.0 / n, bias=eps)
    kb.act(out_ap, out_ap, AF.Exp, W, W, scale=-0.5)


def stage_norm_to_fm(kb, C, src_tm, tok0, ntok, gain_fm, A, a_tok0, tiles=128):
    kb.push()
    xt = [kb.sb([128, D], F32, "xt") for _ in range(2)]
    junk = kb.sb([128, D], BF16, "junk")
    hn = [kb.sb([128, D], BF16, "hn") for _ in range(2)]
    st = [kb.sb([128, 4], F32, "st") for _ in range(2)]
    hT = [kb.sb([128, 8, 128], BF16, "hT") for _ in range(2)]
    nt = (ntok + tiles - 1) // tiles
    for i in range(nt):
        L = min(tiles, ntok - i * tiles)
        x_, h_, s_, o_ = xt[i % 2], hn[i % 2], st[i % 2], hT[i % 2]
        r0 = tok0 + i * tiles
        kb.dma(x_[0:L, :], src_tm[r0:r0 + L, :], [], [x_])
        kb.act(junk[0:L, :], x_[0:L, :], AF.Square, [x_], [junk, s_], accum_out=s_[0:L, 0:1])
        rstd_from_ss(kb, s_[0:L, 0:1], s_[0:L, 1:2], D, [s_], [s_])
        kb.ts(h_[0:L, :], x_[0:L, :], s_[0:L, 1:2], None, ALU.mult, None, [x_, s_], [h_])
        bk = kb.psum(1)
        pv = kb.psb(bk[0])
        for c in range(8):
            kb.tr(pv[:, c * 128:c * 128 + L], h_[0:L, c * 128:(c + 1) * 128], C["id_b"][0:L, 0:L], [h_, C["id_b"]], kb.pb(bk))
        kb.tt(o_[:, :, 0:L], pv.rearrange("p (c t) -> p c t", c=8)[:, :, 0:L], gain_fm[:, :].unsqueeze(2).broadcast_to([128, 8, L]),
              ALU.mult, [kb.pb(bk), gain_fm], [o_])
        kb.dma(A[:, :, a_tok0 + i * tiles:a_tok0 + i * tiles + L].rearrange("c p t -> p c t"), o_[:, :, 0:L], [o_], [])
    kb.pop()


import os as _os
_STOP = int(_os.environ.get('SSD_STOP', '0'))
_GSTOP = int(_os.environ.get('GDN_STOP', '0'))
_GSUB = int(_os.environ.get('GDN_SUB', '0'))
_BSTOP = int(_os.environ.get('B_STOP', '0'))
_FA = int(_os.environ.get('FA_STOP', '0'))


def softplus_(kb, out_ap, in_ap, R, W):
    kb.act(out_ap, in_ap, AF.Exp, R, W)
    kb.act(out_ap, out_ap, AF.Ln, W, W, bias=1.0)


def ssd_weights(kb, P):
    w = P["w_in_hyb"]
    W = {}
    W["wz"] = load_w(kb, w, 0, 1024, "wz")
    W["wx"] = load_w(kb, w, 1024, 2304, "wx")
    W["wdt"] = load_w(kb, w, 2304, 2320, "wdt")
    cw = kb.sb([128, 10, 4], F32, "ssd_cw")
    for j in range(4):
        kb.dma(cw[:, :, j], P["ssd_conv_w"][j].rearrange("(c p) -> p c", p=128), [], [cw], allow_slow_non_contiguous=True)
    cb = kb.sb([128, 10], F32, "ssd_cb")
    kb.dma(cb[:], P["ssd_conv_b"].rearrange("(c p) -> p c", p=128), [], [cb], allow_slow_non_contiguous=True)
    W["cw"], W["cb"] = cw, cb
    W["dtb"] = load_rowrep(kb, P["ssd_dt_bias"], 16, "dtb")
    al = load_rowrep(kb, P["ssd_A_log"], 16, "alog")
    kb.act(al[:], al[:], AF.Exp, [al], [al])
    kb.ts(al[:], al[:], -1.0, None, ALU.mult, None, [al], [al])
    W["a"] = al
    W["dsk"] = load_rowrep(kb, P["ssd_D"], 16, "dsk")
    W["nw"] = load_rowrep(kb, P["ssd_norm"], 1024, "ssdnw")
    return W


def ssd_stream(kb, C, W, A, a_tok0, T, L, M, m_tok0, out_state, out_conv, init_state=None, init_conv=None):
    GT = min(256, T)
    ngroups = T // GT
    ntile = GT // L
    id_b, id_f, tri_b, ones_b = C["id_b"], C["id_f"], C["tri_b"], C["ones_b"]
    kb.push()
    HT = kb.sb([128, 16, 64], F32, "HT")
    HTb = kb.sb([128, 16, 64], BF16, "HTb")
    xp = kb.sb([128, 10, GT + 3], F32, "xp")
    ST = kb.sb([64, 16, 128], F32, "ST")
    if init_state is None:
        kb.memset(HT[:], 0.0, [HT])
        kb.memset(HTb[:], 0.0, [HTb])
        kb.memset(xp[:, :, 0:3], 0.0, [xp])
    else:
        kb.memset(ST[:], 0.0, [ST])
        kb.dma(ST[:, 0:8, 0:64], init_state[0:8].rearrange("h p n -> p h n"), [], [ST])
        kb.dma(ST[:, 8:16, 64:128], init_state[8:16].rearrange("h p n -> p h n"), [], [ST])
        for hh in range(2):
            bk = kb.psum(1)
            pv = kb.psf(bk).rearrange("p (h q) -> p h q", h=8)
            for h in range(8):
                kb.tr(pv[:, h, :], ST[:, hh * 8 + h, :], id_f[0:64, 0:64], [ST, id_f], kb.pb(bk))
            kb.cp(HT[:, hh * 8:hh * 8 + 8, :], pv, kb.pb(bk), [HT])
        kb.cp(HTb[:], HT[:], [HT], [HTb], eng="gpsimd")
        for r in range(3):
            kb.dma(xp[:, :, r], init_conv[r].rearrange("(c p) -> p c", p=128), [], [xp], allow_slow_non_contiguous=True)
    hT = [kb.sb([128, 8, GT], BF16, "hTg") for _ in range(2)]
    xc = [kb.sb([128, 10, GT], BF16, "xc") for _ in range(2)]
    acc = [kb.sb([128, GT], F32, "cacc") for _ in range(2)]
    mixT = [kb.sb([128, 8, GT], BF16, "mixT") for _ in range(2)]
    NB = 2
    sz = [kb.sb([128, 1024], BF16, "sz") for _ in range(NB)]
    sm = [kb.sb([128, 128], F32, "sm") for _ in range(NB)]
    smb = [kb.sb([128, 16], BF16, "smb") for _ in range(NB)]
    Xt = [kb.sb([128, 1024], BF16, "Xt") for _ in range(NB)]
    BCt = [kb.sb([128, 256], BF16, "BCt") for _ in range(NB)]
    TS = [kb.sb([128, 16, L], BF16, "TS") for _ in range(NB)]
    Lm = [kb.sb([128, 16, L], F32, "Lm") for _ in range(NB)]
    Lmb = [kb.sb([128, 16, L], BF16, "Lmb") for _ in range(NB)]
    Gm = [kb.sb([128, 2, L], BF16, "Gm") for _ in range(NB)]
    MT = [kb.sb([128, 16, L], BF16, "MT") for _ in range(NB)]
    Xdt = [kb.sb([128, 1024], BF16, "Xdt") for _ in range(NB)]
    Xw = [kb.sb([128, 1024], BF16, "Xw") for _ in range(NB)]
    y1 = [kb.sb([128, 1024], F32, "y1") for _ in range(NB)]
    y3 = [kb.sb([128, 1024], F32, "y3") for _ in range(NB)]
    mx = [kb.sb([128, 1024], BF16, "mx") for _ in range(NB)]
    junk = kb.sb([128, 512], BF16, "junk")
    Cblk = [kb.sb([128, 2, L], BF16, "Cblk") for _ in range(NB)]
    for q in range(NB):
        kb.memset(Cblk[q][:], 0.0, [Cblk[q]])
    tcount = 0
    for g in range(ngroups):
        t0 = g * GT
        h_, xc_, mT_ = hT[g % 2], xc[g % 2], mixT[g % 2]
        kb.dma(h_[:], A[:, :, a_tok0 + t0:a_tok0 + t0 + GT].rearrange("c p t -> p c t"), [], [h_])
        for cc in range(10):
            bk = kb.psum(1)
            pv = kb.psf(bk)[:, 0:GT]
            for k in range(8):
                kb.mm(pv, W["wx"][:, k, cc * 128:(cc + 1) * 128], h_[:, k, :], k == 0, k == 7, [W["wx"], h_], kb.pb(bk))
            kb.cp(xp[:, cc, 3:3 + GT], pv, kb.pb(bk), [xp], eng="scalar")
            a_ = acc[cc % 2]
            kb.ts(a_[:], xp[:, cc, 0:GT], W["cw"][:, cc, 0:1], W["cb"][:, cc:cc + 1], ALU.mult, ALU.add, [xp, W["cw"], W["cb"]], [a_])
            for j in range(1, 4):
                kb.stt(a_[:], xp[:, cc, j:j + GT], W["cw"][:, cc, j:j + 1], a_[:], ALU.mult, ALU.add, [xp, W["cw"], a_], [a_])
            kb.act(xc_[:, cc, :], a_[:], AF.Silu, [a_], [xc_])
        for i in range(ntile):
            q = tcount % NB
            tcount += 1
            cs = slice(i * L, (i + 1) * L)
            sz_, sm_, smb_, Xt_, BCt_, TS_, Lm_, Lmb_, Gm_, MT_, Xdt_, Xw_, y1_, y3_, mx_ = (
                sz[q], sm[q], smb[q], Xt[q], BCt[q], TS[q], Lm[q], Lmb[q], Gm[q], MT[q], Xdt[q], Xw[q], y1[q], y3[q], mx[q])
            bz = kb.psum(2)
            for nb in range(2):
                for k in range(8):
                    kb.mm(kb.psf([bz[nb]])[0:L, :], h_[:, k, cs], W["wz"][:, k, nb * 512:(nb + 1) * 512], k == 0, k == 7,
                          [h_, W["wz"]], kb.pb([bz[nb]]))
            kb.act(sz_[0:L, :], kb.psf(bz).rearrange("p a b -> p (a b)")[0:L, :], AF.Silu, kb.pb(bz), [sz_])
            if _STOP and _STOP <= 3:
                continue
            bd = kb.psum(1)
            pd = kb.psf(bd)
            for k in range(8):
                kb.mm(pd[0:L, 0:16], h_[:, k, cs], W["wdt"][:, k, :], k == 0, k == 7, [h_, W["wdt"]], kb.pb(bd))
            kb.tt(sm_[0:L, 0:16], pd[0:L, 0:16], W["dtb"][0:L, :], ALU.add, [kb.pb(bd), W["dtb"]], [sm_])
            softplus_(kb, sm_[0:L, 0:16], sm_[0:L, 0:16], [sm_], [sm_])
            kb.tt(sm_[0:L, 16:32], sm_[0:L, 0:16], W["a"][0:L, :], ALU.mult, [sm_, W["a"]], [sm_])
            kb.cp(smb_[0:L, :], sm_[0:L, 16:32], [sm_], [smb_])
            if _STOP and _STOP <= 4:
                continue
            bx = kb.psum(1)
            px = kb.psb(bx[0])
            for c in range(8):
                kb.tr(px[0:L, c * 128:(c + 1) * 128], xc_[:, c, cs], id_b[:, :], [xc_, id_b], kb.pb(bx))
            kb.cp(Xt_[0:L, :], px[0:L, :], kb.pb(bx), [Xt_], eng="scalar")
            bb = kb.psum(1)
            pbc = kb.psb(bb[0])
            for c in range(2):
                kb.tr(pbc[0:L, c * 128:(c + 1) * 128], xc_[:, 8 + c, cs], id_b[:, :], [xc_, id_b], kb.pb(bb))
            kb.cp(BCt_[0:L, :], pbc[0:L, 0:256], kb.pb(bb), [BCt_], eng="scalar")
            if _STOP and _STOP <= 5:
                continue
            bc_ = kb.psum(1)
            pc = kb.psf(bc_)
            kb.mm(pc[0:L, 0:16], tri_b[0:L, 0:L], smb_[0:L, :], True, True, [tri_b, smb_], kb.pb(bc_))
            kb.cp(sm_[0:L, 32:48], pc[0:L, 0:16], kb.pb(bc_), [sm_])
            kb.tt(TS_[0:L, :, :], tri_b[0:L, 0:L].unsqueeze(1).broadcast_to([L, 16, L]),
                  smb_[0:L, :].unsqueeze(2).broadcast_to([L, 16, L]), ALU.mult, [tri_b, smb_], [TS_])
            nbk = max(1, (16 * L) // 512)
            bcb = kb.psum(nbk)
            hpb = 16 // nbk
            for nb in range(nbk):
                kb.mm(kb.psf([bcb[nb]])[:, 0:hpb * L], ones_b[0:L, :], TS_[0:L, nb * hpb:(nb + 1) * hpb, :].rearrange("p h t -> p (h t)"),
                      True, True, [ones_b, TS_], kb.pb([bcb[nb]]))
            if nbk > 1:
                pcbf = kb.psf(bcb).rearrange("p a (h t) -> p (a h) t", t=L)
            else:
                pcbf = kb.psf(bcb)[:, 0:16 * L].rearrange("p (h t) -> p h t", t=L)
            pcb = pcbf[0:L]
            cumc_b = sm_[0:L, 32:48].unsqueeze(2).broadcast_to([L, 16, L])
            kb.tt(Lm_[0:L], pcb, cumc_b, ALU.subtract, [kb.pb(bcb), sm_], [Lm_])
            kb.ts(Lm_[0:L], Lm_[0:L], 0.0, None, ALU.min, None, [Lm_], [Lm_], eng="gpsimd")
            kb.act(Lmb_[0:L], Lm_[0:L], AF.Exp, [Lm_], [Lmb_])
            if _STOP and _STOP <= 6:
                continue
            kb.act(sm_[0:L, 48:64], sm_[0:L, 32:48], AF.Exp, [sm_], [sm_])
            kb.cp(sm_[0:L, 100:116], pcb[:, :, L - 1], kb.pb(bcb), [sm_])
            kb.act(sm_[:, 64:80], pcbf[:, :, L - 1], AF.Exp, kb.pb(bcb), [sm_])
            kb.tt(sm_[0:L, 80:96], sm_[0:L, 100:116], sm_[0:L, 32:48], ALU.subtract, [sm_], [sm_])
            kb.act(sm_[0:L, 80:96], sm_[0:L, 80:96], AF.Exp, [sm_], [sm_])
            kb.tt(sm_[0:L, 80:96], sm_[0:L, 80:96], sm_[0:L, 0:16], ALU.mult, [sm_], [sm_])
            if _STOP and _STOP <= 7:
                continue
            bg = kb.psum(1)
            pg = kb.psf(bg)[0:L, 0:2 * L].rearrange("p (g t) -> p g t", g=2)
            Cb_ = Cblk[q]
            for gi in range(2):
                kb.cp(Cb_[gi * 64:(gi + 1) * 64, gi, :], xc_[gi * 64:(gi + 1) * 64, 9, cs], [xc_], [Cb_], eng="gpsimd")
            kb.mm(kb.psf(bg)[0:L, 0:2 * L], xc_[:, 8, cs], Cb_[:, :, :].rearrange("p g t -> p (g t)"), True, True, [xc_, Cb_], kb.pb(bg))
            kb.tt(Gm_[0:L], pg, tri_b[0:L, 0:L].unsqueeze(1).broadcast_to([L, 2, L]), ALU.mult, [kb.pb(bg), tri_b], [Gm_])
            for gi in range(2):
                kb.tt(MT_[0:L, gi * 8:(gi + 1) * 8, :], Lmb_[0:L, gi * 8:(gi + 1) * 8, :],
                      Gm_[0:L, gi:gi + 1, :].broadcast_to([L, 8, L]), ALU.mult, [Lmb_, Gm_], [MT_])
            Xv = Xt_[0:L, :].rearrange("p (h q) -> p h q", h=16)
            kb.tt(Xdt_[0:L, :].rearrange("p (h q) -> p h q", h=16), Xv, sm_[0:L, 0:16].unsqueeze(2).broadcast_to([L, 16, 64]),
                  ALU.mult, [Xt_, sm_], [Xdt_])
            kb.tt(Xw_[0:L, :].rearrange("p (h q) -> p h q", h=16), Xv, sm_[0:L, 80:96].unsqueeze(2).broadcast_to([L, 16, 64]),
                  ALU.mult, [Xt_, sm_], [Xw_])
            if _STOP and _STOP <= 8:
                continue
            b1 = kb.psum(2)
            p1 = kb.psf(b1).rearrange("p a b -> p (a b)")
            for h in range(16):
                kb.mm(p1[0:L, h * 64:(h + 1) * 64], MT_[0:L, h, :], Xdt_[0:L, h * 64:(h + 1) * 64], True, True, [MT_, Xdt_], kb.pb([b1[h // 8]]))
            b2 = kb.psum(2)
            p2 = kb.psf(b2).rearrange("p a b -> p (a b)")
            for nb in range(2):
                kb.mm(kb.psf([b2[nb]])[0:L, :], xc_[:, 9, cs], HTb[:, nb * 8:(nb + 1) * 8, :].rearrange("p h q -> p (h q)"), True, True,
                      [xc_, HTb], kb.pb([b2[nb]]))
            kb.tt(y1_[0:L, :].rearrange("p (h q) -> p h q", h=16), p2[0:L, :].rearrange("p (h q) -> p h q", h=16),
                  sm_[0:L, 48:64].unsqueeze(2).broadcast_to([L, 16, 64]), ALU.mult, [kb.pb(b2), sm_], [y1_])
            kb.tt(y1_[0:L, :], y1_[0:L, :], p1[0:L, :], ALU.add, [y1_, kb.pb(b1)], [y1_])
            kb.tt(y3_[0:L, :].rearrange("p (h q) -> p h q", h=16), Xv, W["dsk"][0:L, :].unsqueeze(2).broadcast_to([L, 16, 64]),
                  ALU.mult, [Xt_, W["dsk"]], [y3_])
            kb.tt(y1_[0:L, :], y1_[0:L, :], y3_[0:L, :], ALU.add, [y1_, y3_], [y1_])
            kb.tt(y1_[0:L, :], y1_[0:L, :], sz_[0:L, :], ALU.mult, [y1_, sz_], [y1_])
            if _STOP and _STOP <= 9:
                continue
            bu = kb.psum(2)
            for nb in range(2):
                kb.mm(kb.psf([bu[nb]])[:, :], BCt_[0:L, 0:128], Xw_[0:L, nb * 512:(nb + 1) * 512], True, True, [BCt_, Xw_], kb.pb([bu[nb]]))
            for gi in range(2):
                ps_ = slice(gi * 64, (gi + 1) * 64)
                hv = HT[ps_, gi * 8:(gi + 1) * 8, :]
                kb.tt(hv, hv, sm_[ps_, 64 + gi * 8:64 + (gi + 1) * 8].unsqueeze(2).broadcast_to([64, 8, 64]), ALU.mult, [HT, sm_], [HT])
                kb.tt(hv, hv, kb.psf([bu[gi]])[ps_, :].rearrange("p (h q) -> p h q", h=8), ALU.add, [HT, kb.pb([bu[gi]])], [HT])
                kb.cp(HTb[ps_, gi * 8:(gi + 1) * 8, :], hv, [HT], [HTb], eng="scalar")
            if _STOP and _STOP <= 10:
                continue
            for gi in range(2):
                kb.act(junk[0:L, :], y1_[0:L, gi * 512:(gi + 1) * 512], AF.Square, [y1_], [junk, sm_], accum_out=sm_[0:L, 96 + gi:97 + gi])
            rstd_from_ss(kb, sm_[0:L, 96:98], sm_[0:L, 98:100], 512, [sm_], [sm_])
            for gi in range(2):
                kb.stt(mx_[0:L, gi * 512:(gi + 1) * 512], y1_[0:L, gi * 512:(gi + 1) * 512], sm_[0:L, 98 + gi:99 + gi],
                       W["nw"][0:L, gi * 512:(gi + 1) * 512], ALU.mult, ALU.mult, [y1_, sm_, W["nw"]], [mx_])
            bm = kb.psum(1)
            pm = kb.psb(bm[0])
            for c in range(8):
                kb.tr(pm[:, c * 128:c * 128 + L], mx_[0:L, c * 128:(c + 1) * 128], id_b[0:L, 0:L], [mx_, id_b], kb.pb(bm))
            kb.cp(mT_[:, :, cs], pm.rearrange("p (c t) -> p c t", c=8)[:, :, 0:L], kb.pb(bm), [mT_])
        kb.dma(M[0:8, :, m_tok0 + t0:m_tok0 + t0 + GT].rearrange("c p t -> p c t"), mT_[:], [mT_], [])
        if g < ngroups - 1:
            kb.cp(xp[:, :, 0:3], xp[:, :, GT:GT + 3], [xp], [xp], eng="gpsimd")
    for r in range(3):
        kb.dma(out_conv[r].rearrange("(c p) -> p c", p=128), xp[:, :, GT + r], [xp], [], allow_slow_non_contiguous=True)
    for q4 in range(4):
        bk = kb.psum(1)
        pv = kb.psf(bk)[0:64, :].rearrange("p (h q) -> p h q", h=4)
        for h in range(4):
            kb.tr(pv[:, h, :], HT[:, q4 * 4 + h, :], id_f[:, :], [HT, id_f], kb.pb(bk))
        gi = q4 // 2
        kb.cp(ST[:, q4 * 4:q4 * 4 + 4, 0:64], pv[:, :, gi * 64:(gi + 1) * 64], kb.pb(bk), [ST])
    kb.dma(out_state.rearrange("h p n -> p h n"), ST[:, :, 0:64], [ST], [])
    kb.pop()


def gdn_weights(kb, P):
    w = P["w_in_hyb"]
    W = {}
    W["wqkv"] = load_w(kb, w, 2320, 5392, "wqkv")
    W["wgate"] = load_w(kb, w, 5392, 6416, "wgate")
    W["wba"] = load_w(kb, w, 6416, 6432, "wba")
    cw = kb.sb([128, 24, 4], F32, "gdn_cw")
    for j in range(4):
        kb.dma(cw[:, :, j], P["gdn_conv_w"][j].rearrange("(c p) -> p c", p=128), [], [cw], allow_slow_non_contiguous=True)
    W["cw"] = cw
    W["dtb"] = load_rowrep(kb, P["gdn_dt_bias"], 8, "gdtb")
    al = load_rowrep(kb, P["gdn_A_log"], 8, "galog")
    kb.act(al[:], al[:], AF.Exp, [al], [al])
    kb.ts(al[:], al[:], -1.0, None, ALU.mult, None, [al], [al])
    W["nega"] = al
    W["gn"] = load_rowrep(kb, P["gdn_norm"], 128, "gnorm")
    return W


def gdn_stream(kb, C, W, A, a_tok0, T, L, M, m_tok0, out_state, out_conv, init_state=None, init_conv=None):
    GT = min(128, T)
    ngroups = T // GT
    ntile = GT // L
    CH = 64 if L == 128 else L
    nch = L // CH
    nsteps = {64: 5, 8: 2}[CH]
    id_b, id_f, ones_b = C["id_b"], C["id_f"], C["ones_b"]
    if L == 128:
        tri2, blk2, pos2, neg2T = C["tri2_b"], C["blk2_b"], C["pos2"], C["neg2T"]
    else:
        tri2, blk2, pos2, neg2T = C["tri_b"], C["ones_b"], C["pos1"], C["neg1T"]
    kb.push()
    Sf = kb.sb([128, 8, 128], F32, "Sf")
    Sb = kb.sb([128, 8, 128], BF16, "Sb")
    xp = kb.sb([128, 24, GT + 3], F32, "gxp")
    if init_state is None:
        kb.memset(Sf[:], 0.0, [Sf])
        kb.memset(Sb[:], 0.0, [Sb])
        kb.memset(xp[:, :, 0:3], 0.0, [xp])
    else:
        kb.dma(Sf[:], init_state.rearrange("h k v -> k h v"), [], [Sf])
        kb.cp(Sb[:], Sf[:], [Sf], [Sb])
        for r in range(3):
            kb.dma(xp[:, :, r], init_conv[r].rearrange("(c p) -> p c", p=128), [], [xp], allow_slow_non_contiguous=True)
    hT = [kb.sb([128, 8, GT], BF16, "ghT") for _ in range(2)]
    xc = [kb.sb([128, 24, GT], BF16, "gxc") for _ in range(2)]
    acc = [kb.sb([128, GT], F32, "gacc") for _ in range(2)]
    sqb = [kb.sb([128, GT], BF16, "gsq") for _ in range(2)]
    rin = [kb.sb([128, GT], F32, "grin") for _ in range(2)]
    mixT = [kb.sb([128, 8, GT], BF16, "gmixT") for _ in range(2)]
    sm = [kb.sb([128, 128], F32, "gsm") for _ in range(2)]
    gb = [kb.sb([128, 8], BF16, "ggb") for _ in range(2)]
    TS = [kb.sb([128, 8, L], BF16, "gTS") for _ in range(1)] * 2
    Egb = [kb.sb([128, 8, L], BF16, "gEgb") for _ in range(1)] * 2
    eL = [kb.sb([128, 2, 8], F32, "geL") for _ in range(2)]
    gbs = [kb.sb([128, 8, L], F32, "ggbs") for _ in range(1)] * 2
    Qg = [kb.sb([128, 8, L], BF16, "gQg") for _ in range(1)] * 2
    Kbg = [kb.sb([128, 8, 128], BF16, "gKbg") for _ in range(1)] * 2
    K2 = [kb.sb([128, 8, 128], BF16, "gK2") for _ in range(1)] * 2
    Vb = [kb.sb([128, 8, 128], BF16, "gVb") for _ in range(1)] * 2
    tmpf = [kb.sb([128, 4, L], F32, "gtmp") for _ in range(2)]
    dec = [kb.sb([128, 4, L], F32, "gdec") for _ in range(2)]
    Nf = [kb.sb([128, 4, L], F32, "gN") for _ in range(2)]
    Rf = [kb.sb([128, 4, L], F32, "gR") for _ in range(2)]
    Pf = [kb.sb([128, 4, L], F32, "gP") for _ in range(2)]
    Qf = [kb.sb([128, 4, L], F32, "gQ") for _ in range(2)]
    Xf = [kb.sb([128, 4, L], F32, "gX") for _ in range(2)]
    PmT = [kb.sb([128, 8, L], BF16, "gPmT") for _ in range(1)] * 2
    TTp = [[kb.sb([128, 8, L], BF16, "gTTp") for _ in range(nch)] for _ in range(2)]
    nWp = [[kb.sb([128, 8, L], BF16, "gnWp") for _ in range(nch)] for _ in range(2)]
    for q in range(2):
        for c in range(nch):
            kb.memset(TTp[q][c][:], 0.0, [TTp[q][c]])
            kb.memset(nWp[q][c][:], 0.0, [nWp[q][c]])
    Vnb = [kb.sb([128, 4, 128], BF16, "gVnb") for _ in range(2)]
    Of = [kb.sb([128, 8, 128], F32, "gOf") for _ in range(1)] * 2
    Osq = kb.sb([128, 8, 128], BF16, "gOsq")
    sg = [kb.sb([128, 1024], BF16, "gsg") for _ in range(1)] * 2
    mxb = [kb.sb([128, 1024], BF16, "gmx") for _ in range(1)] * 2
    osm = [kb.sb([128, 16], F32, "gosm") for _ in range(2)]
    tcount = 0
    ccount = 0
    for g in range(ngroups):
        t0 = g * GT
        h_, xc_, mT_ = hT[g % 2], xc[g % 2], mixT[g % 2]
        kb.dma(h_[:], A[:, :, a_tok0 + t0:a_tok0 + t0 + GT].rearrange("c p t -> p c t"), [], [h_])
        for cc in range(24):
            bk = kb.psum(1)
            pv = kb.psf(bk)[:, 0:GT]
            for k in range(8):
                kb.mm(pv, W["wqkv"][:, k, cc * 128:(cc + 1) * 128], h_[:, k, :], k == 0, k == 7, [W["wqkv"], h_], kb.pb(bk))
            kb.cp(xp[:, cc, 3:3 + GT], pv, kb.pb(bk), [xp], eng="scalar")
            a_ = acc[cc % 2]
            kb.ts(a_[:], xp[:, cc, 0:GT], W["cw"][:, cc, 0:1], None, ALU.mult, None, [xp, W["cw"]], [a_])
            for j in range(1, 4):
                kb.stt(a_[:], xp[:, cc, j:j + GT], W["cw"][:, cc, j:j + 1], a_[:], ALU.mult, ALU.add, [xp, W["cw"], a_], [a_])
            kb.act(xc_[:, cc, :], a_[:], AF.Silu, [a_], [xc_])
            if cc < 16:
                s_, r_ = sqb[cc % 2], rin[cc % 2]
                kb.act(s_[:], xc_[:, cc, :], AF.Square, [xc_], [s_])
                b2_ = kb.psum(1)
                p2_ = kb.psf(b2_)[:, 0:GT]
                kb.mm(p2_, ones_b[:, :], s_[:], True, True, [ones_b, s_], kb.pb(b2_))
                kb.act(r_[:], p2_, AF.Ln, kb.pb(b2_), [r_], bias=1e-6)
                kb.act(r_[:], r_[:], AF.Exp, [r_], [r_], scale=-0.5)
                kb.stt(xc_[:, cc, :], xc_[:, cc, :], (GDN_DK ** -0.5) if cc < 8 else 1.0, r_[:], ALU.mult, ALU.mult, [xc_, r_], [xc_])
        for i in range(ntile):
            if _GSTOP and _GSTOP <= 1:
                continue
            q = tcount % 2
            tcount += 1
            cs = slice(i * L, (i + 1) * L)
            sm_, gb_, TS_, Egb_, eL_, Qg_, Kbg_, K2_, Vb_, PmT_ = sm[q], gb[q], TS[q], Egb[q], eL[q], Qg[q], Kbg[q], K2[q], Vb[q], PmT[q]
            bd = kb.psum(1)
            pd = kb.psf(bd)
            for k in range(8):
                kb.mm(pd[0:L, 0:16], h_[:, k, cs], W["wba"][:, k, :], k == 0, k == 7, [h_, W["wba"]], kb.pb(bd))
            kb.act(sm_[0:L, 0:8], pd[0:L, 0:8], AF.Exp, kb.pb(bd), [sm_], scale=-1.0)
            kb.ts(sm_[0:L, 0:8], sm_[0:L, 0:8], 1.0, None, ALU.add, None, [sm_], [sm_])
            kb.S.op("vector", lambda e, o_=sm_[0:L, 0:8]: e.reciprocal(out=o_, in_=o_), [sm_.b], [sm_.b])
            kb.tt(sm_[0:L, 8:16], pd[0:L, 8:16], W["dtb"][0:L, :], ALU.add, [kb.pb(bd), W["dtb"]], [sm_])
            softplus_(kb, sm_[0:L, 8:16], sm_[0:L, 8:16], [sm_], [sm_])
            kb.tt(sm_[0:L, 8:16], sm_[0:L, 8:16], W["nega"][0:L, :], ALU.mult, [sm_, W["nega"]], [sm_])
            kb.cp(gb_[0:L, :], sm_[0:L, 8:16], [sm_], [gb_])
            if _GSUB and _GSUB <= 1:
                continue
            bc_ = kb.psum(1)
            pc = kb.psf(bc_)
            kb.mm(pc[0:L, 0:8], tri2[0:L, 0:L], gb_[0:L, :], True, True, [tri2, gb_], kb.pb(bc_))
            kb.mm(pc[0:L, 8:16], blk2[0:L, 0:L], gb_[0:L, :], True, True, [blk2, gb_], kb.pb(bc_))
            kb.cp(sm_[0:L, 16:32], pc[0:L, 0:16], kb.pb(bc_), [sm_])
            if _GSUB and _GSUB <= 2:
                continue
            kb.tt(TS_[0:L, :, :], tri2[0:L, 0:L].unsqueeze(1).broadcast_to([L, 8, L]),
                  gb_[0:L, :].unsqueeze(2).broadcast_to([L, 8, L]), ALU.mult, [tri2, gb_], [TS_])
            nbk = max(1, (8 * L) // 512)
            hpb = 8 // nbk
            bgm = kb.psum(nbk)
            for nb in range(nbk):
                kb.mm(kb.psf([bgm[nb]])[:, 0:hpb * L], ones_b[0:L, :], TS_[0:L, nb * hpb:(nb + 1) * hpb, :].rearrange("p h t -> p (h t)"),
                      True, True, [ones_b, TS_], kb.pb([bgm[nb]]))
            if nbk > 1:
                gbc = kb.psf(bgm).rearrange("p a (h t) -> p (a h) t", t=L)
            else:
                gbc = kb.psf(bgm)[:, 0:8 * L].rearrange("p (h t) -> p h t", t=L)
            if _GSUB and _GSUB <= 3:
                continue
            kb.act(sm_[0:L, 32:40], sm_[0:L, 16:24], AF.Exp, [sm_], [sm_])
            kb.tt(sm_[0:L, 32:40], sm_[0:L, 32:40], sm_[0:L, 0:8], ALU.mult, [sm_], [sm_])
            kb.tt(sm_[0:L, 40:48], sm_[0:L, 24:32], sm_[0:L, 16:24], ALU.subtract, [sm_], [sm_])
            kb.act(sm_[0:L, 40:48], sm_[0:L, 40:48], AF.Exp, [sm_], [sm_])
            kb.ts(sm_[0:L, 48:56], sm_[0:L, 0:8], -1.0, None, ALU.mult, None, [sm_], [sm_])
            if _GSUB and _GSUB <= 4:
                continue
            kb.act(Egb_[:], gbc, AF.Exp, kb.pb(bgm), [Egb_])
            for c in range(nch):
                kb.act(eL_[:, c, :], gbc[:, :, (c + 1) * CH - 1], AF.Exp, kb.pb(bgm), [eL_])
            if _GSUB and _GSUB <= 5:
                continue
            gbs_ = gbs[q]
            kb.cp(gbs_[:], gbc, kb.pb(bgm), [gbs_], eng="scalar")
            if _GSUB and _GSUB <= 6:
                continue
            kb.tt(Qg_[:], xc_[:, 0:8, cs], Egb_[:], ALU.mult, [xc_, Egb_], [Qg_])
            if _GSTOP and _GSTOP <= 2:
                continue
            bk_ = kb.psum(1)
            pk = kb.psb(bk_[0])
            for h in range(8):
                kb.tr(pk[0:L, h * 128:(h + 1) * 128], xc_[:, 8 + h, cs], id_b[:, :], [xc_, id_b], kb.pb(bk_))
            pk3 = pk[0:L, :].rearrange("p (h d) -> p h d", h=8)
            kb.tt(Kbg_[0:L], pk3, sm_[0:L, 32:40].unsqueeze(2).broadcast_to([L, 8, 128]), ALU.mult, [kb.pb(bk_), sm_], [Kbg_])
            kb.tt(K2_[0:L], pk3, sm_[0:L, 40:48].unsqueeze(2).broadcast_to([L, 8, 128]), ALU.mult, [kb.pb(bk_), sm_], [K2_])
            bv_ = kb.psum(1)
            pvv = kb.psb(bv_[0])
            for h in range(8):
                kb.tr(pvv[0:L, h * 128:(h + 1) * 128], xc_[:, 16 + h, cs], id_b[:, :], [xc_, id_b], kb.pb(bv_))
            kb.tt(Vb_[0:L], pvv[0:L, :].rearrange("p (h d) -> p h d", h=8), sm_[0:L, 0:8].unsqueeze(2).broadcast_to([L, 8, 128]),
                  ALU.mult, [kb.pb(bv_), sm_], [Vb_])
            if _GSTOP and _GSTOP <= 3:
                continue
            nbq = 1
            QS = []
            for hq in range(2):
                hs = slice(hq * 4, hq * 4 + 4)
                d = dict(hq=hq, hs=hs, tmp_=tmpf[hq], dec_=dec[hq], N_=Nf[hq], R_=Rf[hq], P_=Pf[hq], Q_=Qf[hq], X_=Xf[hq])
                bG = kb.psum(nbq)
                d["bG"] = bG
                d["pG"] = kb.psf(bG)[0:L, 0:4 * L].rearrange("p (h t) -> p h t", h=4)
                bKQ = kb.psum(nbq)
                d["bKQ"] = bKQ
                d["pKQ"] = kb.psf(bKQ)[0:L, 0:4 * L].rearrange("p (h t) -> p h t", h=4)
                for h in range(4):
                    hh = hq * 4 + h
                    kb.mm(d["pG"][:, h, :], xc_[:, 8 + hh, cs], xc_[:, 8 + hh, cs], True, True, [xc_], kb.pb(bG))
                    kb.mm(d["pKQ"][:, h, :], xc_[:, 8 + hh, cs], xc_[:, hh, cs], True, True, [xc_], kb.pb(bKQ))
                QS.append(d)
            for d in QS:
                hq, hs, tmp_, dec_, N_ = d["hq"], d["hs"], d["tmp_"], d["dec_"], d["N_"]
                gam_b = sm_[0:L, 16 + hq * 4:16 + hq * 4 + 4].unsqueeze(2).broadcast_to([L, 4, L])
                kb.tt(tmp_[0:L], gbs_[0:L, hs, :], gam_b, ALU.subtract, [gbs_, sm_], [tmp_])
                kb.tt(tmp_[0:L], tmp_[0:L], pos2[0:L, 0:L].unsqueeze(1).broadcast_to([L, 4, L]), ALU.max, [tmp_, pos2], [tmp_])
                kb.act(dec_[0:L], tmp_[0:L], AF.Exp, [tmp_], [dec_], scale=-1.0)
                kb.tt(N_[0:L], d["pG"], dec_[0:L], ALU.mult, [kb.pb(d["bG"]), dec_], [N_])
                kb.tt(N_[0:L], N_[0:L], sm_[0:L, 48 + hq * 4:48 + hq * 4 + 4].unsqueeze(2).broadcast_to([L, 4, L]), ALU.mult, [N_, sm_], [N_])
                kb.tt(tmp_[0:L], gbs_[0:L, hs, :], gam_b, ALU.subtract, [gbs_, sm_], [tmp_])
                kb.tt(tmp_[0:L], tmp_[0:L], neg2T[0:L, 0:L].unsqueeze(1).broadcast_to([L, 4, L]), ALU.min, [tmp_, neg2T], [tmp_])
                kb.act(dec_[0:L], tmp_[0:L], AF.Exp, [tmp_], [dec_])
                kb.tt(PmT_[0:L, hs, :], d["pKQ"], dec_[0:L], ALU.mult, [kb.pb(d["bKQ"]), dec_], [PmT_])
            for d in QS:
                N_, R_, X_ = d["N_"], d["R_"], d["X_"]
                bR = kb.psum(nbq)
                pR = kb.psf(bR)[0:L, 0:4 * L].rearrange("p (h t) -> p h t", h=4)
                for h in range(4):
                    kb.tr(pR[:, h, :], N_[0:L, h, :], id_f[0:L, 0:L], [N_, id_f], kb.pb(bR))
                kb.cp(R_[0:L], pR, kb.pb(bR), [R_], eng="scalar")
                kb.tt(X_[0:L], pR, id_f[0:L, 0:L].unsqueeze(1).broadcast_to([L, 4, L]), ALU.add, [kb.pb(bR), id_f], [X_])
                d["Pc"], d["Qc"], d["Pn"], d["Qn"] = N_, R_, d["P_"], d["Q_"]
            for m in range(1, nsteps + 1):
                for d in QS:
                    Pc, Qc, Pn, Qn = d["Pc"], d["Qc"], d["Pn"], d["Qn"]
                    bP = kb.psum(nbq)
                    pP = kb.psf(bP)[0:L, 0:4 * L].rearrange("p (h t) -> p h t", h=4)
                    for h in range(4):
                        kb.mm(pP[:, h, :], Qc[0:L, h, :], Pc[0:L, h, :], True, True, [Qc, Pc], kb.pb(bP))
                    if m < nsteps:
                        bQ = kb.psum(nbq)
                        pQ = kb.psf(bQ)[0:L, 0:4 * L].rearrange("p (h t) -> p h t", h=4)
                        for h in range(4):
                            kb.mm(pQ[:, h, :], Pc[0:L, h, :], Qc[0:L, h, :], True, True, [Qc, Pc], kb.pb(bQ))
                    kb.cp(Pn[0:L], pP, kb.pb(bP), [Pn], eng="scalar")
                    if m < nsteps:
                        kb.cp(Qn[0:L], pQ, kb.pb(bQ), [Qn])
                for d in QS:
                    Pn, X_ = d["Pn"], d["X_"]
                    bX = kb.psum(nbq)
                    pX = kb.psf(bX)[0:L, 0:4 * L].rearrange("p (h t) -> p h t", h=4)
                    for h in range(4):
                        kb.mm(pX[:, h, :], Pn[0:L, h, :], X_[0:L, h, :], True, True, [Pn, X_], kb.pb(bX))
                    kb.tt(X_[0:L], X_[0:L], pX, ALU.add, [X_, kb.pb(bX)], [X_])
                    d["Pc"], d["Pn"] = d["Pn"], d["Pc"]
                    d["Qc"], d["Qn"] = d["Qn"], d["Qc"]
            for d in QS:
                hq, hs, X_ = d["hq"], d["hs"], d["X_"]
                for c in range(nch):
                    ccs = slice(c * CH, (c + 1) * CH)
                    kb.cp(TTp[q][c][0:L, hs, ccs], X_[0:L, :, ccs], [X_], [TTp[q][c]], eng="gpsimd")
            for d in QS:
                hq, hs = d["hq"], d["hs"]
                bW = kb.psum(1)
                pW = kb.psf(bW)[:, 0:4 * L].rearrange("p (h t) -> p h t", h=4)
                for h in range(4):
                    hh = hq * 4 + h
                    for c in range(nch):
                        ccs = slice(c * CH, (c + 1) * CH)
                        kb.mm(pW[:, h, ccs], Kbg_[0:L, hh, :], TTp[q][c][0:L, hh, ccs], True, True, [Kbg_, TTp[q][c]], kb.pb(bW))
                for c in range(nch):
                    ccs = slice(c * CH, (c + 1) * CH)
                    kb.ts(nWp[q][c][:, hs, ccs], pW[:, :, ccs], -1.0, None, ALU.mult, None, kb.pb(bW), [nWp[q][c]])
            for c in range(nch):
                ccs = slice(c * CH, (c + 1) * CH)
                tcs = slice(i * L + c * CH, i * L + (c + 1) * CH)
                o = ccount % 2
                ccount += 1
                Of_, sg_, mx_, osm_ = Of[o], sg[o], mxb[o], osm[o]
                bz = kb.psum(2)
                for nb in range(2):
                    for k in range(8):
                        kb.mm(kb.psf([bz[nb]])[0:CH, :], h_[:, k, tcs], W["wgate"][:, k, nb * 512:(nb + 1) * 512], k == 0, k == 7,
                              [h_, W["wgate"]], kb.pb([bz[nb]]))
                kb.act(sg_[0:CH, :], kb.psf(bz).rearrange("p a b -> p (a b)")[0:CH, :], AF.Silu, kb.pb(bz), [sg_])
                SQ = []
                for hq in range(2):
                    bV = kb.psum(1)
                    pV = kb.psf(bV).rearrange("p (h v) -> p h v", h=4)
                    for h in range(4):
                        hh = hq * 4 + h
                        kb.mm(pV[0:L, h, :], TTp[q][c][0:L, hh, :], Vb_[0:L, hh, :], True, False, [TTp[q][c], Vb_], kb.pb(bV))
                        kb.mm(pV[0:L, h, :], nWp[q][c][:, hh, :], Sb[:, hh, :], False, True, [nWp[q][c], Sb], kb.pb(bV))
                    SQ.append(dict(hq=hq, hs=slice(hq * 4, hq * 4 + 4), v_=Vnb[hq], bV=bV, pV=pV))
                for d in SQ:
                    kb.cp(d["v_"][0:L], d["pV"][0:L], kb.pb(d["bV"]), [d["v_"]], eng="scalar" if d["hq"] == 0 else "vector")
                for d in SQ:
                    hq, v_ = d["hq"], d["v_"]
                    bO = kb.psum(1)
                    pO = kb.psf(bO).rearrange("p (h v) -> p h v", h=4)
                    bS = kb.psum(1)
                    pS = kb.psf(bS).rearrange("p (h v) -> p h v", h=4)
                    for h in range(4):
                        hh = hq * 4 + h
                        kb.mm(pO[0:CH, h, :], Qg_[:, hh, ccs], Sb[:, hh, :], True, False, [Qg_, Sb], kb.pb(bO))
                        kb.mm(pO[0:CH, h, :], PmT_[0:L, hh, ccs], v_[0:L, h, :], False, True, [PmT_, v_], kb.pb(bO))
                    for h in range(4):
                        hh = hq * 4 + h
                        kb.mm(pS[:, h, :], K2_[0:L, hh, :], v_[0:L, h, :], True, True, [K2_, v_], kb.pb(bS))
                    d.update(bO=bO, pO=pO, bS=bS, pS=pS)
                for d in SQ:
                    hs = d["hs"]
                    kb.cp(Of_[0:CH, hs, :], d["pO"][0:CH], kb.pb(d["bO"]), [Of_], eng="scalar")
                    kb.tt(Sf[:, hs, :], Sf[:, hs, :], eL_[:, c, hs].unsqueeze(2).broadcast_to([128, 4, 128]), ALU.mult, [Sf, eL_], [Sf])
                    kb.tt(Sf[:, hs, :], Sf[:, hs, :], d["pS"], ALU.add, [Sf, kb.pb(d["bS"])], [Sf])
                    kb.cp(Sb[:, hs, :], Sf[:, hs, :], [Sf], [Sb], eng="gpsimd")
                kb.tt(Osq[0:CH], Of_[0:CH], Of_[0:CH], ALU.mult, [Of_], [Osq], eng="gpsimd")
                kb.red(osm_[0:CH, 0:8], Osq[0:CH], [Osq], [osm_])
                rstd_from_ss(kb, osm_[0:CH, 0:8], osm_[0:CH, 8:16], 128, [osm_], [osm_])
                kb.tt(Of_[0:CH], Of_[0:CH], osm_[0:CH, 8:16].unsqueeze(2).broadcast_to([CH, 8, 128]), ALU.mult, [Of_, osm_], [Of_])
                kb.tt(Of_[0:CH], Of_[0:CH], W["gn"][0:CH, :].unsqueeze(1).broadcast_to([CH, 8, 128]), ALU.mult, [Of_, W["gn"]], [Of_], eng="gpsimd")
                kb.tt(mx_[0:CH, :], Of_[0:CH].rearrange("p h v -> p (h v)"), sg_[0:CH, :], ALU.mult, [Of_, sg_], [mx_])
                bm = kb.psum(1)
                pm = kb.psb(bm[0])
                for cc in range(8):
                    kb.tr(pm[:, cc * 128:cc * 128 + CH], mx_[0:CH, cc * 128:(cc + 1) * 128], id_b[0:CH, 0:CH], [mx_, id_b], kb.pb(bm))
                kb.cp(mT_[:, :, tcs], pm.rearrange("p (c t) -> p c t", c=8)[:, :, 0:CH], kb.pb(bm), [mT_])
        kb.dma(M[8:16, :, m_tok0 + t0:m_tok0 + t0 + GT].rearrange("c p t -> p c t"), mT_[:], [mT_], [])
        if g < ngroups - 1:
            kb.cp(xp[:, :, 0:3], xp[:, :, GT:GT + 3], [xp], [xp], eng="gpsimd")
    for r in range(3):
        kb.dma(out_conv[r].rearrange("(c p) -> p c", p=128), xp[:, :, GT + r], [xp], [], allow_slow_non_contiguous=True)
    kb.dma(out_state.rearrange("h k v -> k h v"), Sf[:], [Sf], [])
    kb.pop()


def load_gain_fm(kb, g1d, name):
    t = kb.sb([128, 8], F32, name)
    kb.dma(t[:], g1d.rearrange("(c p) -> p c", p=128), [], [t], allow_slow_non_contiguous=True)
    return t


def norm_tile_to_fm(kb, C, h_, L, gain_fm, st_, hn_, oT_, junk):
    kb.act(junk[0:L, :], h_[0:L, :], AF.Square, [h_], [junk, st_], accum_out=st_[0:L, 0:1])
    rstd_from_ss(kb, st_[0:L, 0:1], st_[0:L, 1:2], D, [st_], [st_])
    kb.ts(hn_[0:L, :], h_[0:L, :], st_[0:L, 1:2], None, ALU.mult, None, [h_, st_], [hn_])
    bk = kb.psum(1)
    pv = kb.psb(bk[0])
    for c in range(8):
        kb.tr(pv[:, c * 128:c * 128 + L], hn_[0:L, c * 128:(c + 1) * 128], C["id_b"][0:L, 0:L], [hn_, C["id_b"]], kb.pb(bk))
    kb.tt(oT_[:, :, 0:L], pv.rearrange("p (c t) -> p c t", c=8)[:, :, 0:L], gain_fm[:, :].unsqueeze(2).broadcast_to([128, 8, L]),
          ALU.mult, [kb.pb(bk), gain_fm], [oT_])


def xattn_kv_from_mem(kb, C, mem_tm, g_mem, wk2d, wv2d, out_k, out_v):
    gfm = load_gain_fm(kb, g_mem, "gmem")
    KmT = kb.sb([128, 4, 256], BF16, "KmT")
    Vaug = kb.sb([128, 2, 4, 130], BF16, "Vaug")
    kb.memset(Vaug[:, :, :, 128:129], 1.0, [Vaug])
    kb.push()
    wk = load_w(kb, wk2d, 0, 512, "wk")
    wv = load_w(kb, wv2d, 0, 512, "wv")
    mT = kb.sb([128, 8, 256], BF16, "memT")
    xt = kb.sb([128, D], F32, "mxt")
    junk = kb.sb([128, D], BF16, "mjunk")
    hn = kb.sb([128, D], BF16, "mhn")
    st = kb.sb([128, 4], F32, "mst")
    oT = kb.sb([128, 8, 128], BF16, "moT")
    ko = kb.sb([128, 512], F32, "mko")
    for i in range(2):
        kb.dma(xt[:], mem_tm[i * 128:(i + 1) * 128, :], [], [xt])
        norm_tile_to_fm(kb, C, xt, 128, gfm, st, hn, oT, junk)
        kb.cp(mT[:, :, i * 128:(i + 1) * 128], oT[:], [oT], [mT], eng="gpsimd")
    for i in range(2):
        for (w_, out_, isv) in ((wk, out_k, False), (wv, out_v, True)):
            bk = kb.psum(1)
            pv = kb.psf(bk)
            for k in range(8):
                kb.mm(pv, mT[:, k, i * 128:(i + 1) * 128], w_[:, k, :], k == 0, k == 7, [mT, w_], kb.pb(bk))
            kb.cp(ko[:], pv, kb.pb(bk), [ko], eng="scalar")
            kb.dma(out_[i * 128:(i + 1) * 128, :], ko[:], [ko], [])
            if isv:
                kb.cp(Vaug[:, i, :, 0:128], ko[:, :].rearrange("p (h d) -> p h d", h=4), [ko], [Vaug])
    for h in range(4):
        bk = kb.psum(1)
        pv = kb.psf(bk)[:, 0:256]
        for k in range(8):
            kb.mm(pv, wk[:, k, h * 128:(h + 1) * 128], mT[:, k, :], k == 0, k == 7, [wk, mT], kb.pb(bk))
        kb.cp(KmT[:, h, :], pv, kb.pb(bk), [KmT])
    kb.pop()
    return KmT, Vaug


def xattn_kv_from_cache(kb, C, ck, cv, KmT, Vaug):
    kb.push()
    kt = kb.sb([128, 2, 512], BF16, "ckt")
    kb.dmac(kt[:], ck.rearrange("(c p) h d -> p c (h d)", p=128), [], [kt])
    for i in range(2):
        kb.dmac(Vaug[:, i, :, 0:128], cv[i * 128:(i + 1) * 128], [], [Vaug])
    bk = kb.psum(1)
    pv = kb.psb(bk[0])
    for i in range(2):
        for h in range(4):
            kb.tr(pv[:, (i * 4 + h) * 128:(i * 4 + h + 1) * 128], kt[:, i, h * 128:(h + 1) * 128], C["id_b"][:, :], [kt, C["id_b"]], kb.pb(bk))
    for i in range(2):
        kb.cp(KmT[:, :, i * 128:(i + 1) * 128], pv[:, i * 512:(i + 1) * 512].rearrange("p (h m) -> p h m", h=4), kb.pb(bk), [KmT])
    kb.pop()


def stage_mix_xattn(kb, C, tiles, M, KC, wout2d, resid_src, H, A, P, layer, kv_for_tile):
    kb.push()
    wout = load_w(kb, wout2d, 0, 1024, "wout")
    wq = load_w(kb, P["wq_x"][layer], 0, 512, "wq")
    wo = load_w(kb, P["wo_x"][layer], 0, 1024, "wo")
    gx = load_gain_fm(kb, P["norm_x"][layer], "gx")
    gf = load_gain_fm(kb, P["norm_ffn"][layer], "gf")
    mT = [kb.sb([128, KC, 128], BF16, "xmT") for _ in range(2)]
    rs = [kb.sb([128, D], F32, "xrs") for _ in range(2)]
    hh = [kb.sb([128, D], F32, "xh") for _ in range(2)]
    junk = kb.sb([128, D], BF16, "xjunk")
    hn = [kb.sb([128, D], BF16, "xhn") for _ in range(2)]
    st = [kb.sb([128, 8], F32, "xst") for _ in range(2)]
    oT = [kb.sb([128, 8, 128], BF16, "xoT") for _ in range(2)]
    qT = [kb.sb([128, 4, 128], BF16, "xqT") for _ in range(2)]
    pT = [kb.sb([128, 8, 128], BF16, "xpT") for _ in range(2)]
    on = [kb.sb([128, 4, 129], F32, "xon") for _ in range(2)]
    ob = [kb.sb([128, 512], BF16, "xob") for _ in range(2)]
    obT = [kb.sb([128, 4, 128], BF16, "xobT") for _ in range(2)]
    sc = X_HEAD_SCALE
    for ti, (tok0, L, rsrc) in enumerate(tiles):
        q = ti % 2
        KmT, Vaug = kv_for_tile(ti)
        m_, r_, h_, hn_, st_, oT_, qT_, pT_, on_, ob_, obT_ = mT[q], rs[q], hh[q], hn[q], st[q], oT[q], qT[q], pT[q], on[q], ob[q], obT[q]
        kb.dma(m_[:, :, 0:L], M[0:KC, :, tok0:tok0 + L].rearrange("c p t -> p c t"), [], [m_])
        kb.dma(r_[0:L, :], rsrc, [], [r_])
        bo = kb.psum(2)
        for nb in range(2):
            for k in range(KC):
                kb.mm(kb.psf([bo[nb]])[0:L, :], m_[:, k, 0:L], wout[:, k, nb * 512:(nb + 1) * 512], k == 0, k == KC - 1, [m_, wout], kb.pb([bo[nb]]))
        kb.tt(h_[0:L, :], r_[0:L, :], kb.psf(bo).rearrange("p a b -> p (a b)")[0:L, :], ALU.add, [r_, kb.pb(bo)], [h_])
        norm_tile_to_fm(kb, C, h_, L, gx, st_, hn_, oT_, junk)
        bq = kb.psum(1)
        pq = kb.psf(bq).rearrange("p (h t) -> p h t", h=4)
        for h in range(4):
            for k in range(8):
                kb.mm(pq[:, h, 0:L], wq[:, k, h * 128:(h + 1) * 128], oT_[:, k, 0:L], k == 0, k == 7, [wq, oT_], kb.pb(bq))
        kb.cp(qT_[:, :, 0:L], pq[:, :, 0:L], kb.pb(bq), [qT_], eng="scalar")
        bs = kb.psum(2)
        psc = kb.psf(bs).rearrange("p a (x t) -> p (a x) t", t=128)
        for h in range(4):
            for mc in range(2):
                kb.mm(psc[:, h * 2 + mc, 0:L], KmT[:, h, mc * 128:(mc + 1) * 128], qT_[:, h, 0:L], True, True, [KmT, qT_], kb.pb([bs[(h * 2 + mc) // 4]]))
        kb.act(pT_[:, :, 0:L], psc[:, :, 0:L], AF.Exp, kb.pb(bs), [pT_], scale=sc)
        bv = kb.psum(2)
        pvv = kb.psf(bv)
        for h in range(4):
            for mc in range(2):
                kb.mm(pvv[0:L, h // 2, (h % 2) * 129:(h % 2) * 129 + 129], pT_[:, h * 2 + mc, 0:L], Vaug[:, mc, h, 0:129], mc == 0, mc == 1,
                      [pT_, Vaug], kb.pb([bv[h // 2]]))
        for a in range(2):
            kb.cp(on_[0:L, a * 2:a * 2 + 2, :], pvv[0:L, a, 0:258].rearrange("p (h d) -> p h d", h=2), kb.pb([bv[a]]), [on_], eng="scalar")
        kb.S.op("vector", lambda e, o_=st_[0:L, 4:8], i_=on_[0:L, :, 128]: e.reciprocal(out=o_, in_=i_), [on_.b], [st_.b])
        kb.tt(ob_[0:L, :].rearrange("p (h d) -> p h d", h=4), on_[0:L, :, 0:128], st_[0:L, 4:8].unsqueeze(2).broadcast_to([L, 4, 128]),
              ALU.mult, [on_, st_], [ob_])
        bt = kb.psum(1)
        pt = kb.psb(bt[0])
        for c in range(4):
            kb.tr(pt[:, c * 128:c * 128 + L], ob_[0:L, c * 128:(c + 1) * 128], C["id_b"][0:L, 0:L], [ob_, C["id_b"]], kb.pb(bt))
        kb.cp(obT_[:, :, 0:L], pt[:, 0:512].rearrange("p (c t) -> p c t", c=4)[:, :, 0:L], kb.pb(bt), [obT_])
        bw = kb.psum(2)
        for nb in range(2):
            for k in range(4):
                kb.mm(kb.psf([bw[nb]])[0:L, :], obT_[:, k, 0:L], wo[:, k, nb * 512:(nb + 1) * 512], k == 0, k == 3, [obT_, wo], kb.pb([bw[nb]]))
        kb.tt(h_[0:L, :], h_[0:L, :], kb.psf(bw).rearrange("p a b -> p (a b)")[0:L, :], ALU.add, [h_, kb.pb(bw)], [h_])
        kb.dma(H[tok0:tok0 + L, :], h_[0:L, :], [h_], [])
        norm_tile_to_fm(kb, C, h_, L, gf, st_, hn_, oT_, junk)
        kb.dma(A[:, :, tok0:tok0 + L].rearrange("c p t -> p c t"), oT_[:, :, 0:L], [oT_], [])
    kb.pop()


X_HEAD_SCALE = 128 ** -0.5


def stage_ffn(kb, C, groups, H, A, P, layer, epilogue, gain, out_rows=None):
    kb.push()
    w1 = load_w(kb, P["w1"][layer], 0, DFF, "w1")
    w3 = load_w(kb, P["w3"][layer], 0, DFF, "w3")
    w2 = load_w(kb, P["w2"][layer], 0, 1024, "w2")
    if epilogue == "fm":
        g_ = load_gain_fm(kb, gain, "gnext")
    else:
        g_ = load_rowrep(kb, gain, 1024, "gfin")
    hT = [kb.sb([128, 8, 512], BF16, "fhT") for _ in range(2)]
    gT = kb.sb([128, 22, 512], BF16, "fgT")
    sa = [kb.sb([128, 512], BF16, "fsa") for _ in range(2)]
    hr = [kb.sb([128, D], F32, "fhr") for _ in range(2)]
    junk = kb.sb([128, D], BF16, "fjunk")
    hn = kb.sb([128, D], BF16, "fhn")
    st = [kb.sb([128, 4], F32, "fst") for _ in range(2)]
    oT = [kb.sb([128, 8, 128], BF16, "foT") for _ in range(2)]
    tcount = 0
    for gi, (tok0, n) in enumerate(groups):
        h_ = hT[gi % 2]
        kb.dma(h_[:, :, 0:n], A[:, :, tok0:tok0 + n].rearrange("c p t -> p c t"), [], [h_])
        for f in range(22):
            ba, bb = kb.psum(1), kb.psum(1)
            pa, pb_ = kb.psf(ba)[:, 0:n], kb.psf(bb)[:, 0:n]
            for k in range(8):
                kb.mm(pa, w1[:, k, f * 128:(f + 1) * 128], h_[:, k, 0:n], k == 0, k == 7, [w1, h_], kb.pb(ba))
            for k in range(8):
                kb.mm(pb_, w3[:, k, f * 128:(f + 1) * 128], h_[:, k, 0:n], k == 0, k == 7, [w3, h_], kb.pb(bb))
            s_ = sa[f % 2]
            kb.act(s_[:, 0:n], pa, AF.Silu, kb.pb(ba), [s_])
            kb.tt(gT[:, f, 0:n], s_[:, 0:n], pb_, ALU.mult, [s_, kb.pb(bb)], [gT])
        nt = (n + 127) // 128
        for i in range(nt):
            L = min(128, n - i * 128)
            q = tcount % 2
            tcount += 1
            r_, st_, oT_ = hr[q], st[q], oT[q]
            r0 = tok0 + i * 128
            kb.dma(r_[0:L, :], H[r0:r0 + L, :], [], [r_])
            bo = kb.psum(2)
            for nb in range(2):
                for f in range(22):
                    kb.mm(kb.psf([bo[nb]])[0:L, :], gT[:, f, i * 128:i * 128 + L], w2[:, f, nb * 512:(nb + 1) * 512], f == 0, f == 21, [gT, w2], kb.pb([bo[nb]]))
            kb.tt(r_[0:L, :], r_[0:L, :], kb.psf(bo).rearrange("p a b -> p (a b)")[0:L, :], ALU.add, [r_, kb.pb(bo)], [r_])
            if epilogue == "fm":
                kb.dma(H[r0:r0 + L, :], r_[0:L, :], [r_], [])
                norm_tile_to_fm(kb, C, r_, L, g_, st_, hn, oT_, junk)
                kb.dma(A[:, :, r0:r0 + L].rearrange("c p t -> p c t"), oT_[:, :, 0:L], [oT_], [])
            else:
                kb.act(junk[0:L, :], r_[0:L, :], AF.Square, [r_], [junk, st_], accum_out=st_[0:L, 0:1])
                rstd_from_ss(kb, st_[0:L, 0:1], st_[0:L, 1:2], D, [st_], [st_])
                kb.stt(r_[0:L, :], r_[0:L, :], st_[0:L, 1:2], g_[0:L, :], ALU.mult, ALU.mult, [r_, st_, g_], [r_])
                kb.dma(out_rows(r0, L), r_[0:L, :], [r_], [])
    kb.pop()


def stage_fox_proj(kb, C, groups, A, P, QT, KT, VA, LFS, negF, Fbase, out_k, out_v, out_lf, n_prompt):
    kb.push()
    wf = load_w(kb, P["w_in_fox"], 0, IN_FOX, "wfox")
    bfr = load_rowrep(kb, P["b_fox_f"], 16, "bfox")
    hT = [kb.sb([128, 8, 512], BF16, "phT") for _ in range(2)]
    qs = [kb.sb([64, 16, 512], BF16, "pqs") for _ in range(2)]
    ks = [kb.sb([64, 16, 512], BF16, "pks") for _ in range(2)]
    kf = [kb.sb([128, 1024], F32, "pkf") for _ in range(2)]
    vf = [kb.sb([128, 1024], F32, "pvf") for _ in range(2)]
    va = [kb.sb([128, 16, 66], BF16, "pva") for _ in range(2)]
    for q in range(2):
        kb.memset(va[q][:], 0.0, [va[q]])
        kb.memset(va[q][:, :, 0:1], 1.0, [va[q]])
    lf = [kb.sb([128, 32], F32, "plf") for _ in range(2)]
    carry = kb.sb([128, 16], F32, "pcarry")
    kb.memset(carry[:], 0.0, [carry])
    tcount = 0
    for gi, (tok0, n) in enumerate(groups):
        h_, qs_, ks_ = hT[gi % 2], qs[gi % 2], ks[gi % 2]
        kb.dma(h_[:, :, 0:n], A[:, :, tok0:tok0 + n].rearrange("c p t -> p c t"), [], [h_])
        for (dst, c0) in ((qs_, 0), (ks_, 1024)):
            for h in range(16):
                bk = kb.psum(1)
                pv = kb.psf(bk)[0:64, 0:n]
                for k in range(8):
                    kb.mm(pv, wf[:, k, c0 + h * 64:c0 + (h + 1) * 64], h_[:, k, 0:n], k == 0, k == 7, [wf, h_], kb.pb(bk))
                kb.cp(dst[:, h, 0:n], pv, kb.pb(bk), [dst], eng="scalar" if h % 2 else "vector")
        kb.dma(QT[:, :, tok0:tok0 + n].rearrange("h d t -> d h t"), qs_[:, :, 0:n], [qs_], [])
        kb.dma(KT[:, :, tok0:tok0 + n].rearrange("h d t -> d h t"), ks_[:, :, 0:n], [ks_], [])
        if tok0 < n_prompt and (tok0 // 512) < 16:
            kb.cp(Fbase[:, tok0 // 512, :], carry[:], [carry], [Fbase], eng="gpsimd")
        nt = (n + 127) // 128
        for i in range(nt):
            L = min(128, n - i * 128)
            q = tcount % 2
            tcount += 1
            kf_, vf_, va_, lf_ = kf[q], vf[q], va[q], lf[q]
            r0 = tok0 + i * 128
            cs = slice(i * 128, i * 128 + L)
            for (dstf, c0, isv) in ((kf_, 1024, False), (vf_, 2048, True)):
                bo = kb.psum(2)
                for nb in range(2):
                    for k in range(8):
                        kb.mm(kb.psf([bo[nb]])[0:L, :], h_[:, k, cs], wf[:, k, c0 + nb * 512:c0 + (nb + 1) * 512], k == 0, k == 7, [h_, wf], kb.pb([bo[nb]]))
                kb.cp(dstf[0:L, :], kb.psf(bo).rearrange("p a b -> p (a b)")[0:L, :], kb.pb(bo), [dstf], eng="scalar")
                if isv:
                    kb.cp(va_[0:L, :, 2:66], dstf[0:L, :].rearrange("p (h d) -> p h d", h=16), [dstf], [va_], eng="gpsimd")
            kb.dma(out_k(r0, L), kf_[0:L, :], [kf_], [])
            kb.dma(out_v(r0, L), vf_[0:L, :], [vf_], [])
            kb.dma(VA[r0:r0 + L], va_[0:L], [va_], [])
            bf_ = kb.psum(1)
            pf = kb.psf(bf_)
            for k in range(8):
                kb.mm(pf[0:L, 0:16], h_[:, k, cs], wf[:, k, 3072:3088], k == 0, k == 7, [h_, wf], kb.pb(bf_))
            kb.tt(lf_[0:L, 0:16], pf[0:L, 0:16], bfr[0:L, :], ALU.add, [kb.pb(bf_), bfr], [lf_])
            kb.act(lf_[0:L, 0:16], lf_[0:L, 0:16], AF.Exp, [lf_], [lf_], scale=-1.0)
            kb.act(lf_[0:L, 0:16], lf_[0:L, 0:16], AF.Ln, [lf_], [lf_], bias=1.0)
            kb.ts(lf_[0:L, 0:16], lf_[0:L, 0:16], -1.0, None, ALU.mult, None, [lf_], [lf_])
            kb.dma(out_lf(r0, L), lf_[0:L, 0:16], [lf_], [])
            if r0 < n_prompt:
                blk = r0 // 128
                bc_ = kb.psum(1)
                pc = kb.psf(bc_)
                kb.mm(pc[:, 0:16], C["tri_f"][:, :], lf_[:, 0:16], True, True, [C["tri_f"], lf_], kb.pb(bc_))
                kb.mm(pc[:, 16:32], C["ones_f"][:, :], lf_[:, 0:16], True, True, [C["ones_f"], lf_], kb.pb(bc_))
                kb.tt(lf_[:, 16:32], pc[:, 0:16], carry[:], ALU.add, [kb.pb(bc_), carry], [lf_])
                kb.ts(negF[:, blk, :], lf_[:, 16:32], -1.0, None, ALU.mult, None, [lf_], [negF])
                kb.tt(carry[:], carry[:], pc[:, 16:32], ALU.add, [carry, kb.pb(bc_)], [carry])
            else:
                kb.dma(LFS[r0 - n_prompt:r0 - n_prompt + L, :], lf_[0:L, 0:16], [lf_], [])
        if tok0 + n == n_prompt:
            kb.cp(Fbase[:, n_prompt // 512, :], carry[:], [carry], [Fbase], eng="gpsimd")
    kb.pop()


def stage_fox_prompt_attn(kb, C, QT, KT, VA, negF, Fbase, M, T):
    kb.push()
    NB = T // 128
    NG = T // 512
    kb.ps_limit = 6
    kb.ps_rr = 0
    masks = kb.sb([128, 4, 512], BF16, "fmask")
    kb.memset(masks[:], 1.0, [masks])
    for r in range(4):
        kb.S.op("gpsimd", lambda e, r=r: e.affine_select(out=masks[:, r, :], in_=masks[:, r, :], pattern=[[1, 512]], compare_op=ALU.is_ge,
                                                          fill=0.0, base=-128 * r, channel_multiplier=-1), [masks.b], [masks.b])
    ktl = [kb.sb([64, T], BF16, "aK") for _ in range(2)]
    qtl = [kb.sb([64, T], BF16, "aQ") for _ in range(2)]
    vtl = [kb.sb([128, NB, 66], BF16, "aV") for _ in range(2)]
    bt = [kb.sb([128, NB], F32, "abt") for _ in range(2)]
    bmid = [kb.sb([128, 2], F32, "abm") for _ in range(2)]
    pT = [kb.sb([128, 512], BF16, "apT") for _ in range(4)]
    accs = [kb.sb([66, 512], F32, "aacc") for _ in range(2)]
    rc = [kb.sb([1, 512], F32, "arc") for _ in range(2)]
    ob = [kb.sb([66, 512], BF16, "aob") for _ in range(2)]
    o2 = [kb.sb([64, 512], BF16, "ao2") for _ in range(2)]
    sel = kb.sb([128, 128], BF16, "asel")
    kb.memset(sel[:], 1.0, [sel])
    kb.S.op("gpsimd", lambda e: e.affine_select(out=sel[:], in_=sel[:], pattern=[[-1, 128]], compare_op=ALU.is_equal,
                                                 fill=0.0, base=-2, channel_multiplier=1), [sel.b], [sel.b])
    pcount = 0
    gcount = 0
    for h in range(16):
        k_, q_, v_ = ktl[h % 2], qtl[h % 2], vtl[h % 2]
        kb.dma(k_[:], KT[h, :, 0:T], [], [k_])
        kb.dma(q_[:], QT[h, :, 0:T], [], [q_])
        kb.dma(v_[:], VA[0:T, h, :].rearrange("(b p) e -> p b e", p=128), [], [v_])
        for g in range(NG):
            if _FA == 1:
                continue
            w = gcount % 2
            gcount += 1
            bt_, acc_, rc_, ob_ = bt[w], accs[w], rc[w], ob[w]
            nj = 4 * (g + 1)
            bm_ = bmid[w]
            kb.tt(bm_[:, 0:1], Fbase[:, g, h:h + 1], Fbase[:, g + 1, h:h + 1], ALU.add, [Fbase], [bm_])
            kb.ts(bm_[:, 0:1], bm_[:, 0:1], 0.5, None, ALU.mult, None, [bm_], [bm_])
            kb.ts(bt_[:, 0:nj], negF[:, 0:nj, h], bm_[:, 0:1], None, ALU.add, None, [negF, bm_], [bt_])
            bacc = [6 + w]
            pacc = kb.psf(bacc)[0:66, :]
            LOOK = 2
            pend = []
            for jx in range(nj + LOOK):
                if jx < nj:
                    bs = kb.psum(1)
                    ps_ = kb.psf(bs)
                    kb.mm(ps_, k_[:, jx * 128:(jx + 1) * 128], q_[:, g * 512:(g + 1) * 512], True, True, [k_, q_], kb.pb(bs))
                    pend.append((jx, bs, ps_))
                if jx >= LOOK:
                    j, bs, ps_ = pend.pop(0)
                    p_ = pT[pcount % 4]
                    pcount += 1
                    kb.act(p_[:], ps_, AF.Exp, kb.pb(bs) + [bt_.b], [p_], scale=0.125, bias=bt_[:, j:j + 1])
                    if j >= 4 * g:
                        kb.tt(p_[:], p_[:], masks[:, j - 4 * g, :], ALU.mult, [p_, masks], [p_], eng="gpsimd")
                    kb.mm(pacc, v_[:, j, :], p_[:], j == 0, j == nj - 1, [v_, p_], kb.pb(bacc))
            if _FA in (2, 3, 4):
                continue
            kb.cp(acc_[:], pacc, kb.pb(bacc), [acc_])
            kb.S.op("vector", lambda e, o_=rc_[:], i_=acc_[0:1, :]: e.reciprocal(out=o_, in_=i_), [acc_.b], [rc_.b])
            if _FA == 5:
                continue
            bb = kb.psum(1)
            pbb = kb.psf(bb)[0:66, :]
            kb.mm(pbb, C["ones_f"][0:1, 0:66], rc_[:], True, True, [C["ones_f"], rc_], kb.pb(bb))
            kb.tt(ob_[:], acc_[:], pbb, ALU.mult, [acc_, kb.pb(bb)], [ob_])
            if _FA == 6:
                continue
            b2_ = kb.psum(1)
            p2_ = kb.psf(b2_)[0:64, :]
            kb.mm(p2_, sel[0:66, 0:64], ob_[:], True, True, [sel, ob_], kb.pb(b2_))
            o2_ = o2[w]
            kb.cp(o2_[:], p2_, kb.pb(b2_), [o2_], eng="scalar")
            kb.dma(M[h // 2, (h % 2) * 64:(h % 2) * 64 + 64, g * 512:(g + 1) * 512], o2_[:], [o2_], [])
    kb.ps_limit = 8
    kb.pop()


def stage_fox_sample_attn(kb, C, QT, KT, VA, LFS, M, n_prompt, nseq, pt_rows, pool_k, pool_v, pool_lf, NPAGE):
    kb.push()
    kb.ps_limit = 5
    kb.ps_rr = 0
    id_b = C["id_b"]
    iota_i = kb.sb([128, 1], I32, "iota_i")
    kb.S.op("gpsimd", lambda e: e.iota(out=iota_i[:], pattern=[[0, 1]], base=0, channel_multiplier=1), [], [iota_i.b])
    iota_f = kb.sb([128, 1], F32, "iota_f")
    kb.cp(iota_f[:], iota_i[:], [iota_i], [iota_f])
    tri_su = kb.sb([128, 128], F32, "tri_su")
    kb.memset(tri_su[:], 1.0, [tri_su])
    kb.S.op("gpsimd", lambda e: e.affine_select(out=tri_su[:], in_=tri_su[:], pattern=[[-1, 128]], compare_op=ALU.is_gt,
                                                 fill=0.0, base=0, channel_multiplier=1), [tri_su.b], [tri_su.b])
    pti = kb.sb([128, NPAGE], I32, "pti")
    ptf = kb.sb([128, NPAGE], F32, "ptf")
    idx = kb.sb([128, NPAGE], I32, "idx")
    Lp = kb.sb([128, NPAGE, 16], F32, "Lp")
    rev = kb.sb([128, NPAGE, 16], F32, "rev")
    pre = kb.sb([128, 16, NPAGE], F32, "pre")
    pre2 = kb.sb([128, 16, NPAGE], F32, "pre2")
    sfx = kb.sb([128, 16, NPAGE], F32, "sfx")
    ones16 = kb.sb([128, NPAGE], F32, "ones16")
    kb.memset(ones16[:], 1.0, [ones16])
    Qblk = kb.sb([128, 8, 16], BF16, "Qblk")
    kb.memset(Qblk[:], 0.0, [Qblk])
    Kp = [kb.sb([128, 1024], F32, "Kp") for _ in range(2)]
    Vp = [kb.sb([128, 1024], F32, "Vp") for _ in range(2)]
    Kb = [kb.sb([128, 1024], BF16, "Kb") for _ in range(2)]
    KT2 = [kb.sb([128, 8, 128], BF16, "KT2") for _ in range(2)]
    Vb = [kb.sb([128, 16, 66], BF16, "Vbs") for _ in range(2)]
    for q in range(2):
        kb.memset(Vb[q][:], 0.0, [Vb[q]])
        kb.memset(Vb[q][:, :, 0:1], 1.0, [Vb[q]])
    tmp = [kb.sb([128, 16, 8], F32, "stmp") for _ in range(2)]
    PT = [kb.sb([128, 16, 8], BF16, "sPT") for _ in range(2)]
    KT2n = kb.sb([128, 8, 8], BF16, "KT2n")
    VAn = kb.sb([8, 16, 66], BF16, "VAn")
    lfn = kb.sb([8, 32], F32, "lfn")
    osb = kb.sb([8, 16, 66], F32, "osb")
    ofull = kb.sb([128, 1056], F32, "ofull")
    OSCt = kb.dram("OSC", [nseq, 8, 16, 66], F32)
    OSC, OSCb = OSCt.h, OSCt.b
    rcp = kb.sb([8, 16], F32, "rcp")
    att = kb.sb([8, 1024], BF16, "satt")
    attT = kb.sb([128, 8, 8], BF16, "sattT")
    pcount = 0
    for s in range(nseq):
        c0 = n_prompt + 8 * s
        kb.dma(pti[:], pt_rows[s].partition_broadcast(128), [], [pti])
        kb.cp(ptf[:], pti[:], [pti], [ptf])
        kb.ts(ptf[:], ptf[:], 128.0, iota_f[:, 0:1], ALU.mult, ALU.add, [ptf, iota_f], [ptf])
        kb.cp(idx[:], ptf[:], [ptf], [idx])
        for j in range(NPAGE):
            kb.S.dma("gpsimd", lambda e, j=j: e.indirect_dma_start(out=Lp[:, j, :], out_offset=None, in_=pool_lf,
                                                                   in_offset=bass.IndirectOffsetOnAxis(ap=idx[:, j:j + 1], axis=0)),
                     [idx.b], [Lp.b])
        Lflat = Lp[:, :, :].rearrange("p j h -> p (j h)")
        nb2 = (NPAGE * 16 + 511) // 512
        for nb in range(nb2):
            w_ = min(512, NPAGE * 16 - nb * 512)
            b1 = kb.psum(1)
            b2 = kb.psum(1)
            kb.mm(kb.psf(b1)[:, 0:w_], tri_su[:, :], Lflat[:, nb * 512:nb * 512 + w_], True, True, [tri_su, Lp], kb.pb(b1))
            kb.mm(kb.psf(b2)[:, 0:w_], C["ones_f"][:, :], Lflat[:, nb * 512:nb * 512 + w_], True, True, [C["ones_f"], Lp], kb.pb(b2))
            jn = w_ // 16
            j0 = nb * 32
            kb.cp(rev[:, j0:j0 + jn, :], kb.psf(b1)[:, 0:w_].rearrange("p (j h) -> p j h", h=16), kb.pb(b1), [rev], eng="scalar")
            kb.cp(pre[:, :, j0:j0 + jn], kb.psf(b2)[:, 0:w_].rearrange("p (j h) -> p h j", h=16), kb.pb(b2), [pre])
        for h in range(16):
            kb.S.op("vector", lambda e, h=h: e.tensor_tensor_scan(out=pre2[:, h, :], data0=ones16[:, :], data1=pre[:, h, :], initial=0.0,
                                                                  op0=ALU.mult, op1=ALU.add), [pre.b, ones16.b], [pre2.b])
        kb.tt(sfx[:, :, :], pre2[:, :, NPAGE - 1:NPAGE].broadcast_to([128, 16, NPAGE]), pre2[:, :, :], ALU.subtract, [pre2], [sfx])
        kb.tt(rev[:, :, :], rev[:, :, :], sfx[:, :, :].rearrange("p h j -> p j h"), ALU.add, [rev, sfx], [rev])
        for c in range(8):
            kb.dma(Qblk[0:64, c, 0:8], QT[2 * c, :, c0:c0 + 8], [], [Qblk])
            kb.dma(Qblk[64:128, c, 8:16], QT[2 * c + 1, :, c0:c0 + 8], [], [Qblk])
        kb.dma(KT2n[:], KT[:, :, c0:c0 + 8].rearrange("(c h2) d t -> (h2 d) c t", h2=2), [], [KT2n])
        kb.dma(VAn[:], VA[c0:c0 + 8], [], [VAn])
        kb.dma(lfn[:, 0:16], LFS[8 * s:8 * s + 8, :], [], [lfn])
        bacc = [5, 6, 7]
        NSP = ((0, 512), (512, 1024), (1024, 1056))
        for j in range(NPAGE):
            w = pcount % 2
            pcount += 1
            Kp_, Vp_, Kb_, KT2_, Vb_, tmp_, PT_ = Kp[w], Vp[w], Kb[w], KT2[w], Vb[w], tmp[w], PT[w]
            kb.S.dma("gpsimd", lambda e, j=j, Kp_=Kp_: e.indirect_dma_start(out=Kp_[:, :], out_offset=None, in_=pool_k,
                                                                            in_offset=bass.IndirectOffsetOnAxis(ap=idx[:, j:j + 1], axis=0)),
                     [idx.b], [Kp_.b])
            kb.S.dma("gpsimd", lambda e, j=j, Vp_=Vp_: e.indirect_dma_start(out=Vp_[:, :], out_offset=None, in_=pool_v,
                                                                            in_offset=bass.IndirectOffsetOnAxis(ap=idx[:, j:j + 1], axis=0)),
                     [idx.b], [Vp_.b])
            kb.cp(Kb_[:], Kp_[:], [Kp_], [Kb_], eng="scalar")
            kb.cp(Vb_[:, :, 2:66], Vp_[:, :].rearrange("p (h d) -> p h d", h=16), [Vp_], [Vb_])
            bt_ = kb.psum(1)
            ptk = kb.psb(bt_[0])
            for c in range(8):
                kb.tr(ptk[:, c * 128:(c + 1) * 128], Kb_[:, c * 128:(c + 1) * 128], id_b[:, :], [Kb_, id_b], kb.pb(bt_))
            kb.cp(KT2_[:], ptk.rearrange("p (c t) -> p c t", c=8), kb.pb(bt_), [KT2_], eng="scalar")
            bs = kb.psum(1)
            pss = kb.psf(bs)[:, 0:128]
            for c in range(8):
                kb.mm(pss[:, c * 16:(c + 1) * 16], KT2_[:, c, :], Qblk[:, c, :], True, True, [KT2_, Qblk], kb.pb(bs))
            kb.stt(tmp_[:], pss.rearrange("p (h q) -> p h q", q=8), 0.125, rev[:, j, :].unsqueeze(2).broadcast_to([128, 16, 8]),
                   ALU.mult, ALU.add, kb.pb(bs) + [rev.b], [tmp_])
            kb.act(PT_[:], tmp_[:], AF.Exp, [tmp_], [PT_])
            for a, (n0, n1) in enumerate(NSP):
                kb.mm(kb.psf([bacc[a]])[:, 0:n1 - n0], PT_[:, :, :].rearrange("p h q -> p (h q)"), Vb_[:, :, :].rearrange("p h e -> p (h e)")[:, n0:n1],
                      j == 0, False, [PT_, Vb_], kb.pb([bacc[a]]))
        bn = kb.psum(1)
        psn = kb.psf(bn)[0:8, 0:128]
        for c in range(8):
            kb.mm(psn[:, c * 16:(c + 1) * 16], KT2n[:, c, :], Qblk[:, c, :], True, True, [KT2n, Qblk], kb.pb(bn))
        bfn = kb.psum(1)
        kb.mm(kb.psf(bfn)[0:8, 0:16], C["tri_f"][0:8, 0:8], lfn[0:8, 0:16], True, True, [C["tri_f"], lfn], kb.pb(bfn))
        kb.ts(lfn[0:8, 16:32], kb.psf(bfn)[0:8, 0:16], -1.0, None, ALU.mult, None, kb.pb(bfn), [lfn])
        tn, PTn = tmp[0], PT[0]
        kb.stt(tn[0:8], psn.rearrange("p (h q) -> p h q", q=8), 0.125, lfn[0:8, 16:32].unsqueeze(2).broadcast_to([8, 16, 8]),
               ALU.mult, ALU.add, kb.pb(bn) + [lfn.b], [tn])
        kb.tt(tn[0:8], tn[0:8], C["neg1T"][0:8, 0:8].unsqueeze(1).broadcast_to([8, 16, 8]), ALU.add, [tn, C["neg1T"]], [tn])
        kb.act(PTn[0:8], tn[0:8], AF.Exp, [tn], [PTn])
        for a, (n0, n1) in enumerate(NSP):
            kb.mm(kb.psf([bacc[a]])[:, 0:n1 - n0], PTn[0:8, :, :].rearrange("p h q -> p (h q)"), VAn[0:8, :, :].rearrange("p h e -> p (h e)")[:, n0:n1],
                  False, True, [PTn, VAn], kb.pb([bacc[a]]))
        for a, (n0, n1) in enumerate(NSP):
            kb.cp(ofull[:, n0:n1], kb.psf([bacc[a]])[:, 0:n1 - n0], kb.pb([bacc[a]]), [ofull], eng="scalar")
        for h in range(16):
            kb.dma(OSC[s, :, h, :], ofull[h * 8:(h + 1) * 8, h * 66:(h + 1) * 66], [ofull], [OSCb])
        kb.dma(osb[:], OSC[s], [OSCb], [osb])
        kb.S.op("vector", lambda e: e.reciprocal(out=rcp[:, :], in_=osb[:, :, 0]), [osb.b], [rcp.b])
        kb.tt(att[:, :].rearrange("p (h d) -> p h d", h=16), osb[:, :, 2:66], rcp[:, :].unsqueeze(2).broadcast_to([8, 16, 64]), ALU.mult,
              [osb, rcp], [att])
        bo = kb.psum(1)
        po = kb.psb(bo[0])
        for c in range(8):
            kb.tr(po[:, c * 128:c * 128 + 8], att[0:8, c * 128:(c + 1) * 128], id_b[0:8, 0:8], [att, id_b], kb.pb(bo))
        kb.cp(attT[:], po.rearrange("p (c t) -> p c t", c=8)[:, :, 0:8], kb.pb(bo), [attT])
        kb.dma(M[0:8, :, c0:c0 + 8].rearrange("c p t -> p c t"), attT[:], [attT], [])
    kb.ps_limit = 8
    kb.pop()


WEIGHT_SHAPES = {
    "norm_mix": [2, 1024], "norm_x": [2, 1024], "norm_mem": [2, 1024], "norm_ffn": [2, 1024], "norm_final": [1024],
    "w_in_hyb": [1, 1024, IN_HYB], "w_out_hyb": [1, 2048, 1024], "ssd_conv_w": [1, 4, 1280], "ssd_conv_b": [1, 1280],
    "ssd_dt_bias": [1, 16], "ssd_A_log": [1, 16], "ssd_D": [1, 16], "ssd_norm": [1, 1024],
    "gdn_conv_w": [1, 4, 3072], "gdn_dt_bias": [1, 8], "gdn_A_log": [1, 8], "gdn_norm": [1, 128],
    "w_in_fox": [1, 1024, IN_FOX], "b_fox_f": [1, 16], "w_out_fox": [1, 1024, 1024],
    "wq_x": [2, 1024, 512], "wk_x": [2, 1024, 512], "wv_x": [2, 1024, 512], "wo_x": [2, 512, 1024],
    "w1": [2, 1024, DFF], "w3": [2, 1024, DFF], "w2": [2, DFF, 1024],
}
NSEQ = 4


def build(T, NPAGE, NPOOL):
    nc = bass.Bass("TRN2", target_bir_lowering=False)
    kb = KB(nc)
    NT = T + 8 * NSEQ

    def inp(n, shp, dt=F32):
        return nc.dram_tensor(n, list(shp), dt, kind="ExternalInput").ap()

    def outp(n, shp):
        return nc.dram_tensor(n, list(shp), F32, kind="ExternalOutput").ap()

    Wd = {n: inp(n, shp) for n, shp in WEIGHT_SHAPES.items()}
    xp = inp("xp", [T, D]); xs = inp("xs", [8 * NSEQ, D]); mem = inp("mem", [MEM, D])
    st_ssd = inp("st_ssd", [NSEQ, 16, 64, 64]); st_ssd_conv = inp("st_ssd_conv", [NSEQ, 3, 1280])
    st_gdn = inp("st_gdn", [NSEQ, 8, 128, 128]); st_gdn_conv = inp("st_gdn_conv", [NSEQ, 3, 3072])
    pool_k = inp("pool_k", [NPOOL, 128, 16, 64]); pool_v = inp("pool_v", [NPOOL, 128, 16, 64]); pool_lf = inp("pool_lf", [NPOOL, 128, 16])
    pt = inp("pt", [NSEQ, NPAGE], I32)
    cmk = inp("cmk", [2, NSEQ, MEM, 4, 128]); cmv = inp("cmv", [2, NSEQ, MEM, 4, 128])
    O = {}
    O["y_p"] = outp("y_p", [T, D]); O["y_s"] = outp("y_s", [8 * NSEQ, D])
    O["p_ssd"] = outp("p_ssd", [16, 64, 64]); O["p_ssd_conv"] = outp("p_ssd_conv", [3, 1280])
    O["p_gdn"] = outp("p_gdn", [8, 128, 128]); O["p_gdn_conv"] = outp("p_gdn_conv", [3, 3072])
    O["p_fox_k"] = outp("p_fox_k", [T, 1024]); O["p_fox_v"] = outp("p_fox_v", [T, 1024]); O["p_fox_lf"] = outp("p_fox_lf", [T, 16])
    O["p_mem_k"] = outp("p_mem_k", [2, MEM, 512]); O["p_mem_v"] = outp("p_mem_v", [2, MEM, 512])
    O["s_ssd"] = outp("s_ssd", [NSEQ, 16, 64, 64]); O["s_ssd_conv"] = outp("s_ssd_conv", [NSEQ, 3, 1280])
    O["s_gdn"] = outp("s_gdn", [NSEQ, 8, 128, 128]); O["s_gdn_conv"] = outp("s_gdn_conv", [NSEQ, 3, 3072])
    O["s_fox_k"] = outp("s_fox_k", [8 * NSEQ, 1024]); O["s_fox_v"] = outp("s_fox_v", [8 * NSEQ, 1024]); O["s_fox_lf"] = outp("s_fox_lf", [8 * NSEQ, 16])
    H = kb.dram("H", [NT, D], F32); A = kb.dram("A", [8, 128, NT], BF16); M = kb.dram("M", [16, 128, NT], BF16)
    QT = kb.dram("QT", [16, 64, NT], BF16); KT = kb.dram("KT", [16, 64, NT], BF16); VA = kb.dram("VA", [NT, 16, 66], BF16)
    LFS = kb.dram("LFS", [8 * NSEQ, 16], F32)
    C = make_consts(kb)
    negF = kb.sb([128, max(T // 128, 1), 16], F32, "negF")
    Fbase = kb.sb([128, T // 512 + 1, 16], F32, "Fbase")
    P0 = {"w_in_hyb": Wd["w_in_hyb"][0], "ssd_conv_w": Wd["ssd_conv_w"][0], "ssd_conv_b": Wd["ssd_conv_b"][0], "ssd_dt_bias": Wd["ssd_dt_bias"][0],
          "ssd_A_log": Wd["ssd_A_log"][0], "ssd_D": Wd["ssd_D"][0], "ssd_norm": Wd["ssd_norm"][0], "gdn_conv_w": Wd["gdn_conv_w"][0],
          "gdn_dt_bias": Wd["gdn_dt_bias"][0], "gdn_A_log": Wd["gdn_A_log"][0], "gdn_norm": Wd["gdn_norm"][0],
          "w_in_fox": Wd["w_in_fox"][0], "b_fox_f": Wd["b_fox_f"][0]}
    PL = {k: Wd[k] for k in ("wq_x", "wo_x", "norm_x", "norm_ffn", "w1", "w3", "w2")}

    kb.push()
    g0 = load_gain_fm(kb, Wd["norm_mix"][0], "g0")
    stage_norm_to_fm(kb, C, xp, 0, T, g0, A, 0)
    stage_norm_to_fm(kb, C, xs, 0, 8 * NSEQ, g0, A, T)
    kb.pop()
    if _BSTOP == 1:
        kb.S.emit()
        return nc
    kb.push()
    W = ssd_weights(kb, P0)
    ssd_stream(kb, C, W, A, 0, T, 128, M, 0, O["p_ssd"], O["p_ssd_conv"])
    for s in range(NSEQ):
        ssd_stream(kb, C, W, A, T + 8 * s, 8, 8, M, T + 8 * s, O["s_ssd"][s], O["s_ssd_conv"][s], st_ssd[s], st_ssd_conv[s])
    kb.pop()
    if _BSTOP == 2:
        kb.S.emit()
        return nc
    kb.push()
    W = gdn_weights(kb, P0)
    gdn_stream(kb, C, W, A, 0, T, 128, M, 0, O["p_gdn"], O["p_gdn_conv"])
    for s in range(NSEQ):
        gdn_stream(kb, C, W, A, T + 8 * s, 8, 8, M, T + 8 * s, O["s_gdn"][s], O["s_gdn_conv"][s], st_gdn[s], st_gdn_conv[s])
    kb.pop()
    if _BSTOP == 3:
        kb.S.emit()
        return nc
    groups = [(g * 512, 512) for g in range(T // 512)] + [(T, 8 * NSEQ)]

    def xattn_layer(layer, KC, wout2d, resid_p, resid_s):
        kb.push()
        KmT, Vaug = xattn_kv_from_mem(kb, C, mem, Wd["norm_mem"][layer], Wd["wk_x"][layer], Wd["wv_x"][layer], O["p_mem_k"][layer], O["p_mem_v"][layer])
        skv = []
        for s in range(NSEQ):
            KmS = kb.sb([128, 4, 256], BF16, "KmS")
            VaS = kb.sb([128, 2, 4, 130], BF16, "VaS")
            kb.memset(VaS[:, :, :, 128:129], 1.0, [VaS])
            xattn_kv_from_cache(kb, C, cmk[layer, s], cmv[layer, s], KmS, VaS)
            skv.append((KmS, VaS))
        tiles = [(i * 128, 128, resid_p[i * 128:(i + 1) * 128, :]) for i in range(T // 128)]
        tiles += [(T + 8 * s, 8, resid_s[8 * s:8 * s + 8, :]) for s in range(NSEQ)]
        npt = T // 128
        stage_mix_xattn(kb, C, tiles, M, KC, wout2d, None, H, A, PL, layer, lambda ti: (KmT, Vaug) if ti < npt else skv[ti - npt])
        kb.pop()

    xattn_layer(0, 16, Wd["w_out_hyb"][0], xp, xs)
    if _BSTOP == 4:
        kb.S.emit()
        return nc
    stage_ffn(kb, C, groups, H, A, PL, 0, "fm", Wd["norm_mix"][1])
    if _BSTOP == 5:
        kb.S.emit()
        return nc
    def rows(ap_p, ap_s):
        return lambda r0, L: (ap_p[r0:r0 + L, :] if r0 < T else ap_s[r0 - T:r0 - T + L, :])
    stage_fox_proj(kb, C, groups, A, P0, QT, KT, VA, LFS, negF, Fbase, rows(O["p_fox_k"], O["s_fox_k"]), rows(O["p_fox_v"], O["s_fox_v"]),
                   rows(O["p_fox_lf"], O["s_fox_lf"]), T)
    if _BSTOP == 6:
        kb.S.emit()
        return nc
    stage_fox_prompt_attn(kb, C, QT, KT, VA, negF, Fbase, M, T)
    if _BSTOP == 7:
        kb.S.emit()
        return nc
    stage_fox_sample_attn(kb, C, QT, KT, VA, LFS, M, T, NSEQ, pt, pool_k.rearrange("n p h d -> (n p) (h d)"),
                          pool_v.rearrange("n p h d -> (n p) (h d)"), pool_lf.rearrange("n p h -> (n p) h"), NPAGE)
    if _BSTOP == 8:
        kb.S.emit()
        return nc
    Hh = H.h
    xattn_layer(1, 8, Wd["w_out_fox"][0], Hh[0:T], Hh[T:NT])
    stage_ffn(kb, C, groups, H, A, PL, 1, "final", Wd["norm_final"], out_rows=rows(O["y_p"], O["y_s"]))
    kb.S.emit()
    return nc


def kernel(**inp):
    f32 = lambda a: np.ascontiguousarray(np.asarray(a), dtype=np.float32)
    xpr = np.asarray(inp["x_prompt"]); xsa = np.asarray(inp["x_sample"])
    B, T, _ = xpr.shape
    DB = xsa.shape[0]
    ptab = np.asarray(inp["page_table"]).astype(np.int32)
    NPAGE = ptab.shape[1]
    ck = np.asarray(inp["cache_fox_k"])[0]; cv = np.asarray(inp["cache_fox_v"])[0]; clf = np.asarray(inp["cache_fox_lf"])[0]
    NPOOL = ck.shape[0]
    ncore = 8
    assert DB == NSEQ * ncore and B * 4 == ncore
    nc = build(T, NPAGE, NPOOL)
    wmap = {n: f32(inp[n]) for n in WEIGHT_SHAPES}
    ck = f32(ck); cv = f32(cv); clf = f32(clf)
    in_maps = []
    for c in range(ncore):
        b = c // 4
        sl = slice(NSEQ * c, NSEQ * (c + 1))
        m = dict(wmap)
        m.update(xp=f32(xpr[b]), xs=f32(xsa[sl].reshape(8 * NSEQ, D)), mem=f32(inp["mem_prompt"][b]),
                 st_ssd=f32(inp["state_ssd"][0, sl]), st_ssd_conv=f32(inp["state_ssd_conv"][0, sl]),
                 st_gdn=f32(inp["state_gdn"][0, sl]), st_gdn_conv=f32(inp["state_gdn_conv"][0, sl]),
                 pool_k=ck, pool_v=cv, pool_lf=clf, pt=np.ascontiguousarray(ptab[sl]),
                 cmk=f32(np.asarray(inp["cache_mem_k"])[:, sl]), cmv=f32(np.asarray(inp["cache_mem_v"])[:, sl]))
        in_maps.append(m)
    res = run_bass_kernel_spmd(nc, in_maps, core_ids=list(range(ncore)))
    R = res.results
    pc = [0, 4]
    cat = lambda name: np.stack([R[c][name] for c in pc])
    scat = lambda name: np.concatenate([R[c][name] for c in range(ncore)], axis=0)
    y_prompt = cat("y_p")
    y_sample = scat("y_s").reshape(DB, 8, D)
    out = (
        y_prompt, y_sample,
        cat("p_ssd")[None], cat("p_ssd_conv")[None], cat("p_gdn")[None], cat("p_gdn_conv")[None],
        cat("p_fox_k").reshape(1, B, T, 16, 64), cat("p_fox_v").reshape(1, B, T, 16, 64), cat("p_fox_lf").reshape(1, B, T, 16),
        np.stack([R[c]["p_mem_k"] for c in pc], axis=1).reshape(2, B, MEM, 4, 128),
        np.stack([R[c]["p_mem_v"] for c in pc], axis=1).reshape(2, B, MEM, 4, 128),
        scat("s_ssd")[None], scat("s_ssd_conv")[None], scat("s_gdn")[None], scat("s_gdn_conv")[None],
        scat("s_fox_k").reshape(1, DB, 8, 16, 64), scat("s_fox_v").reshape(1, DB, 8, 16, 64), scat("s_fox_lf").reshape(1, DB, 8, 16),
    )
    return tuple(np.ascontiguousarray(o, dtype=np.float32) for o in out)
```
